# Optimizing a Trainium2 kernel written in Bass

```python
import math
import jax, jax.numpy as jnp
from jax import lax
import numpy as np

D_MODEL = 1024
BATCH = 8
SEQ = 8192
DEPTH = 4

N_A = DEPTH // 2
N_B = DEPTH - N_A
MIX_W = D_MODEL
MEM_W = D_MODEL
IN_W = MIX_W + MEM_W
POOL_WINDOWS = (2, 4, 8, 16)
N_POOL = len(POOL_WINDOWS)
POOL_C = MIX_W // N_POOL
DIFF_HEADS = 4
DIFF_HD = 128
DIFF_VD = 2 * DIFF_HD
Q_BLOCK = 128
SUBLN_EPS = 1e-5
ROPE_THETA = 500000.0
ROT_DIM = DIFF_HD // 4
N_MEM = 256
MEM_HEADS = 4
MEM_HD = MEM_W // MEM_HEADS
D_FF = ((8 * D_MODEL // 3 + 255) // 256) * 256
NORM_EPS = 1e-6

kernel_name = "yoco_pool_diffattn_memory_block"


def rmsnorm(x, g, eps=NORM_EPS):
    xf = x.astype(jnp.float32)
    y = xf * lax.rsqrt(jnp.mean(xf * xf, axis=-1, keepdims=True) + eps)
    return (y * g.astype(jnp.float32)).astype(x.dtype)


def rope_tables(seq_len):
    pos = jnp.arange(seq_len, dtype=jnp.float32)
    inv_freq = jnp.power(jnp.float32(ROPE_THETA), -jnp.arange(0, ROT_DIM, 2, dtype=jnp.float32) / ROT_DIM)
    ang = pos[:, None] * inv_freq[None, :]
    ang = jnp.concatenate([ang, ang], axis=-1)
    return jnp.cos(ang), jnp.sin(ang)


def apply_partial_rope(t, cos, sin):
    shp = (1, t.shape[1]) + (1,) * (t.ndim - 3) + (ROT_DIM,)
    c = cos.reshape(shp)
    s = sin.reshape(shp)
    tr = t[..., :ROT_DIM].astype(jnp.float32)
    half = ROT_DIM // 2
    rot = jnp.concatenate([-tr[..., half:], tr[..., :half]], axis=-1)
    return jnp.concatenate([(tr * c + rot * s).astype(t.dtype), t[..., ROT_DIM:]], axis=-1)


def multi_scale_pool(u, w_pool, scale):
    b, s, _ = u.shape
    uf = u.astype(jnp.float32).reshape(b, s, N_POOL, POOL_C)
    csum = lax.cumsum(uf, axis=1)
    t = jnp.arange(s)
    outs = []
    for g, w in enumerate(POOL_WINDOWS):
        cg = csum[:, :, g]
        lagged = jnp.pad(cg, ((0, 0), (w, 0), (0, 0)))[:, :s]
        cnt = jnp.minimum(t + 1, w).astype(jnp.float32)[None, :, None]
        outs.append((cg - lagged) / cnt - uf[:, :, g])
    pooled = jnp.stack(outs, axis=2).astype(u.dtype)
    y = jnp.einsum('bsgc,gce->bsge', pooled, w_pool).reshape(b, s, MIX_W)
    return y * scale


def shared_kv(h, g_kv, w_kv, cos, sin):
    b, s, _ = h.shape
    kv = rmsnorm(h, g_kv) @ w_kv
    k = kv[..., :DIFF_HEADS * 2 * DIFF_HD].reshape(b, s, DIFF_HEADS, 2, DIFF_HD)
    v = kv[..., DIFF_HEADS * 2 * DIFF_HD:].reshape(b, s, DIFF_HEADS, DIFF_VD)
    return apply_partial_rope(k, cos, sin), v


def diff_attention(zq, k, v, lq1, lk1, lq2, lk2, g_sub, lambda_init, cos, sin):
    b, s, _ = zq.shape
    q = apply_partial_rope(zq.reshape(b, s, DIFF_HEADS, 2, DIFF_HD), cos, sin)
    lam = (jnp.exp(jnp.sum(lq1.astype(jnp.float32) * lk1.astype(jnp.float32)))
           - jnp.exp(jnp.sum(lq2.astype(jnp.float32) * lk2.astype(jnp.float32)))
           + lambda_init)
    scale = DIFF_HD ** -0.5
    nblk = s // Q_BLOCK
    qb = q.reshape(b, nblk, Q_BLOCK, DIFF_HEADS, 2, DIFF_HD).transpose(1, 0, 2, 3, 4, 5)
    kpos = jnp.arange(s)

    def one_block(args):
        qi, bi = args
        qpos = bi * Q_BLOCK + jnp.arange(Q_BLOCK)
        sc = jnp.einsum('bqhcd,bkhcd->bhcqk', qi, k).astype(jnp.float32) * scale
        mask = kpos[None, :] <= qpos[:, None]
        sc = jnp.where(mask[None, None, None], sc, -jnp.inf)
        p = jax.nn.softmax(sc, axis=-1)
        a = p[:, :, 0] - lam * p[:, :, 1]
        return jnp.einsum('bhqk,bkhe->bqhe', a.astype(v.dtype), v)

    o = lax.map(one_block, (qb, jnp.arange(nblk)))
    o = o.transpose(1, 0, 2, 3, 4).reshape(b, s, DIFF_HEADS, DIFF_VD)
    o = rmsnorm(o, g_sub, SUBLN_EPS) * (1.0 - lambda_init)
    return o.reshape(b, s, MIX_W)


def memory_attention(zq, mem, g_mem, w_mem_kv):
    b, s, _ = zq.shape
    q = zq.reshape(b, s, MEM_HEADS, MEM_HD)
    kv = rmsnorm(mem, g_mem) @ w_mem_kv
    m = mem.shape[1]
    km = kv[..., :MEM_W].reshape(b, m, MEM_HEADS, MEM_HD)
    vm = kv[..., MEM_W:].reshape(b, m, MEM_HEADS, MEM_HD)
    sc = jnp.einsum('bshd,bmhd->bhsm', q, km).astype(jnp.float32) * (MEM_HD ** -0.5)
    p = jax.nn.softmax(sc, axis=-1)
    o = jnp.einsum('bhsm,bmhd->bshd', p.astype(vm.dtype), vm)
    return o.reshape(b, s, MEM_W)


def swiglu(x, w_gate, w_up, w_down):
    return (jax.nn.silu(x @ w_gate) * (x @ w_up)) @ w_down


def setup_inputs(seed: int = 0) -> dict:
    key = jax.random.key(seed)
    ks = jax.random.split(key, 24)
    f32 = jnp.float32

    def nrm(k, shape, fan_in):
        return jax.random.normal(k, shape, f32) * (fan_in ** -0.5)

    def gain(k, shape):
        return 1.0 + 0.02 * jax.random.normal(k, shape, f32)

    return {
        "x": jax.random.normal(ks[0], (BATCH, SEQ, D_MODEL), f32),
        "mem": jax.random.normal(ks[1], (BATCH, N_MEM, D_MODEL), f32),
        "w_in": nrm(ks[2], (DEPTH, D_MODEL, IN_W), D_MODEL),
        "w_out": nrm(ks[3], (DEPTH, IN_W, D_MODEL), IN_W),
        "g_mix": gain(ks[4], (DEPTH, D_MODEL)),
        "g_ffn": gain(ks[5], (DEPTH, D_MODEL)),
        "w_gate": nrm(ks[6], (DEPTH, D_MODEL, D_FF), D_MODEL),
        "w_up": nrm(ks[7], (DEPTH, D_MODEL, D_FF), D_MODEL),
        "w_down": nrm(ks[8], (DEPTH, D_FF, D_MODEL), D_FF),
        "g_mem": gain(ks[9], (DEPTH, D_MODEL)),
        "w_mem_kv": nrm(ks[10], (DEPTH, D_MODEL, 2 * MEM_W), D_MODEL),
        "pool_w": nrm(ks[11], (N_A, N_POOL, POOL_C, POOL_C), POOL_C),
        "pool_scale": 1.0 + 0.1 * jax.random.normal(ks[12], (N_A, MIX_W), f32),
        "g_kv": gain(ks[13], (D_MODEL,)),
        "w_kv": nrm(ks[14], (D_MODEL, DIFF_HEADS * 2 * DIFF_HD + DIFF_HEADS * DIFF_VD), D_MODEL),
        "lam_q1": 0.1 * jax.random.normal(ks[15], (N_B, DIFF_HD), f32),
        "lam_k1": 0.1 * jax.random.normal(ks[16], (N_B, DIFF_HD), f32),
        "lam_q2": 0.1 * jax.random.normal(ks[17], (N_B, DIFF_HD), f32),
        "lam_k2": 0.1 * jax.random.normal(ks[18], (N_B, DIFF_HD), f32),
        "g_subln": gain(ks[19], (N_B, DIFF_VD)),
        "g_final": gain(ks[20], (D_MODEL,)),
    }


def reference(x, mem, w_in, w_out, g_mix, g_ffn, w_gate, w_up, w_down, g_mem, w_mem_kv,
              pool_w, pool_scale, g_kv, w_kv, lam_q1, lam_k1, lam_q2, lam_k2, g_subln, g_final):
    cos, sin = rope_tables(x.shape[1])
    h = x
    k_sh = None
    v_sh = None
    for i in range(DEPTH):
        z = rmsnorm(h, g_mix[i]) @ w_in[i]
        z_mix = z[..., :MIX_W]
        z_memq = z[..., MIX_W:]
        if i < N_A:
            mix = multi_scale_pool(z_mix, pool_w[i], pool_scale[i])
        else:
            j = i - N_A
            lambda_init = 0.8 - 0.6 * math.exp(-0.3 * i)
            mix = diff_attention(z_mix, k_sh, v_sh, lam_q1[j], lam_k1[j], lam_q2[j], lam_k2[j],
                                 g_subln[j], lambda_init, cos, sin)
        mem_out = memory_attention(z_memq, mem, g_mem[i], w_mem_kv[i])
        h = h + jnp.concatenate([mix, mem_out], axis=-1) @ w_out[i]
        h = h + swiglu(rmsnorm(h, g_ffn[i]), w_gate[i], w_up[i], w_down[i])
        if i == N_A - 1:
            k_sh, v_sh = shared_kv(h, g_kv, w_kv, cos, sin)
    return rmsnorm(h, g_final)
```

```python
import math
import numpy as np
from contextlib import ExitStack
import concourse.bass as bass
import concourse.mybir as mybir
from concourse.bass_utils import run_bass_kernel_spmd

F32 = mybir.dt.float32
BF16 = mybir.dt.bfloat16
AF = mybir.ActivationFunctionType
ALU = mybir.AluOpType

PE, ACT, DVE, POOL, SP = "tensor", "scalar", "vector", "gpsimd", "sync"
ENGS = [PE, ACT, DVE, POOL, SP]

S = 8192
D = 1024
T = 512
NT = S // T
DC = 8
FF = 2816
FC = 22
NMEM = 256
DEPTH = 4
N_A = 2
POOL_W = (2, 4, 8, 16)
NORM_EPS = 1e-6
SUBLN_EPS = 1e-5
NV = 140
NCST = 224 + 2048
SBUF_LO = 16512
SBUF_HI = 229344


class Buf:
    __slots__ = ("name", "last_w", "readers", "slot", "phase", "excl")

    def __init__(self, name, excl=False):
        self.name = name
        self.excl = excl
        self.last_w = None
        self.readers = []
        self.slot = None
        self.phase = -1


class Op:
    __slots__ = ("eng", "fn", "deps", "signals", "sigval", "is_dma", "dbuf", "slot")

    def __init__(self, eng, fn):
        self.eng = eng
        self.fn = fn
        self.deps = []
        self.signals = False
        self.sigval = 0
        self.is_dma = False
        self.dbuf = None


class Prog:
    def __init__(self):
        self.ops = {e: [] for e in ENGS}
        self.slot_cnt = []
        self.slot_kind = []
        self.free_slots = {"sw": [], "hw": []}
        self.phase = 0
        self.sems = None
        self.last = {e: None for e in ENGS}
        self.dma_since = {}
        self.pending = {e: None for e in ENGS}

    def _add(self, op, reads, writes):
        if any(b.excl for b in reads):
            writes = list(writes) + [b for b in reads if b.excl]
            reads = [b for b in reads if not b.excl]
        deps = {}
        for b in reads:
            w = b.last_w
            if w is not None:
                deps[id(w)] = w
        for b in writes:
            w = b.last_w
            if w is not None:
                deps[id(w)] = w
            for r in b.readers:
                if r.eng == op.eng == PE and not r.is_dma and not op.is_dma:
                    continue
                deps[id(r)] = r
        pb = self.pending[op.eng]
        if pb is not None:
            for d in pb:
                deps[id(d)] = d
            self.pending[op.eng] = None
        for d in deps.values():
            if d is op:
                continue
            if (not d.is_dma) and (not op.is_dma) and d.eng == PE and op.eng == PE:
                continue
            d.signals = True
            op.deps.append(d)
        for b in reads:
            b.readers.append(op)
        for b in writes:
            b.last_w = op
            b.readers = []
        self.ops[op.eng].append(op)
        if op.is_dma:
            self.dma_since[id(op.dbuf)] = op
        else:
            self.last[op.eng] = op
        return op

    def op(self, eng, fn, reads=(), writes=()):
        return self._add(Op(eng, fn), reads, writes)

    def dma(self, eng, fn, sbuf, reads=(), writes=()):
        o = Op(eng, fn)
        o.is_dma = True
        o.dbuf = sbuf
        o.signals = True
        kind = "sw" if eng == POOL else "hw"
        if sbuf.slot is None or sbuf.phase != self.phase or self.slot_kind[sbuf.slot] != kind:
            if self.free_slots[kind]:
                sbuf.slot = self.free_slots[kind].pop()
            else:
                sbuf.slot = len(self.slot_cnt)
                self.slot_cnt.append(0)
                self.slot_kind.append(kind)
            sbuf.phase = self.phase
        self.slot_cnt[sbuf.slot] += 1
        o.slot = sbuf.slot
        o.sigval = 16 * self.slot_cnt[sbuf.slot]
        return self._add(o, reads, writes)

    def barrier(self):
        deps = [o for o in self.last.values() if o is not None] + list(self.dma_since.values())
        for o in deps:
            o.signals = True
        for e in ENGS:
            cur = self.pending[e]
            self.pending[e] = list(deps) + (cur or [])
        self.dma_since = {}
        self.phase += 1
        self.free_slots = {"sw": [i for i, k in enumerate(self.slot_kind) if k == "sw"],
                           "hw": [i for i, k in enumerate(self.slot_kind) if k == "hw"]}

    def emit(self, nc, final_wait_ops=()):
        with ExitStack() as es:
            esem = {e: es.enter_context(nc.semaphore("s_" + e)) for e in ENGS}
            dsem = [es.enter_context(nc.semaphore(f"d_{i}")) for i in range(len(self.slot_cnt))]
            for e in ENGS:
                c = 0
                for o in self.ops[e]:
                    if o.is_dma:
                        continue
                    if o.signals:
                        c += 1
                        o.sigval = c
            block = es.enter_context(nc.Block())

            def run(engname, eh, tail=None):
                waited = {}

                def wait_for(d):
                    if d.is_dma:
                        s, v = dsem[d.slot], d.sigval
                    else:
                        s, v = esem[d.eng], d.sigval
                    k = id(s)
                    if waited.get(k, 0) < v:
                        eh.wait_ge(s, v)
                        waited[k] = v

                for o in self.ops[engname]:
                    for d in o.deps:
                        wait_for(d)
                    ins = o.fn(eh)
                    if o.is_dma:
                        ins.then_inc(dsem[o.slot], 16)
                    elif o.signals:
                        ins.then_inc(esem[engname], 1)
                if tail is not None:
                    for d in tail:
                        wait_for(d)

            @block.sync
            def _(eh):
                run(SP, eh, tail=final_wait_ops)

            @block.tensor
            def _(eh):
                run(PE, eh)

            @block.vector
            def _(eh):
                run(DVE, eh)

            @block.scalar
            def _(eh):
                run(ACT, eh)

            @block.gpsimd
            def _(eh):
                run(POOL, eh)


class Ctx:
    pass


def build(stop_after=None, debug=False, only=None):
    nc = bass.Bass("TRN2", target_bir_lowering=False)
    P = Prog()
    cx = Ctx()
    cx.nc, cx.P = nc, P
    okind = "ExternalOutput" if debug else None

    def dram_in(name, shape, dt=F32):
        return nc.dram_tensor(name, list(shape), dt, kind="ExternalInput").ap()

    def dram_scr(name, shape, dt):
        if debug:
            return nc.dram_tensor(name, list(shape), dt, kind="ExternalOutput").ap()
        return nc.dram_tensor(name, list(shape), dt).ap()

    x = dram_in("x", [S, D])
    mem = dram_in("mem", [NMEM, D])
    w_in = dram_in("w_in", [DEPTH, D, 2 * D])
    w_out = dram_in("w_out", [DEPTH, 2 * D, D])
    w_gate = dram_in("w_gate", [DEPTH, D, FF])
    w_up = dram_in("w_up", [DEPTH, D, FF])
    w_down = dram_in("w_down", [DEPTH, FF, D])
    w_mem_kv = dram_in("w_mem_kv", [DEPTH, D, 2 * D])
    pool_w = dram_in("pool_w", [N_A, 4, 256, 256])
    w_kv = dram_in("w_kv", [D, 2 * D])
    vecs_d = dram_in("vecs", [128, NV])
    cst_d = dram_in("cst", [128, NCST])
    cs_d = dram_in("cs", [2, 128, S])
    out = nc.dram_tensor("out", [S, D], F32, kind="ExternalOutput").ap()

    hT = dram_scr("hT", [D, S], F32)
    kT = dram_scr("kT", [D, S], BF16)
    vS = dram_scr("vS", [4, 128, S // 128, 256], BF16)
    qT = dram_scr("qT", [D, S], BF16)
    mixT = dram_scr("mixT", [D, S], BF16)
    memoT = dram_scr("memoT", [D, S], BF16)
    hT_b = [[Buf(f"hT{t}_{c}") for c in range(DC)] for t in range(NT)]
    kT_b = [Buf(f"kT{c}") for c in range(DC)]
    vS_b = [Buf(f"vS{h}") for h in range(4)]
    qT_b = [[Buf(f"qT{t}_{c}") for c in range(DC)] for t in range(NT)]
    mixT_b = [[Buf(f"mixT{t}_{c}") for c in range(DC)] for t in range(NT)]
    memoT_b = [[Buf(f"memoT{t}_{c}") for c in range(DC)] for t in range(NT)]

    cx.off = SBUF_LO
    cx.uid = 0

    def sb(name, shape, dt):
        size = 1
        for s_ in shape[1:]:
            size *= s_
        size *= 4 if dt == F32 else 2
        off = (cx.off + 31) // 32 * 32
        cx.off = off + size
        assert cx.off <= SBUF_HI, f"SBUF overflow at {name}: {cx.off}"
        cx.uid += 1
        return nc.alloc_sbuf_tensor_at(f"{name}_{cx.uid}", list(shape), dt, offset=off)

    def sb_at(name, shape, dt, off):
        cx.uid += 1
        return nc.alloc_sbuf_tensor_at(f"{name}_{cx.uid}", list(shape), dt, offset=off)

    ps = [nc.alloc_psum_tensor(f"psb{i}", [128, 512], F32) for i in range(8)]
    ps_b = [Buf(f"ps{i}", excl=True) for i in range(8)]
    held = [False] * 8
    bank_i = [0]

    def bank(hold=False):
        for _ in range(8):
            k = bank_i[0]
            bank_i[0] = (k + 1) % 8
            if not held[k]:
                if hold:
                    held[k] = True
                return k
        raise RuntimeError("no free psum bank")

    def release(k):
        held[k] = False

    evac_i = [0]

    def evac_eng():
        evac_i[0] += 1
        return ACT if evac_i[0] % 2 else DVE

    def copy_op(eng, dst, src, reads, writes):
        if eng == ACT:
            P.op(ACT, lambda e: e.copy(out=dst, in_=src), reads=reads, writes=writes)
        elif eng == DVE:
            P.op(DVE, lambda e: e.tensor_copy(out=dst, in_=src), reads=reads, writes=writes)
        else:
            P.op(POOL, lambda e: e.tensor_copy(out=dst, in_=src), reads=reads, writes=writes)

    ident = sb("ident", [128, 128], F32)
    ones_bf = sb("ones_bf", [128, 128], BF16)
    ones_f = sb("ones_f", [128, 128], F32)
    perm_bf = sb("perm_bf", [128, 128], BF16)
    rcw = sb("rcw", [128, 64], F32)
    vecs = sb("vecs", [128, NV], F32)
    misc = sb("misc", [128, 16], F32)
    masks = sb("masks", [128, 4, 512], BF16)
    b_const = Buf("const")
    b_vecs = Buf("vecs")
    b_misc = Buf("misc")
    persist_end = cx.off

    lambda_init = [0.8 - 0.6 * math.exp(-0.3 * i) for i in range(DEPTH)]

    cst_tmp = sb("cst_tmp", [128, NCST], F32)
    b_ctmp = Buf("cst_tmp")
    P.dma(SP, lambda e: e.dma_start(out=cst_tmp[:], in_=cst_d), b_ctmp, writes=[b_ctmp])
    P.dma(SP, lambda e: e.dma_start(out=vecs[:], in_=vecs_d), b_vecs, writes=[b_vecs])
    P.op(DVE, lambda e: e.tensor_copy(out=ident[:], in_=cst_tmp[:, 0:128]), reads=[b_ctmp], writes=[b_const])
    P.op(DVE, lambda e: e.memset(perm_bf[:], 0.0), writes=[b_const])
    P.op(DVE, lambda e: e.tensor_copy(out=perm_bf[0:32, 0:32], in_=cst_tmp[0:32, 128:160]), reads=[b_ctmp, b_const], writes=[b_const])
    P.op(DVE, lambda e: e.tensor_copy(out=rcw[:], in_=cst_tmp[:, 160:224]), reads=[b_ctmp], writes=[b_const])
    P.op(DVE, lambda e: e.tensor_copy(out=masks[:].rearrange("p a b -> p (a b)"), in_=cst_tmp[:, 224:224 + 2048]),
         reads=[b_ctmp], writes=[b_const])
    P.op(DVE, lambda e: e.memset(ones_bf[:], 1.0), writes=[b_const])
    P.op(DVE, lambda e: e.memset(ones_f[:], 1.0), writes=[b_const])
    P.op(DVE, lambda e: e.memset(misc[:, 0:1], NORM_EPS), writes=[b_misc])
    for j in range(2):
        li = lambda_init[N_A + j]
        P.op(DVE, (lambda j, li: lambda e: e.memset(misc[:, 1 + j:2 + j], SUBLN_EPS / (1.0 - li) ** 2))(j, li), writes=[b_misc])
    for j in range(2):
        li = lambda_init[N_A + j]
        P.op(DVE, (lambda j: lambda e: e.tensor_tensor(out=misc[:, 5:6], in0=vecs[:, 132 + j:133 + j], in1=vecs[:, 134 + j:135 + j], op=ALU.mult))(j),
             reads=[b_vecs, b_misc], writes=[b_misc])
        P.op(DVE, (lambda j: lambda e: e.tensor_tensor(out=misc[:, 6:7], in0=vecs[:, 136 + j:137 + j], in1=vecs[:, 138 + j:139 + j], op=ALU.mult))(j),
             reads=[b_vecs, b_misc], writes=[b_misc])
        k = bank()
        P.op(PE, (lambda k: lambda e: e.matmul(ps[k][:, 0:2], lhsT=ones_f[:], rhs=misc[:, 5:7], start=True, stop=True))(k),
             reads=[b_const, b_misc], writes=[ps_b[k]])
        P.op(ACT, (lambda k: lambda e: e.activation(out=misc[:, 7:9], in_=ps[k][:, 0:2], func=AF.Exp))(k), reads=[ps_b[k], b_misc], writes=[b_misc])
        P.op(DVE, (lambda j, li: lambda e: e.scalar_tensor_tensor(out=misc[:, 3 + j:4 + j], in0=misc[:, 8:9], scalar=-li, in1=misc[:, 7:8],
                                                                  op0=ALU.add, op1=ALU.subtract))(j, li),
             reads=[b_misc], writes=[b_misc])
    P.barrier()
    cx.off = persist_end

    def load_w(dst, dst_b, src_rows_ap, kc_n, eng=POOL):
        for kc in range(kc_n):
            P.dma(eng, (lambda kc: lambda e: e.dma_start(out=dst[:, kc, :], in_=src_rows_ap[kc * 128:(kc + 1) * 128, :]))(kc),
                  dst_b[kc], writes=[dst_b[kc]])

    def norm_tile(hA, hA_b, gcol, xn, xn_b, ncol, scr, out_eng_f32=False, eps_col=0, scale=1.0 / D, nch=DC):
        sq, sq_b, rs, rs_b, rstd, rstd_b = scr
        k = bank()
        for c in range(nch):
            s = c % len(sq)
            P.op(ACT, (lambda c, s: lambda e: e.activation(out=sq[s][:, 0:ncol], in_=hA[:, c, 0:ncol], func=AF.Square))(c, s),
                 reads=[hA_b[c]], writes=[sq_b[s]])
            P.op(PE, (lambda c, s, k: lambda e: e.matmul(ps[k][:, 0:ncol], lhsT=ones_bf[:], rhs=sq[s][:, 0:ncol], start=(c == 0), stop=(c == nch - 1)))(c, s, k),
                 reads=[sq_b[s], b_const], writes=[ps_b[k]])
        P.op(ACT, (lambda k: lambda e: e.activation(out=rs[:, 0:ncol], in_=ps[k][:, 0:ncol], func=AF.Ln, bias=misc[:, eps_col:eps_col + 1], scale=scale))(k),
             reads=[ps_b[k], b_misc], writes=[rs_b])
        P.op(ACT, lambda e: e.activation(out=rstd[:, 0:ncol], in_=rs[:, 0:ncol], func=AF.Exp, scale=-0.5), reads=[rs_b], writes=[rstd_b])
        for c in range(nch):
            P.op(DVE, (lambda c: lambda e: e.scalar_tensor_tensor(out=xn[:, c, 0:ncol], in0=hA[:, c, 0:ncol], scalar=vecs[:, gcol + c:gcol + c + 1],
                                                                  in1=rstd[:, 0:ncol], op0=ALU.mult, op1=ALU.mult))(c),
                 reads=[hA_b[c], rstd_b, b_vecs], writes=[xn_b[c]])

    def norm_scratch(tag):
        sq = [sb(f"sq{tag}{i}", [128, T], BF16) for i in range(4)]
        sq_b = [Buf(f"sq{tag}{i}") for i in range(4)]
        rs = sb(f"rs{tag}", [128, T], F32)
        rstd = sb(f"rstd{tag}", [128, T], F32)
        return (sq, sq_b, rs, Buf(f"rs{tag}"), rstd, Buf(f"rstd{tag}"))

    def proj(w, w_b, kcn, x_, x_b, ocs, evac, ncol=T, wcol0=0):
        ocs = list(ocs)
        first, rest = ocs[:4], ocs[4:]
        bk = [bank() for _ in first]
        for kc in range(kcn):
            for i_, oc in enumerate(first):
                k = bk[i_]
                P.op(PE, (lambda oc, kc, k: lambda e: e.matmul(ps[k][:, 0:ncol], lhsT=w[:, kc, wcol0 + oc * 128:wcol0 + (oc + 1) * 128],
                                                                  rhs=x_[:, kc, 0:ncol], start=(kc == 0), stop=(kc == kcn - 1)))(oc, kc, k),
                     reads=[w_b[kc], x_b[kc]], writes=[ps_b[k]])
        for i_, oc in enumerate(first):
            evac(oc, bk[i_])
        for oc in rest:
            k = bank()
            for kc in range(kcn):
                P.op(PE, (lambda oc, kc, k: lambda e: e.matmul(ps[k][:, 0:ncol], lhsT=w[:, kc, wcol0 + oc * 128:wcol0 + (oc + 1) * 128],
                                                                  rhs=x_[:, kc, 0:ncol], start=(kc == 0), stop=(kc == kcn - 1)))(oc, kc, k),
                     reads=[w_b[kc], x_b[kc]], writes=[ps_b[k]])
            evac(oc, k)

    def load_h_chunks(hA, hA_b, t, cs=range(DC)):
        for c in cs:
            P.dma(SP, (lambda c: lambda e: e.dma_start(out=hA[:, c, :], in_=hT[c * 128:(c + 1) * 128, t * T:(t + 1) * T]))(c),
                  hA_b[c], reads=[hT_b[t][c]], writes=[hA_b[c]])

    stg_state = {}

    def make_stage(n=3):
        st = [sb(f"stg{i}", [128, T], F32) for i in range(n)]
        stb = [Buf(f"stg{i}") for i in range(n)]
        return st, stb, [0]

    def residual_store(stage, hA, hA_b, c, k, t):
        st, stb, cnt = stage
        s = cnt[0] % len(st)
        cnt[0] += 1
        P.op(DVE, (lambda c, k, s: lambda e: e.tensor_tensor(out=st[s][:], in0=hA[:, c, :], in1=ps[k][:], op=ALU.add))(c, k, s),
             reads=[hA_b[c], ps_b[k]], writes=[stb[s]])
        P.dma(SP, (lambda c, s: lambda e: e.dma_start(out=hT[c * 128:(c + 1) * 128, t * T:(t + 1) * T], in_=st[s][:]))(c, s),
              stb[s], reads=[stb[s]], writes=[hT_b[t][c]])

    def mem_kv_prologue(l, wbuf, wbuf_b, kmT, kmT_b, vm, vm_b, memT, memT_b, mn, mn_b, nscr):
        load_w(wbuf, wbuf_b, w_mem_kv[l], DC)
        norm_tile(memT, memT_b, 64 + l * 8, mn, mn_b, NMEM, nscr)

        def ev_k(oc, k):
            copy_op(evac_eng(), kmT[:, oc, :], ps[k][:, 0:NMEM], [ps_b[k]], [kmT_b])
        proj(wbuf, wbuf_b, DC, mn, mn_b, range(DC), ev_k, ncol=NMEM)
        for mc in range(2):
            for half in range(2):
                k = bank()
                for kc in range(DC):
                    P.op(PE, (lambda mc, half, kc, k: lambda e: e.matmul(ps[k][:], lhsT=mn[:, kc, mc * 128:(mc + 1) * 128],
                                                                         rhs=wbuf[:, kc, D + half * 512:D + (half + 1) * 512],
                                                                         start=(kc == 0), stop=(kc == DC - 1)))(mc, half, kc, k),
                         reads=[mn_b[kc], wbuf_b[kc]], writes=[ps_b[k]])
                copy_op(evac_eng(), vm[:, mc, half * 512:(half + 1) * 512], ps[k][:], [ps_b[k]], [vm_b])

    def mem_attention(qm, qm_b, kmT, kmT_b, vm, vm_b, pT, pT_b, rl, rl_b, dst, dst_b, dst_c0):
        sc = 1.0 / 16.0
        for hh in range(4):
            for mc in range(2):
                k = bank()
                for kc in range(2):
                    P.op(PE, (lambda hh, mc, kc, k: lambda e: e.matmul(ps[k][:], lhsT=kmT[:, 2 * hh + kc, mc * 128:(mc + 1) * 128],
                                                                       rhs=qm[:, 2 * hh + kc, :], start=(kc == 0), stop=(kc == 1)))(hh, mc, kc, k),
                         reads=[kmT_b, qm_b[2 * hh + kc]], writes=[ps_b[k]])
                P.op(ACT, (lambda mc, k: lambda e: e.activation(out=pT[mc][:], in_=ps[k][:], func=AF.Exp, scale=sc))(mc, k),
                     reads=[ps_b[k]], writes=[pT_b[mc]])
            kl = bank()
            for mc in range(2):
                P.op(PE, (lambda mc, kl: lambda e: e.matmul(ps[kl][:], lhsT=ones_bf[:], rhs=pT[mc][:], start=(mc == 0), stop=(mc == 1)))(mc, kl),
                     reads=[pT_b[mc], b_const], writes=[ps_b[kl]])
            P.op(ACT, (lambda kl: lambda e: e.activation(out=rl[:], in_=ps[kl][:], func=AF.Ln))(kl), reads=[ps_b[kl]], writes=[rl_b])
            P.op(ACT, lambda e: e.activation(out=rl[:], in_=rl[:], func=AF.Exp, scale=-1.0), reads=[rl_b], writes=[rl_b])
            for oc in range(2):
                k = bank()
                for mc in range(2):
                    P.op(PE, (lambda hh, oc, mc, k: lambda e: e.matmul(ps[k][:], lhsT=vm[:, mc, hh * 256 + oc * 128:hh * 256 + (oc + 1) * 128],
                                                                       rhs=pT[mc][:], start=(mc == 0), stop=(mc == 1)))(hh, oc, mc, k),
                         reads=[vm_b, pT_b[mc]], writes=[ps_b[k]])
                ci = dst_c0 + 2 * hh + oc
                P.op(DVE, (lambda ci, k: lambda e: e.tensor_tensor(out=dst[:, ci, :], in0=ps[k][:], in1=rl[:], op=ALU.mult))(ci, k),
                     reads=[ps_b[k], rl_b], writes=[dst_b[ci]])

    def rope_chunk(k, dstT, dst_b, c, cs_t, cs_b, rtmp):
        kb, kb_b, t1, t1_b, t2, t2_b = rtmp
        copy_op(ACT, kb[:], ps[k][:], [ps_b[k]], [kb_b])
        P.op(DVE, (lambda k: lambda e: e.tensor_tensor(out=t2[:], in0=ps[k][:], in1=cs_t[:, 0, :], op=ALU.mult))(k),
             reads=[ps_b[k], cs_b], writes=[t2_b])
        k2 = bank()
        P.op(PE, (lambda k2: lambda e: e.matmul(ps[k2][:], lhsT=perm_bf[:], rhs=kb[:], start=True, stop=True))(k2),
             reads=[kb_b, b_const], writes=[ps_b[k2]])
        P.op(DVE, (lambda k2: lambda e: e.tensor_tensor(out=t1[:], in0=ps[k2][:], in1=cs_t[:, 1, :], op=ALU.mult))(k2),
             reads=[ps_b[k2], cs_b], writes=[t1_b])
        P.op(POOL, (lambda c: lambda e: e.tensor_tensor(out=dstT[:, c, :], in0=t1[:], in1=t2[:], op=ALU.add))(c),
             reads=[t1_b, t2_b], writes=[dst_b[c]])

    def rope_scratch():
        kb = sb("rkb", [128, T], BF16)
        t1 = sb("rt1", [128, T], F32)
        t2 = sb("rt2", [128, T], F32)
        return (kb, Buf("rkb"), t1, Buf("rt1"), t2, Buf("rt2"))

    def load_cs(cs_t, cs_b, t):
        P.dma(SP, lambda e: e.dma_start(out=cs_t[:], in_=cs_d[:, :, t * T:(t + 1) * T].rearrange("a p s -> p a s")), cs_b, writes=[cs_b])

    dbg_stop = [False]

    def stop(name):
        if stop_after == name:
            dbg_stop[0] = True
        return dbg_stop[0]

    def mixer_pool_phase(l):
        base = cx.off
        w1 = sb("w_in", [128, DC, 2 * D], BF16)
        w1_b = [Buf(f"w_in{i}") for i in range(DC)]
        w2_off = (cx.off + 31) // 32 * 32
        w2 = sb("w_out", [128, 16, D], BF16)
        w2_b = [Buf(f"w_out{i}") for i in range(16)]
        w2v = sb_at("w_mkvv", [128, DC, 2 * D], BF16, w2_off)
        w2v_b = [Buf(f"w_mkvv{i}") for i in range(DC)]
        pw = sb("pw", [128, 8, 256], BF16)
        pw_b = [Buf(f"pw{i}") for i in range(8)]
        kmT = sb("kmT", [128, DC, NMEM], BF16)
        vm = sb("vm", [128, 2, D], BF16)
        kmT_b, vm_b = Buf("kmT"), Buf("vm")
        nscr = norm_scratch("m")
        hbufs = [sb(f"hA{i}", [128, DC, T], F32) for i in range(2)]
        hbufs_b = [[Buf(f"hA{i}_{c}") for c in range(DC)] for i in range(2)]
        xn = sb("xn", [128, DC, T], BF16)
        xn_b = [Buf(f"xn{c}") for c in range(DC)]
        uext = sb("uext", [128, DC, 16 + T], F32)
        uext_b = [Buf(f"uext{c}") for c in range(DC)]
        lv = [sb(f"lv{i}", [128, 16 + T], F32) for i in range(4)]
        lv_b = [Buf(f"lv{i}") for i in range(4)]
        pq_off = (cx.off + 31) // 32 * 32
        pooled = sb("pooled", [128, DC, T], BF16)
        pooled_b = [Buf(f"pooled{c}") for c in range(DC)]
        qm = sb("qm", [128, DC, T], BF16)
        qm_b = [Buf(f"qm{c}") for c in range(DC)]
        cat = sb("cat", [128, 16, T], BF16)
        cat_b = [Buf(f"cat{c}") for c in range(16)]
        pT = [sb(f"pT{i}", [128, T], BF16) for i in range(2)]
        pT_b = [Buf(f"pT{i}") for i in range(2)]
        rl = sb("rl", [128, T], F32)
        rl_b = Buf("rl")
        stage = make_stage(2)
        xin = None
        if l == 0:
            xin = [sb_at(f"xin{i}", [128, D], F32, pq_off + i * 4096) for i in range(4)]
            xin_b = [Buf(f"xin{i}") for i in range(4)]
            xin_al = [pooled_b[0:4], pooled_b[4:8], qm_b[0:4], qm_b[4:8]]

        memrow = [sb(f"memrow{i}", [128, D], F32) for i in range(2)] if l != 0 else xin[0:2]
        memrow_b = [Buf(f"memrow{i}") for i in range(2)]
        memT, memT_b = hbufs[1], hbufs_b[1]
        for blk in range(2):
            P.dma(SP, (lambda blk: lambda e: e.dma_start(out=memrow[blk][:], in_=mem[blk * 128:(blk + 1) * 128, :]))(blk),
                  memrow_b[blk], writes=[memrow_b[blk]])
        for c in range(DC):
            k = bank()
            for blk in range(2):
                P.op(PE, (lambda c, blk, k: lambda e: e.transpose(ps[k][:, blk * 128:(blk + 1) * 128], memrow[blk][:, c * 128:(c + 1) * 128], ident[:]))(c, blk, k),
                     reads=[memrow_b[blk], b_const], writes=[ps_b[k]])
            copy_op(evac_eng(), memT[:, c, 0:NMEM], ps[k][:, 0:NMEM], [ps_b[k]], [memT_b[c]])
        mem_kv_prologue(l, w2v, w2v_b, kmT, kmT_b, vm, vm_b, memT, memT_b, xn, xn_b, nscr)
        P.barrier()
        load_w(w1, w1_b, w_in[l], DC)
        for g in range(4):
            for kc in range(2):
                i = g * 2 + kc
                P.dma(POOL, (lambda g, kc, i: lambda e: e.dma_start(out=pw[:, i, :], in_=pool_w[l, g, kc * 128:(kc + 1) * 128, :]))(g, kc, i),
                      pw_b[i], writes=[pw_b[i]])
        load_w(w2, w2_b, w_out[l], 16)
        for c in range(DC):
            P.op(POOL, (lambda c: lambda e: e.memset(uext[:, c, 0:16], 0.0))(c), writes=[uext_b[c]])

        def load_dma(t):
            hb, hb_b = hbufs[t % 2], hbufs_b[t % 2]
            if l == 0:
                for blk in range(4):
                    P.dma(SP, (lambda blk: lambda e: e.dma_start(out=xin[blk][:], in_=x[t * T + blk * 128:t * T + (blk + 1) * 128, :]))(blk),
                          xin_b[blk], writes=[xin_b[blk]] + xin_al[blk])
            else:
                load_h_chunks(hb, hb_b, t)

        def load_post(t):
            if l != 0:
                return
            hb, hb_b = hbufs[t % 2], hbufs_b[t % 2]
            for c in range(DC):
                k = bank()
                for blk in range(4):
                    P.op(PE, (lambda c, blk, k: lambda e: e.transpose(ps[k][:, blk * 128:(blk + 1) * 128], xin[blk][:, c * 128:(c + 1) * 128], ident[:]))(c, blk, k),
                         reads=[xin_b[blk], b_const] + xin_al[blk], writes=[ps_b[k]])
                copy_op(evac_eng(), hb[:, c, :], ps[k][:], [ps_b[k]], [hb_b[c]])

        load_dma(0)
        load_post(0)
        for t in range(NT):
            hb, hb_b = hbufs[t % 2], hbufs_b[t % 2]
            if t + 1 < NT and l != 0:
                load_dma(t + 1)
            norm_tile(hb, hb_b, 0 + l * 8, xn, xn_b, T, nscr)

            def ev_in(oc, k):
                if oc < DC:
                    copy_op(evac_eng(), uext[:, oc, 16:16 + T], ps[k][:], [ps_b[k]], [uext_b[oc]])
                else:
                    copy_op(evac_eng(), qm[:, oc - DC, :], ps[k][:], [ps_b[k]], [qm_b[oc - DC]])
            proj(w1, w1_b, DC, xn, xn_b, range(16), ev_in)
            W = 16 + T
            for c in range(DC):
                g = c // 2
                A, A_b, B, B_b = lv[(c % 2) * 2], lv_b[(c % 2) * 2], lv[(c % 2) * 2 + 1], lv_b[(c % 2) * 2 + 1]
                PENG = POOL if g < 2 else DVE
                P.op(PENG, (lambda c, A: lambda e: e.tensor_tensor(out=A[:, 1:W], in0=uext[:, c, 1:W], in1=uext[:, c, 0:W - 1], op=ALU.add))(c, A),
                     reads=[uext_b[c]], writes=[A_b])
                fin, fin_b = A, A_b
                if g >= 1:
                    P.op(PENG, (lambda A, B: lambda e: e.tensor_tensor(out=B[:, 3:W], in0=A[:, 3:W], in1=A[:, 1:W - 2], op=ALU.add))(A, B),
                         reads=[A_b], writes=[B_b])
                    fin, fin_b = B, B_b
                if g >= 2:
                    P.op(PENG, (lambda A, B: lambda e: e.tensor_tensor(out=A[:, 7:W], in0=B[:, 7:W], in1=B[:, 3:W - 4], op=ALU.add))(A, B),
                         reads=[B_b], writes=[A_b])
                    fin, fin_b = A, A_b
                if g >= 3:
                    P.op(PENG, (lambda A, B: lambda e: e.tensor_tensor(out=B[:, 15:W], in0=A[:, 15:W], in1=A[:, 7:W - 8], op=ALU.add))(A, B),
                         reads=[A_b], writes=[B_b])
                    fin, fin_b = B, B_b
                if t == 0:
                    P.op(DVE, (lambda g, fin: lambda e: e.tensor_tensor(out=fin[:, 16:32], in0=fin[:, 16:32], in1=rcw[:, g * 16:(g + 1) * 16], op=ALU.mult))(g, fin),
                         reads=[fin_b, b_const], writes=[fin_b])
                P.op(DVE, (lambda c, g, fin: lambda e: e.scalar_tensor_tensor(out=pooled[:, c, :], in0=fin[:, 16:W], scalar=1.0 / POOL_W[g], in1=uext[:, c, 16:W],
                                                                            op0=ALU.mult, op1=ALU.subtract))(c, g, fin),
                     reads=[fin_b, uext_b[c]], writes=[pooled_b[c]])
                P.op(POOL, (lambda c: lambda e: e.tensor_copy(out=uext[:, c, 0:16], in_=uext[:, c, T:T + 16]))(c),
                     reads=[uext_b[c]], writes=[uext_b[c]])
            mem_attention(qm, qm_b, kmT, kmT_b, vm, vm_b, pT, pT_b, rl, rl_b, cat, cat_b, DC)
            for g in range(4):
                for oc in range(2):
                    k = bank()
                    for kc in range(2):
                        P.op(PE, (lambda g, oc, kc, k: lambda e: e.matmul(ps[k][:], lhsT=pw[:, g * 2 + kc, oc * 128:(oc + 1) * 128], rhs=pooled[:, 2 * g + kc, :],
                                                                          start=(kc == 0), stop=(kc == 1)))(g, oc, kc, k),
                             reads=[pw_b[g * 2 + kc], pooled_b[2 * g + kc]], writes=[ps_b[k]])
                    ci = 2 * g + oc
                    P.op(DVE, (lambda ci, k: lambda e: e.tensor_scalar(out=cat[:, ci, :], in0=ps[k][:], scalar1=vecs[:, 96 + l * 8 + ci:96 + l * 8 + ci + 1], scalar2=None,
                                                                       op0=ALU.mult))(ci, k),
                         reads=[ps_b[k], b_vecs], writes=[cat_b[ci]])
            if t + 1 < NT and l == 0:
                load_dma(t + 1)

            def ev_out(oc, k):
                residual_store(stage, hb, hb_b, oc, k, t)
            proj(w2, w2_b, 16, cat, cat_b, range(DC), ev_out)
            if t + 1 < NT:
                load_post(t + 1)
        P.barrier()
        cx.off = base

    def ffn_phase(l):
        base = cx.off
        wg = sb("wg", [128, DC, FF], BF16)
        wu = sb("wu", [128, DC, FF], BF16)
        wd = sb("wd", [128, FC, D], BF16)
        wg_b = [Buf(f"wg{i}") for i in range(DC)]
        wu_b = [Buf(f"wu{i}") for i in range(DC)]
        wd_b = [Buf(f"wd{i}") for i in range(FC)]
        nscr = norm_scratch("f")
        hA = sb("hAf", [128, DC, T], F32)
        hA_b = [Buf(f"hAf{c}") for c in range(DC)]
        xn = sb("xnf", [128, DC, T], BF16)
        xn_b = [Buf(f"xnf{c}") for c in range(DC)]
        aT = sb("aT", [128, FC, T], BF16)
        aT_b = [Buf(f"aT{c}") for c in range(FC)]
        sg = [sb(f"sg{i}", [128, T], F32) for i in range(2)]
        sg_b = [Buf(f"sg{i}") for i in range(2)]
        stage = make_stage(2)
        load_w(wg, wg_b, w_gate[l], DC)
        load_w(wu, wu_b, w_up[l], DC)
        load_w(wd, wd_b, w_down[l], FC)
        load_h_chunks(hA, hA_b, 0)
        for t in range(NT):
            norm_tile(hA, hA_b, 32 + l * 8, xn, xn_b, T, nscr)
            for fc in range(FC):
                kg = bank()
                ku = bank()
                for kc in range(DC):
                    P.op(PE, (lambda fc, kc, kg: lambda e: e.matmul(ps[kg][:], lhsT=wg[:, kc, fc * 128:(fc + 1) * 128], rhs=xn[:, kc, :],
                                                                    start=(kc == 0), stop=(kc == DC - 1)))(fc, kc, kg),
                         reads=[wg_b[kc], xn_b[kc]], writes=[ps_b[kg]])
                for kc in range(DC):
                    P.op(PE, (lambda fc, kc, ku: lambda e: e.matmul(ps[ku][:], lhsT=wu[:, kc, fc * 128:(fc + 1) * 128], rhs=xn[:, kc, :],
                                                                    start=(kc == 0), stop=(kc == DC - 1)))(fc, kc, ku),
                         reads=[wu_b[kc], xn_b[kc]], writes=[ps_b[ku]])
                s = fc % 2
                P.op(ACT, (lambda s, kg: lambda e: e.activation(out=sg[s][:], in_=ps[kg][:], func=AF.Silu))(s, kg), reads=[ps_b[kg]], writes=[sg_b[s]])
                P.op(DVE, (lambda fc, s, ku: lambda e: e.tensor_tensor(out=aT[:, fc, :], in0=sg[s][:], in1=ps[ku][:], op=ALU.mult))(fc, s, ku),
                     reads=[sg_b[s], ps_b[ku]], writes=[aT_b[fc]])

            def ev_down(oc, k):
                residual_store(stage, hA, hA_b, oc, k, t)
                if t + 1 < NT:
                    load_h_chunks(hA, hA_b, t + 1, cs=[oc])
            proj(wd, wd_b, FC, aT, aT_b, range(DC), ev_down)
        P.barrier()
        cx.off = base

    def kv_phase():
        base = cx.off
        wk = sb("wkv", [128, DC, 2 * D], BF16)
        wk_b = [Buf(f"wkv{i}") for i in range(DC)]
        nscr = norm_scratch("k")
        hbufs = [sb(f"hAk{i}", [128, DC, T], F32) for i in range(2)]
        hbufs_b = [[Buf(f"hAk{i}_{c}") for c in range(DC)] for i in range(2)]
        xn = sb("xnk", [128, DC, T], BF16)
        xn_b = [Buf(f"xnk{c}") for c in range(DC)]
        ksb = [sb(f"ksb{i}", [128, DC, T], BF16) for i in range(2)]
        ksb_b = [[Buf(f"ksb{i}_{c}") for c in range(DC)] for i in range(2)]
        vsb = [sb(f"vsb{i}", [128, 4, D], BF16) for i in range(2)]
        vsb_b = [[Buf(f"vsb{i}_{c}") for c in range(4)] for i in range(2)]
        cs_t = [sb(f"cs{i}", [128, 2, T], F32) for i in range(2)]
        cs_b = [Buf(f"cs{i}") for i in range(2)]
        rtmp = rope_scratch()
        load_w(wk, wk_b, w_kv, DC)
        load_h_chunks(hbufs[0], hbufs_b[0], 0)
        load_cs(cs_t[0], cs_b[0], 0)
        for t in range(NT):
            hb, hb_b = hbufs[t % 2], hbufs_b[t % 2]
            if t + 1 < NT:
                load_h_chunks(hbufs[(t + 1) % 2], hbufs_b[(t + 1) % 2], t + 1)
                load_cs(cs_t[(t + 1) % 2], cs_b[(t + 1) % 2], t + 1)
            norm_tile(hb, hb_b, 112, xn, xn_b, T, nscr)
            kk, kk_b = ksb[t % 2], ksb_b[t % 2]

            def ev_k(oc, k):
                import os
                if os.environ.get("KV_NOROPE"):
                    copy_op(ACT, kk[:, oc, :], ps[k][:], [ps_b[k]], [kk_b[oc]])
                else:
                    rope_chunk(k, kk, kk_b, oc, cs_t[t % 2], cs_b[t % 2], rtmp)
                P.dma(SP, (lambda oc, t, kk: lambda e: e.dma_start(out=kT[oc * 128:(oc + 1) * 128, t * T:(t + 1) * T], in_=kk[:, oc, :]))(oc, t, kk),
                      kk_b[oc], reads=[kk_b[oc]])
            proj(wk, wk_b, DC, xn, xn_b, range(DC), ev_k)
            vv, vv_b = vsb[t % 2], vsb_b[t % 2]
            import os
            for blk in (range(4) if not os.environ.get("KV_NOV") else []):
                for half in range(2):
                    k = bank()
                    for kc in range(DC):
                        P.op(PE, (lambda blk, half, kc, k: lambda e: e.matmul(ps[k][:], lhsT=xn[:, kc, blk * 128:(blk + 1) * 128],
                                                                              rhs=wk[:, kc, D + half * 512:D + (half + 1) * 512],
                                                                              start=(kc == 0), stop=(kc == DC - 1)))(blk, half, kc, k),
                             reads=[xn_b[kc], wk_b[kc]], writes=[ps_b[k]])
                    copy_op(evac_eng(), vv[:, blk, half * 512:(half + 1) * 512], ps[k][:], [ps_b[k]], [vv_b[blk]])
                P.dma(SP, (lambda blk, t, vv: lambda e: e.dma_start(out=vS[:, :, t * 4 + blk, :].rearrange("h p f -> p h f"),
                                                                    in_=vv[:, blk, :].rearrange("p (h f) -> p h f", h=4)))(blk, t, vv),
                      vv_b[blk], reads=[vv_b[blk]])
        P.barrier()
        cx.off = base

    def diff_a_phase(l):
        j = l - N_A
        base = cx.off
        w1 = sb("w_inq", [128, DC, 2 * D], BF16)
        w1_b = [Buf(f"w_inq{i}") for i in range(DC)]
        w2 = sb("w_mkv", [128, DC, 2 * D], BF16)
        w2_b = [Buf(f"w_mkv{i}") for i in range(DC)]
        kmT = sb("kmTq", [128, DC, NMEM], BF16)
        vm = sb("vmq", [128, 2, D], BF16)
        kmT_b, vm_b = Buf("kmTq"), Buf("vmq")
        nscr = norm_scratch("q")
        hbufs = [sb(f"hAq{i}", [128, DC, T], F32) for i in range(2)]
        hbufs_b = [[Buf(f"hAq{i}_{c}") for c in range(DC)] for i in range(2)]
        xn = sb("xnq", [128, DC, T], BF16)
        xn_b = [Buf(f"xnq{c}") for c in range(DC)]
        qsb = [sb(f"qsb{i}", [128, DC, T], BF16) for i in range(2)]
        qsb_b = [[Buf(f"qsb{i}_{c}") for c in range(DC)] for i in range(2)]
        qm = sb("qmq", [128, DC, T], BF16)
        qm_b = [Buf(f"qmq{c}") for c in range(DC)]
        mo = [sb(f"mo{i}", [128, DC, T], BF16) for i in range(2)]
        mo_b = [[Buf(f"mo{i}_{c}") for c in range(DC)] for i in range(2)]
        pT = [sb(f"pTq{i}", [128, T], BF16) for i in range(2)]
        pT_b = [Buf(f"pTq{i}") for i in range(2)]
        rl = sb("rlq", [128, T], F32)
        rl_b = Buf("rlq")
        cs_t = [sb(f"csq{i}", [128, 2, T], F32) for i in range(2)]
        cs_b = [Buf(f"csq{i}") for i in range(2)]
        rtmp = rope_scratch()
        memrow = [sb(f"memrowq{i}", [128, D], F32) for i in range(2)]
        memrow_b = [Buf(f"memrowq{i}") for i in range(2)]
        memT, memT_b = hbufs[1], hbufs_b[1]
        for blk in range(2):
            P.dma(SP, (lambda blk: lambda e: e.dma_start(out=memrow[blk][:], in_=mem[blk * 128:(blk + 1) * 128, :]))(blk),
                  memrow_b[blk], writes=[memrow_b[blk]])
        for c in range(DC):
            k = bank()
            for blk in range(2):
                P.op(PE, (lambda c, blk, k: lambda e: e.transpose(ps[k][:, blk * 128:(blk + 1) * 128], memrow[blk][:, c * 128:(c + 1) * 128], ident[:]))(c, blk, k),
                     reads=[memrow_b[blk], b_const], writes=[ps_b[k]])
            copy_op(evac_eng(), memT[:, c, 0:NMEM], ps[k][:, 0:NMEM], [ps_b[k]], [memT_b[c]])
        mem_kv_prologue(l, w2, w2_b, kmT, kmT_b, vm, vm_b, memT, memT_b, xn, xn_b, nscr)
        load_w(w1, w1_b, w_in[l], DC)
        load_h_chunks(hbufs[0], hbufs_b[0], 0)
        load_cs(cs_t[0], cs_b[0], 0)
        for t in range(NT):
            hb, hb_b = hbufs[t % 2], hbufs_b[t % 2]
            if t + 1 < NT:
                load_h_chunks(hbufs[(t + 1) % 2], hbufs_b[(t + 1) % 2], t + 1)
                load_cs(cs_t[(t + 1) % 2], cs_b[(t + 1) % 2], t + 1)
            norm_tile(hb, hb_b, 0 + l * 8, xn, xn_b, T, nscr)
            qq, qq_b = qsb[t % 2], qsb_b[t % 2]
            mm, mm_b = mo[t % 2], mo_b[t % 2]

            def ev_in(oc, k):
                if oc < DC:
                    rope_chunk(k, qq, qq_b, oc, cs_t[t % 2], cs_b[t % 2], rtmp)
                    P.dma(SP, (lambda oc, t, qq: lambda e: e.dma_start(out=qT[oc * 128:(oc + 1) * 128, t * T:(t + 1) * T], in_=qq[:, oc, :]))(oc, t, qq),
                          qq_b[oc], reads=[qq_b[oc]], writes=[qT_b[t][oc]])
                else:
                    copy_op(evac_eng(), qm[:, oc - DC, :], ps[k][:], [ps_b[k]], [qm_b[oc - DC]])
            proj(w1, w1_b, DC, xn, xn_b, range(16), ev_in)
            mem_attention(qm, qm_b, kmT, kmT_b, vm, vm_b, pT, pT_b, rl, rl_b, mm, mm_b, 0)
            for c in range(DC):
                P.dma(SP, (lambda c, t, mm: lambda e: e.dma_start(out=memoT[c * 128:(c + 1) * 128, t * T:(t + 1) * T], in_=mm[:, c, :]))(c, t, mm),
                      mm_b[c], reads=[mm_b[c]], writes=[memoT_b[t][c]])
        P.barrier()
        cx.off = base

    def diff_b_phase(l):
        j = l - N_A
        li = lambda_init[l]
        base = cx.off
        kres = [[sb(f"kres{i}_{m}", [128, S], BF16) for m in range(2)] for i in range(2)]
        kres_b = [[Buf(f"kres{i}_{m}") for m in range(2)] for i in range(2)]
        vres = [sb(f"vres{i}", [128, S // 128, 256], BF16) for i in range(2)]
        vres_b = [Buf(f"vres{i}") for i in range(2)]
        qt_sb = [sb(f"qt{i}", [128, 2, T], BF16) for i in range(2)]
        qt_b = [[Buf(f"qt{i}_{m}") for m in range(2)] for i in range(2)]
        NPT = 6
        pT = [sb(f"pTa{i}", [128, T], BF16) for i in range(NPT)]
        pT_b = [Buf(f"pTa{i}") for i in range(NPT)]
        rl = sb("rla", [128, T], F32)
        rl_b = Buf("rla")
        accD = [sb(f"accD{i}", [128, T], F32) for i in range(2)]
        accD_b = [Buf(f"accD{i}") for i in range(2)]
        accP = [sb(f"accP{i}", [128, T], F32) for i in range(2)]
        accP_b = [Buf(f"accP{i}") for i in range(2)]
        sumbf = sb("sumbf", [128, T], BF16)
        sumbf_b = Buf("sumbf")
        tmpo = sb("tmpo", [128, T], F32)
        tmpo_b = Buf("tmpo")
        oacc = sb("oacc", [128, 2, T], F32)
        oacc_b = [Buf(f"oacc{i}") for i in range(2)]
        sq = [sb(f"sqa{i}", [128, T], BF16) for i in range(2)]
        sq_b = [Buf(f"sqa{i}") for i in range(2)]
        rs = sb("rsa", [128, T], F32)
        rs_b = Buf("rsa")
        rstd = sb("rstda", [128, T], F32)
        rstd_b = Buf("rstda")
        mixs = [sb(f"mixs{i}", [128, 2, T], BF16) for i in range(2)]
        mixs_b = [[Buf(f"mixs{i}_{c}") for c in range(2)] for i in range(2)]
        sc = 128.0 ** -0.5
        LOOK = 2

        def load_head(hh):
            i = hh % 2
            for m in range(2):
                c = 2 * hh + m
                P.dma(SP, (lambda m, c: lambda e: e.dma_start(out=kres[i][m][:], in_=kT[c * 128:(c + 1) * 128, :]))(m, c),
                      kres_b[i][m], writes=[kres_b[i][m]])
            P.dma(SP, lambda e: e.dma_start(out=vres[i][:], in_=vS[hh]), vres_b[i], writes=[vres_b[i]])

        def load_q(g):
            hh, qt = divmod(g, NT)
            slot = g % 2
            for m in range(2):
                c = 2 * hh + m
                P.dma(SP, (lambda m, c: lambda e: e.dma_start(out=qt_sb[slot][:, m, :], in_=qT[c * 128:(c + 1) * 128, qt * T:(qt + 1) * T]))(m, c),
                      qt_b[slot][m], reads=[qT_b[qt][c]], writes=[qt_b[slot][m]])

        its = []
        for g in range(4 * NT):
            hh, qt = divmod(g, NT)
            nkc = 4 * (qt + 1)
            for m in range(2):
                for kc in range(nkc):
                    its.append(dict(g=g, hh=hh, qt=qt, m=m, kc=kc, nkc=nkc))
        N = len(its)
        pcnt = [0]

        def emit_score(it):
            g, hh, qt, m, kc = it["g"], it["hh"], it["qt"], it["m"], it["kc"]
            i, slot = hh % 2, g % 2
            if m == 0 and kc == 0:
                if g == 0:
                    load_head(0)
                    load_q(0)
                if g + 1 < 4 * NT:
                    load_q(g + 1)
            k = bank()
            p = pcnt[0] % NPT
            pcnt[0] += 1
            c0 = (kc - 4 * qt) * 128 if kc >= 4 * qt else 0
            it["k"], it["p"], it["c0"] = k, p, c0
            P.op(PE, (lambda m, kc, k, i, slot, c0: lambda e: e.matmul(ps[k][:, c0:T], lhsT=kres[i][m][:, kc * 128:(kc + 1) * 128], rhs=qt_sb[slot][:, m, c0:T],
                                                                       start=True, stop=True))(m, kc, k, i, slot, c0),
                 reads=[kres_b[i][m], qt_b[slot][m]], writes=[ps_b[k]])
            P.op(ACT, (lambda k, p, c0: lambda e: e.activation(out=pT[p][:, c0:T], in_=ps[k][:, c0:T], func=AF.Exp, scale=sc))(k, p, c0),
                 reads=[ps_b[k]], writes=[pT_b[p]])
            if kc >= 4 * qt:
                dd = kc - 4 * qt
                P.op(DVE, (lambda p, dd, c0: lambda e: e.tensor_tensor(out=pT[p][:, c0:c0 + 128], in0=pT[p][:, c0:c0 + 128], in1=masks[:, dd, c0:c0 + 128], op=ALU.mult))(p, dd, c0),
                     reads=[pT_b[p], b_const], writes=[pT_b[p]])

        acc = {}

        def finalize(it):
            g, hh, qt, m = it["g"], it["hh"], it["qt"], it["m"]
            slot = g % 2
            ko, kl = acc["ko"], acc["kl"]
            a_ = acc["set"]
            if acc["nD"] > 0 and acc["nP"] > 0:
                P.op(DVE, (lambda a_: lambda e: e.tensor_tensor(out=sumbf[:], in0=accD[a_][:], in1=accP[a_][:], op=ALU.add))(a_),
                     reads=[accD_b[a_], accP_b[a_]], writes=[sumbf_b])
            elif acc["nD"] > 0:
                P.op(DVE, (lambda a_: lambda e: e.tensor_copy(out=sumbf[:], in_=accD[a_][:]))(a_), reads=[accD_b[a_]], writes=[sumbf_b])
            elif acc["nP"] > 0:
                P.op(DVE, (lambda a_: lambda e: e.tensor_copy(out=sumbf[:], in_=accP[a_][:]))(a_), reads=[accP_b[a_]], writes=[sumbf_b])
            if acc["nD"] + acc["nP"] > 0:
                P.op(PE, (lambda kl: lambda e: e.matmul(ps[kl][:], lhsT=ones_bf[:], rhs=sumbf[:], start=False, stop=True))(kl),
                     reads=[sumbf_b, b_const], writes=[ps_b[kl]])
            P.op(ACT, (lambda kl: lambda e: e.activation(out=rl[:], in_=ps[kl][:], func=AF.Ln))(kl), reads=[ps_b[kl]], writes=[rl_b])
            P.op(ACT, lambda e: e.activation(out=rl[:], in_=rl[:], func=AF.Exp, scale=-1.0), reads=[rl_b], writes=[rl_b])
            for oc in range(2):
                if m == 0:
                    P.op(DVE, (lambda oc, k: lambda e: e.tensor_tensor(out=oacc[:, oc, :], in0=ps[k][:], in1=rl[:], op=ALU.mult))(oc, ko[oc]),
                         reads=[ps_b[ko[oc]], rl_b], writes=[oacc_b[oc]])
                else:
                    P.op(DVE, (lambda oc, k: lambda e: e.tensor_tensor(out=tmpo[:], in0=ps[k][:], in1=rl[:], op=ALU.mult))(oc, ko[oc]),
                         reads=[ps_b[ko[oc]], rl_b], writes=[tmpo_b])
                    P.op(DVE, (lambda oc: lambda e: e.scalar_tensor_tensor(out=oacc[:, oc, :], in0=tmpo[:], scalar=misc[:, 3 + j:4 + j], in1=oacc[:, oc, :],
                                                                           op0=ALU.mult, op1=ALU.add))(oc),
                         reads=[tmpo_b, oacc_b[oc], b_misc], writes=[oacc_b[oc]])
            release(ko[0])
            release(ko[1])
            release(kl)
            if m == 0:
                return
            k = bank()
            for oc in range(2):
                P.op(ACT, (lambda oc: lambda e: e.activation(out=sq[oc][:], in_=oacc[:, oc, :], func=AF.Square))(oc), reads=[oacc_b[oc]], writes=[sq_b[oc]])
                P.op(PE, (lambda oc, k: lambda e: e.matmul(ps[k][:], lhsT=ones_bf[:], rhs=sq[oc][:], start=(oc == 0), stop=(oc == 1)))(oc, k),
                     reads=[sq_b[oc], b_const], writes=[ps_b[k]])
            P.op(ACT, (lambda k: lambda e: e.activation(out=rs[:], in_=ps[k][:], func=AF.Ln, bias=misc[:, 1 + j:2 + j], scale=1.0 / (256.0 * (1.0 - li) ** 2)))(k),
                 reads=[ps_b[k], b_misc], writes=[rs_b])
            P.op(ACT, lambda e: e.activation(out=rstd[:], in_=rs[:], func=AF.Exp, scale=-0.5), reads=[rs_b], writes=[rstd_b])
            ms, ms_b = mixs[slot], mixs_b[slot]
            for oc in range(2):
                P.op(DVE, (lambda oc, ms: lambda e: e.scalar_tensor_tensor(out=ms[:, oc, :], in0=oacc[:, oc, :], scalar=vecs[:, 128 + 2 * j + oc:128 + 2 * j + oc + 1],
                                                                          in1=rstd[:], op0=ALU.mult, op1=ALU.mult))(oc, ms),
                     reads=[oacc_b[oc], rstd_b, b_vecs], writes=[ms_b[oc]])
                c = 2 * hh + oc
                P.dma(SP, (lambda oc, c, ms, qt: lambda e: e.dma_start(out=mixT[c * 128:(c + 1) * 128, qt * T:(qt + 1) * T], in_=ms[:, oc, :]))(oc, c, ms, qt),
                      ms_b[oc], reads=[ms_b[oc]], writes=[mixT_b[qt][c]])

        for n in range(min(LOOK, N)):
            emit_score(its[n])
        for n in range(N):
            it = its[n]
            if n + LOOK < N:
                emit_score(its[n + LOOK])
            kc, nkc, p, i = it["kc"], it["nkc"], it["p"], it["hh"] % 2
            if kc == 0 and it["m"] == 0 and it["qt"] == 0 and it["hh"] + 1 < 4:
                load_head(it["hh"] + 1)
            if kc == 0:
                acc["ko"] = [bank(hold=True), bank(hold=True)]
                acc["kl"] = bank(hold=True)
                acc["set"] = acc.get("set", 1) ^ 1
                acc["nD"] = 0
                acc["nP"] = 0
                kinds = []
                for kk_ in range(nkc):
                    if kk_ >= 4 * it["qt"]:
                        kinds.append("PE")
                    else:
                        kinds.append(("PE", "DVE", "POOL", "DVE", "POOL", "DVE")[acc.get("rr", 0) % 6])
                        acc["rr"] = acc.get("rr", 0) + 1
                acc["kinds"] = kinds
                acc["pe_first"] = kinds.index("PE")
                acc["pe_last"] = max(q_ for q_ in range(nkc) if kinds[q_] == "PE")
                acc["has_acc"] = any(q_ != "PE" for q_ in kinds)
            ko, kl = acc["ko"], acc["kl"]
            a_ = acc["set"]
            kind = acc["kinds"][kc]
            if kind == "POOL":
                if acc["nP"] == 0:
                    P.op(POOL, (lambda p, a_: lambda e: e.tensor_copy(out=accP[a_][:], in_=pT[p][:]))(p, a_), reads=[pT_b[p]], writes=[accP_b[a_]])
                else:
                    P.op(POOL, (lambda p, a_: lambda e: e.tensor_tensor(out=accP[a_][:], in0=accP[a_][:], in1=pT[p][:], op=ALU.add))(p, a_),
                         reads=[pT_b[p], accP_b[a_]], writes=[accP_b[a_]])
                acc["nP"] += 1
            elif kind == "DVE":
                if acc["nD"] == 0:
                    P.op(DVE, (lambda p, a_: lambda e: e.tensor_copy(out=accD[a_][:], in_=pT[p][:]))(p, a_), reads=[pT_b[p]], writes=[accD_b[a_]])
                else:
                    P.op(DVE, (lambda p, a_: lambda e: e.tensor_tensor(out=accD[a_][:], in0=accD[a_][:], in1=pT[p][:], op=ALU.add))(p, a_),
                         reads=[pT_b[p], accD_b[a_]], writes=[accD_b[a_]])
                acc["nD"] += 1
            else:
                P.op(PE, (lambda p, kl, st_, sp_, c0: lambda e: e.matmul(ps[kl][:, c0:T], lhsT=ones_bf[:], rhs=pT[p][:, c0:T], start=st_, stop=sp_))(
                    p, kl, kc == acc["pe_first"], (kc == acc["pe_last"]) and not acc["has_acc"], it["c0"]),
                     reads=[pT_b[p], b_const], writes=[ps_b[kl]])
            for oc in range(2):
                P.op(PE, (lambda kc, oc, p, i, kb_, st_, sp_, c0: lambda e: e.matmul(ps[kb_][:, c0:T], lhsT=vres[i][:, kc, oc * 128:(oc + 1) * 128], rhs=pT[p][:, c0:T],
                                                                                     start=st_, stop=sp_))(kc, oc, p, i, ko[oc], kc == 0, kc == nkc - 1, it["c0"]),
                     reads=[vres_b[i], pT_b[p]], writes=[ps_b[ko[oc]]])
            if kc == nkc - 1:
                finalize(it)
        P.barrier()
        cx.off = base

    def diff_c_phase(l):
        base = cx.off
        w2 = sb("w_outc", [128, 16, D], BF16)
        w2_b = [Buf(f"w_outc{i}") for i in range(16)]
        hbufs = [sb(f"hAc{i}", [128, DC, T], F32) for i in range(2)]
        hbufs_b = [[Buf(f"hAc{i}_{c}") for c in range(DC)] for i in range(2)]
        cats = [sb(f"catc{i}", [128, 16, T], BF16) for i in range(2)]
        cats_b = [[Buf(f"catc{i}_{c}") for c in range(16)] for i in range(2)]
        stage = make_stage(3)
        load_w(w2, w2_b, w_out[l], 16)

        def load(t):
            load_h_chunks(hbufs[t % 2], hbufs_b[t % 2], t)
            for c in range(DC):
                P.dma(SP, (lambda c: lambda e: e.dma_start(out=cats[t % 2][:, c, :], in_=mixT[c * 128:(c + 1) * 128, t * T:(t + 1) * T]))(c),
                      cats_b[t % 2][c], reads=[mixT_b[t][c]], writes=[cats_b[t % 2][c]])
                P.dma(SP, (lambda c: lambda e: e.dma_start(out=cats[t % 2][:, DC + c, :], in_=memoT[c * 128:(c + 1) * 128, t * T:(t + 1) * T]))(c),
                      cats_b[t % 2][DC + c], reads=[memoT_b[t][c]], writes=[cats_b[t % 2][DC + c]])
        load(0)
        for t in range(NT):
            if t + 1 < NT:
                load(t + 1)
            hb, hb_b = hbufs[t % 2], hbufs_b[t % 2]

            def ev_out(oc, k):
                residual_store(stage, hb, hb_b, oc, k, t)
            proj(w2, w2_b, 16, cats[t % 2], cats_b[t % 2], range(DC), ev_out)
        P.barrier()
        cx.off = base

    fin_ops = []

    def final_phase():
        base = cx.off
        nscr = norm_scratch("z")
        hbufs = [sb(f"hAz{i}", [128, DC, T], F32) for i in range(2)]
        hbufs_b = [[Buf(f"hAz{i}_{c}") for c in range(DC)] for i in range(2)]
        yn = sb("yn", [128, DC, T], F32)
        yn_b = [Buf(f"yn{c}") for c in range(DC)]
        rows = [sb(f"rows{i}", [128, D], F32) for i in range(4)]
        rows_b = [Buf(f"rows{i}") for i in range(4)]
        rc = [0]
        load_h_chunks(hbufs[0], hbufs_b[0], 0)
        for t in range(NT):
            hb, hb_b = hbufs[t % 2], hbufs_b[t % 2]
            if t + 1 < NT:
                load_h_chunks(hbufs[(t + 1) % 2], hbufs_b[(t + 1) % 2], t + 1)
            norm_tile(hb, hb_b, 120, yn, yn_b, T, nscr)
            for blk in range(4):
                r = rc[0] % 4
                rc[0] += 1
                for half in range(2):
                    k = bank()
                    for cc in range(4):
                        c = half * 4 + cc
                        P.op(PE, (lambda c, cc, blk, k: lambda e: e.transpose(ps[k][:, cc * 128:(cc + 1) * 128], yn[:, c, blk * 128:(blk + 1) * 128], ident[:]))(c, cc, blk, k),
                             reads=[yn_b[c], b_const], writes=[ps_b[k]])
                    copy_op(evac_eng(), rows[r][:, half * 512:(half + 1) * 512], ps[k][:], [ps_b[k]], [rows_b[r]])
                fin_ops.append(P.dma(SP, (lambda blk, r, t: lambda e: e.dma_start(out=out[t * T + blk * 128:t * T + (blk + 1) * 128, :], in_=rows[r][:]))(blk, r, t),
                                     rows_b[r], reads=[rows_b[r]]))
        cx.off = base

    phases = []
    for l in range(N_A):
        phases.append((f"mix{l}", lambda l=l: mixer_pool_phase(l)))
        phases.append((f"ffn{l}", lambda l=l: ffn_phase(l)))
    phases.append(("kv", kv_phase))
    for l in range(N_A, DEPTH):
        phases.append((f"da{l}", lambda l=l: diff_a_phase(l)))
        phases.append((f"db{l}", lambda l=l: diff_b_phase(l)))
        phases.append((f"dc{l}", lambda l=l: diff_c_phase(l)))
        phases.append((f"ffn{l}", lambda l=l: ffn_phase(l)))
    for name, fn in phases:
        if only is not None and name not in only:
            continue
        fn()
        if stop(name):
            break
    if only is None or "final" in only:
        final_phase()
    P.emit(nc, final_wait_ops=fin_ops[-8:] if len(fin_ops) > 8 else fin_ops)
    return nc


def host_constants():
    cst = np.zeros((128, NCST), np.float32)
    cst[:, 0:128] = np.eye(128, dtype=np.float32)
    perm = np.zeros((32, 32), np.float32)
    for i in range(16):
        perm[i + 16, i] = -1.0
        perm[i, i + 16] = 1.0
    cst[0:32, 128:160] = perm
    for g, w in enumerate(POOL_W):
        for tt in range(16):
            cst[:, 160 + g * 16 + tt] = float(w) / float(min(tt + 1, w))
    j = np.arange(128)[:, None]
    i = np.arange(512)[None, :]
    for c in range(4):
        cst[:, 224 + c * 512:224 + (c + 1) * 512] = ((c * 128 + j) <= i).astype(np.float32)
    pos = np.arange(S, dtype=np.float32)
    inv_freq = np.power(np.float32(500000.0), -np.arange(0, 32, 2, dtype=np.float32) / np.float32(32)).astype(np.float32)
    ang = pos[:, None] * inv_freq[None, :]
    ang = np.concatenate([ang, ang], axis=-1)
    cs = np.zeros((2, 128, S), np.float32)
    cs[0, :, :] = 1.0
    cs[0, 0:32, :] = np.cos(ang).T
    cs[1, 0:32, :] = np.sin(ang).T
    return cst, np.ascontiguousarray(cs)


def pack_vecs(inp):
    v = np.zeros((128, NV), np.float32)

    def col8(a):
        return np.asarray(a, np.float32).reshape(8, 128).T
    for l in range(DEPTH):
        v[:, 0 + l * 8:8 + l * 8] = col8(inp["g_mix"][l])
        v[:, 32 + l * 8:40 + l * 8] = col8(inp["g_ffn"][l])
        v[:, 64 + l * 8:72 + l * 8] = col8(inp["g_mem"][l])
    for j in range(2):
        v[:, 96 + j * 8:104 + j * 8] = col8(inp["pool_scale"][j])
        v[:, 128 + 2 * j:130 + 2 * j] = np.asarray(inp["g_subln"][j], np.float32).reshape(2, 128).T
        v[:, 132 + j] = inp["lam_q1"][j]
        v[:, 134 + j] = inp["lam_k1"][j]
        v[:, 136 + j] = inp["lam_q2"][j]
        v[:, 138 + j] = inp["lam_k2"][j]
    v[:, 112:120] = col8(inp["g_kv"])
    v[:, 120:128] = col8(inp["g_final"])
    return v


def make_in_maps(inp, ncores=8):
    cst, cs = host_constants()
    vecs = pack_vecs(inp)
    f = lambda a: np.ascontiguousarray(np.asarray(a, dtype=np.float32))
    shared = {
        "w_in": f(inp["w_in"]), "w_out": f(inp["w_out"]), "w_gate": f(inp["w_gate"]), "w_up": f(inp["w_up"]),
        "w_down": f(inp["w_down"]), "w_mem_kv": f(inp["w_mem_kv"]), "pool_w": f(inp["pool_w"]), "w_kv": f(inp["w_kv"]),
        "vecs": vecs, "cst": cst, "cs": cs,
    }
    maps = []
    for b in range(ncores):
        m = dict(shared)
        m["x"] = f(inp["x"][b])
        m["mem"] = f(inp["mem"][b])
        maps.append(m)
    return maps


_NC_CACHE = {}


def kernel(**inputs):
    if "nc" not in _NC_CACHE:
        _NC_CACHE["nc"] = build()
    nc = _NC_CACHE["nc"]
    maps = make_in_maps(inputs, 8)
    res = run_bass_kernel_spmd(nc, maps, core_ids=list(range(8)))
    return np.stack([np.asarray(r["out"], dtype=np.float32) for r in res.results], axis=0)
```

```python
import math
import numpy as np
from contextlib import ExitStack
import concourse.bass as bass
import concourse.mybir as mybir
from concourse.bass_utils import run_bass_kernel_spmd

F32 = mybir.dt.float32
BF16 = mybir.dt.bfloat16
AF = mybir.ActivationFunctionType
ALU = mybir.AluOpType

PE, ACT, DVE, POOL, SP = "tensor", "scalar", "vector", "gpsimd", "sync"
ENGS = [PE, ACT, DVE, POOL, SP]

S = 8192
D = 1024
T = 512
NT = S // T
DC = 8
FF = 2816
FC = 22
NMEM = 256
DEPTH = 4
N_A = 2
POOL_W = (2, 4, 8, 16)
NORM_EPS = 1e-6
SUBLN_EPS = 1e-5
NV = 140
NCST = 224 + 2048
SBUF_LO = 16512
SBUF_HI = 229344


class Buf:
    __slots__ = ("name", "last_w", "readers", "slot", "phase", "excl")

    def __init__(self, name, excl=False):
        self.name = name
        self.excl = excl
        self.last_w = None
        self.readers = []
        self.slot = None
        self.phase = -1


class Op:
    __slots__ = ("eng", "fn", "deps", "signals", "sigval", "is_dma", "dbuf", "slot")

    def __init__(self, eng, fn):
        self.eng = eng
        self.fn = fn
        self.deps = []
        self.signals = False
        self.sigval = 0
        self.is_dma = False
        self.dbuf = None


class Prog:
    def __init__(self):
        self.ops = {e: [] for e in ENGS}
        self.slot_cnt = []
        self.slot_kind = []
        self.free_slots = {"sw": [], "hw": []}
        self.phase = 0
        self.sems = None
        self.last = {e: None for e in ENGS}
        self.dma_since = {}
        self.pending = {e: None for e in ENGS}

    def _add(self, op, reads, writes):
        if any(b.excl for b in reads):
            writes = list(writes) + [b for b in reads if b.excl]
            reads = [b for b in reads if not b.excl]
        deps = {}
        for b in reads:
            w = b.last_w
            if w is not None:
                deps[id(w)] = w
        for b in writes:
            w = b.last_w
            if w is not None:
                deps[id(w)] = w
            for r in b.readers:
                if r.eng == op.eng == PE and not r.is_dma and not op.is_dma:
                    continue
                deps[id(r)] = r
        pb = self.pending[op.eng]
        if pb is not None:
            for d in pb:
                deps[id(d)] = d
            self.pending[op.eng] = None
        for d in deps.values():
            if d is op:
                continue
            if (not d.is_dma) and (not op.is_dma) and d.eng == PE and op.eng == PE:
                continue
            d.signals = True
            op.deps.append(d)
        for b in reads:
            b.readers.append(op)
        for b in writes:
            b.last_w = op
            b.readers = []
        self.ops[op.eng].append(op)
        if op.is_dma:
            self.dma_since[id(op.dbuf)] = op
        else:
            self.last[op.eng] = op
        return op

    def op(self, eng, fn, reads=(), writes=()):
        return self._add(Op(eng, fn), reads, writes)

    def dma(self, eng, fn, sbuf, reads=(), writes=()):
        o = Op(eng, fn)
        o.is_dma = True
        o.dbuf = sbuf
        o.signals = True
        kind = "sw" if eng == POOL else "hw"
        if sbuf.slot is None or sbuf.phase != self.phase or self.slot_kind[sbuf.slot] != kind:
            if self.free_slots[kind]:
                sbuf.slot = self.free_slots[kind].pop()
            else:
                sbuf.slot = len(self.slot_cnt)
                self.slot_cnt.append(0)
                self.slot_kind.append(kind)
            sbuf.phase = self.phase
        self.slot_cnt[sbuf.slot] += 1
        o.slot = sbuf.slot
        o.sigval = 16 * self.slot_cnt[sbuf.slot]
        return self._add(o, reads, writes)

    def barrier(self):
        deps = [o for o in self.last.values() if o is not None] + list(self.dma_since.values())
        for o in deps:
            o.signals = True
        for e in ENGS:
            cur = self.pending[e]
            self.pending[e] = list(deps) + (cur or [])
        self.dma_since = {}
        self.phase += 1
        self.free_slots = {"sw": [i for i, k in enumerate(self.slot_kind) if k == "sw"],
                           "hw": [i for i, k in enumerate(self.slot_kind) if k == "hw"]}

    def emit(self, nc, final_wait_ops=()):
        with ExitStack() as es:
            esem = {e: es.enter_context(nc.semaphore("s_" + e)) for e in ENGS}
            dsem = [es.enter_context(nc.semaphore(f"d_{i}")) for i in range(len(self.slot_cnt))]
            for e in ENGS:
                c = 0
                for o in self.ops[e]:
                    if o.is_dma:
                        continue
                    if o.signals:
                        c += 1
                        o.sigval = c
            block = es.enter_context(nc.Block())

            def run(engname, eh, tail=None):
                waited = {}

                def wait_for(d):
                    if d.is_dma:
                        s, v = dsem[d.slot], d.sigval
                    else:
                        s, v = esem[d.eng], d.sigval
                    k = id(s)
                    if waited.get(k, 0) < v:
                        eh.wait_ge(s, v)
                        waited[k] = v

                for o in self.ops[engname]:
                    for d in o.deps:
                        wait_for(d)
                    ins = o.fn(eh)
                    if o.is_dma:
                        ins.then_inc(dsem[o.slot], 16)
                    elif o.signals:
                        ins.then_inc(esem[engname], 1)
                if tail is not None:
                    for d in tail:
                        wait_for(d)

            @block.sync
            def _(eh):
                run(SP, eh, tail=final_wait_ops)

            @block.tensor
            def _(eh):
                run(PE, eh)

            @block.vector
            def _(eh):
                run(DVE, eh)

            @block.scalar
            def _(eh):
                run(ACT, eh)

            @block.gpsimd
            def _(eh):
                run(POOL, eh)


class Ctx:
    pass


def build(stop_after=None, debug=False, only=None):
    nc = bass.Bass("TRN2", target_bir_lowering=False)
    P = Prog()
    cx = Ctx()
    cx.nc, cx.P = nc, P
    okind = "ExternalOutput" if debug else None

    def dram_in(name, shape, dt=F32):
        return nc.dram_tensor(name, list(shape), dt, kind="ExternalInput").ap()

    def dram_scr(name, shape, dt):
        if debug:
            return nc.dram_tensor(name, list(shape), dt, kind="ExternalOutput").ap()
        return nc.dram_tensor(name, list(shape), dt).ap()

    x = dram_in("x", [S, D])
    mem = dram_in("mem", [NMEM, D])
    w_in = dram_in("w_in", [DEPTH, D, 2 * D])
    w_out = dram_in("w_out", [DEPTH, 2 * D, D])
    w_gate = dram_in("w_gate", [DEPTH, D, FF])
    w_up = dram_in("w_up", [DEPTH, D, FF])
    w_down = dram_in("w_down", [DEPTH, FF, D])
    w_mem_kv = dram_in("w_mem_kv", [DEPTH, D, 2 * D])
    pool_w = dram_in("pool_w", [N_A, 4, 256, 256])
    w_kv = dram_in("w_kv", [D, 2 * D])
    vecs_d = dram_in("vecs", [128, NV])
    cst_d = dram_in("cst", [128, NCST])
    cs_d = dram_in("cs", [2, 128, S])
    out = nc.dram_tensor("out", [S, D], F32, kind="ExternalOutput").ap()

    hT = dram_scr("hT", [D, S], F32)
    kT = dram_scr("kT", [D, S], BF16)
    vS = dram_scr("vS", [4, 128, S // 128, 256], BF16)
    qT = dram_scr("qT", [D, S], BF16)
    mixT = dram_scr("mixT", [D, S], BF16)
    memoT = dram_scr("memoT", [D, S], BF16)
    hT_b = [[Buf(f"hT{t}_{c}") for c in range(DC)] for t in range(NT)]
    kT_b = [Buf(f"kT{c}") for c in range(DC)]
    vS_b = [Buf(f"vS{h}") for h in range(4)]
    qT_b = [[Buf(f"qT{t}_{c}") for c in range(DC)] for t in range(NT)]
    mixT_b = [[Buf(f"mixT{t}_{c}") for c in range(DC)] for t in range(NT)]
    memoT_b = [[Buf(f"memoT{t}_{c}") for c in range(DC)] for t in range(NT)]

    cx.off = SBUF_LO
    cx.uid = 0

    def sb(name, shape, dt):
        size = 1
        for s_ in shape[1:]:
            size *= s_
        size *= 4 if dt == F32 else 2
        off = (cx.off + 31) // 32 * 32
        cx.off = off + size
        assert cx.off <= SBUF_HI, f"SBUF overflow at {name}: {cx.off}"
        cx.uid += 1
        return nc.alloc_sbuf_tensor_at(f"{name}_{cx.uid}", list(shape), dt, offset=off)

    def sb_at(name, shape, dt, off):
        cx.uid += 1
        return nc.alloc_sbuf_tensor_at(f"{name}_{cx.uid}", list(shape), dt, offset=off)

    ps = [nc.alloc_psum_tensor(f"psb{i}", [128, 512], F32) for i in range(8)]
    ps_b = [Buf(f"ps{i}", excl=True) for i in range(8)]
    held = [False] * 8
    bank_i = [0]

    def bank(hold=False):
        for _ in range(8):
            k = bank_i[0]
            bank_i[0] = (k + 1) % 8
            if not held[k]:
                if hold:
                    held[k] = True
                return k
        raise RuntimeError("no free psum bank")

    def release(k):
        held[k] = False

    evac_i = [0]

    def evac_eng():
        evac_i[0] += 1
        return ACT if evac_i[0] % 2 else DVE

    def copy_op(eng, dst, src, reads, writes):
        if eng == ACT:
            P.op(ACT, lambda e: e.copy(out=dst, in_=src), reads=reads, writes=writes)
        elif eng == DVE:
            P.op(DVE, lambda e: e.tensor_copy(out=dst, in_=src), reads=reads, writes=writes)
        else:
            P.op(POOL, lambda e: e.tensor_copy(out=dst, in_=src), reads=reads, writes=writes)

    ident = sb("ident", [128, 128], F32)
    ones_bf = sb("ones_bf", [128, 128], BF16)
    ones_f = sb("ones_f", [128, 128], F32)
    perm_bf = sb("perm_bf", [128, 128], BF16)
    rcw = sb("rcw", [128, 64], F32)
    vecs = sb("vecs", [128, NV], F32)
    misc = sb("misc", [128, 16], F32)
    masks = sb("masks", [128, 4, 512], BF16)
    b_const = Buf("const")
    b_vecs = Buf("vecs")
    b_misc = Buf("misc")
    persist_end = cx.off

    lambda_init = [0.8 - 0.6 * math.exp(-0.3 * i) for i in range(DEPTH)]

    cst_tmp = sb("cst_tmp", [128, NCST], F32)
    b_ctmp = Buf("cst_tmp")
    P.dma(SP, lambda e: e.dma_start(out=cst_tmp[:], in_=cst_d), b_ctmp, writes=[b_ctmp])
    P.dma(SP, lambda e: e.dma_start(out=vecs[:], in_=vecs_d), b_vecs, writes=[b_vecs])
    P.op(DVE, lambda e: e.tensor_copy(out=ident[:], in_=cst_tmp[:, 0:128]), reads=[b_ctmp], writes=[b_const])
    P.op(DVE, lambda e: e.memset(perm_bf[:], 0.0), writes=[b_const])
    P.op(DVE, lambda e: e.tensor_copy(out=perm_bf[0:32, 0:32], in_=cst_tmp[0:32, 128:160]), reads=[b_ctmp, b_const], writes=[b_const])
    P.op(DVE, lambda e: e.tensor_copy(out=rcw[:], in_=cst_tmp[:, 160:224]), reads=[b_ctmp], writes=[b_const])
    P.op(DVE, lambda e: e.tensor_copy(out=masks[:].rearrange("p a b -> p (a b)"), in_=cst_tmp[:, 224:224 + 2048]),
         reads=[b_ctmp], writes=[b_const])
    P.op(DVE, lambda e: e.memset(ones_bf[:], 1.0), writes=[b_const])
    P.op(DVE, lambda e: e.memset(ones_f[:], 1.0), writes=[b_const])
    P.op(DVE, lambda e: e.memset(misc[:, 0:1], NORM_EPS), writes=[b_misc])
    for j in range(2):
        li = lambda_init[N_A + j]
        P.op(DVE, (lambda j, li: lambda e: e.memset(misc[:, 1 + j:2 + j], SUBLN_EPS / (1.0 - li) ** 2))(j, li), writes=[b_misc])
    for j in range(2):
        li = lambda_init[N_A + j]
        P.op(DVE, (lambda j: lambda e: e.tensor_tensor(out=misc[:, 5:6], in0=vecs[:, 132 + j:133 + j], in1=vecs[:, 134 + j:135 + j], op=ALU.mult))(j),
             reads=[b_vecs, b_misc], writes=[b_misc])
        P.op(DVE, (lambda j: lambda e: e.tensor_tensor(out=misc[:, 6:7], in0=vecs[:, 136 + j:137 + j], in1=vecs[:, 138 + j:139 + j], op=ALU.mult))(j),
             reads=[b_vecs, b_misc], writes=[b_misc])
        k = bank()
        P.op(PE, (lambda k: lambda e: e.matmul(ps[k][:, 0:2], lhsT=ones_f[:], rhs=misc[:, 5:7], start=True, stop=True))(k),
             reads=[b_const, b_misc], writes=[ps_b[k]])
        P.op(ACT, (lambda k: lambda e: e.activation(out=misc[:, 7:9], in_=ps[k][:, 0:2], func=AF.Exp))(k), reads=[ps_b[k], b_misc], writes=[b_misc])
        P.op(DVE, (lambda j, li: lambda e: e.scalar_tensor_tensor(out=misc[:, 3 + j:4 + j], in0=misc[:, 8:9], scalar=-li, in1=misc[:, 7:8],
                                                                  op0=ALU.add, op1=ALU.subtract))(j, li),
             reads=[b_misc], writes=[b_misc])
    P.barrier()
    cx.off = persist_end

    def load_w(dst, dst_b, src_rows_ap, kc_n, eng=POOL):
        for kc in range(kc_n):
            P.dma(eng, (lambda kc: lambda e: e.dma_start(out=dst[:, kc, :], in_=src_rows_ap[kc * 128:(kc + 1) * 128, :]))(kc),
                  dst_b[kc], writes=[dst_b[kc]])

    def norm_tile(hA, hA_b, gcol, xn, xn_b, ncol, scr, out_eng_f32=False, eps_col=0, scale=1.0 / D, nch=DC):
        sq, sq_b, rs, rs_b, rstd, rstd_b = scr
        k = bank()
        for c in range(nch):
            s = c % len(sq)
            P.op(ACT, (lambda c, s: lambda e: e.activation(out=sq[s][:, 0:ncol], in_=hA[:, c, 0:ncol], func=AF.Square))(c, s),
                 reads=[hA_b[c]], writes=[sq_b[s]])
            P.op(PE, (lambda c, s, k: lambda e: e.matmul(ps[k][:, 0:ncol], lhsT=ones_bf[:], rhs=sq[s][:, 0:ncol], start=(c == 0), stop=(c == nch - 1)))(c, s, k),
                 reads=[sq_b[s], b_const], writes=[ps_b[k]])
        P.op(ACT, (lambda k: lambda e: e.activation(out=rs[:, 0:ncol], in_=ps[k][:, 0:ncol], func=AF.Ln, bias=misc[:, eps_col:eps_col + 1], scale=scale))(k),
             reads=[ps_b[k], b_misc], writes=[rs_b])
        P.op(ACT, lambda e: e.activation(out=rstd[:, 0:ncol], in_=rs[:, 0:ncol], func=AF.Exp, scale=-0.5), reads=[rs_b], writes=[rstd_b])
        for c in range(nch):
            P.op(DVE, (lambda c: lambda e: e.scalar_tensor_tensor(out=xn[:, c, 0:ncol], in0=hA[:, c, 0:ncol], scalar=vecs[:, gcol + c:gcol + c + 1],
                                                                  in1=rstd[:, 0:ncol], op0=ALU.mult, op1=ALU.mult))(c),
                 reads=[hA_b[c], rstd_b, b_vecs], writes=[xn_b[c]])

    def norm_scratch(tag):
        sq = [sb(f"sq{tag}{i}", [128, T], BF16) for i in range(4)]
        sq_b = [Buf(f"sq{tag}{i}") for i in range(4)]
        rs = sb(f"rs{tag}", [128, T], F32)
        rstd = sb(f"rstd{tag}", [128, T], F32)
        return (sq, sq_b, rs, Buf(f"rs{tag}"), rstd, Buf(f"rstd{tag}"))

    def proj(w, w_b, kcn, x_, x_b, ocs, evac, ncol=T, wcol0=0):
        ocs = list(ocs)
        first, rest = ocs[:4], ocs[4:]
        bk = [bank() for _ in first]
        for kc in range(kcn):
            for i_, oc in enumerate(first):
                k = bk[i_]
                P.op(PE, (lambda oc, kc, k: lambda e: e.matmul(ps[k][:, 0:ncol], lhsT=w[:, kc, wcol0 + oc * 128:wcol0 + (oc + 1) * 128],
                                                                  rhs=x_[:, kc, 0:ncol], start=(kc == 0), stop=(kc == kcn - 1)))(oc, kc, k),
                     reads=[w_b[kc], x_b[kc]], writes=[ps_b[k]])
        for i_, oc in enumerate(first):
            evac(oc, bk[i_])
        for oc in rest:
            k = bank()
            for kc in range(kcn):
                P.op(PE, (lambda oc, kc, k: lambda e: e.matmul(ps[k][:, 0:ncol], lhsT=w[:, kc, wcol0 + oc * 128:wcol0 + (oc + 1) * 128],
                                                                  rhs=x_[:, kc, 0:ncol], start=(kc == 0), stop=(kc == kcn - 1)))(oc, kc, k),
                     reads=[w_b[kc], x_b[kc]], writes=[ps_b[k]])
            evac(oc, k)

    def load_h_chunks(hA, hA_b, t, cs=range(DC)):
        for c in cs:
            P.dma(SP, (lambda c: lambda e: e.dma_start(out=hA[:, c, :], in_=hT[c * 128:(c + 1) * 128, t * T:(t + 1) * T]))(c),
                  hA_b[c], reads=[hT_b[t][c]], writes=[hA_b[c]])

    stg_state = {}

    def make_stage(n=3):
        st = [sb(f"stg{i}", [128, T], F32) for i in range(n)]
        stb = [Buf(f"stg{i}") for i in range(n)]
        return st, stb, [0]

    def residual_store(stage, hA, hA_b, c, k, t):
        st, stb, cnt = stage
        s = cnt[0] % len(st)
        cnt[0] += 1
        P.op(DVE, (lambda c, k, s: lambda e: e.tensor_tensor(out=st[s][:], in0=hA[:, c, :], in1=ps[k][:], op=ALU.add))(c, k, s),
             reads=[hA_b[c], ps_b[k]], writes=[stb[s]])
        P.dma(SP, (lambda c, s: lambda e: e.dma_start(out=hT[c * 128:(c + 1) * 128, t * T:(t + 1) * T], in_=st[s][:]))(c, s),
              stb[s], reads=[stb[s]], writes=[hT_b[t][c]])

    def mem_kv_prologue(l, wbuf, wbuf_b, kmT, kmT_b, vm, vm_b, memT, memT_b, mn, mn_b, nscr):
        load_w(wbuf, wbuf_b, w_mem_kv[l], DC)
        norm_tile(memT, memT_b, 64 + l * 8, mn, mn_b, NMEM, nscr)

        def ev_k(oc, k):
            copy_op(evac_eng(), kmT[:, oc, :], ps[k][:, 0:NMEM], [ps_b[k]], [kmT_b])
        proj(wbuf, wbuf_b, DC, mn, mn_b, range(DC), ev_k, ncol=NMEM)
        for mc in range(2):
            for half in range(2):
                k = bank()
                for kc in range(DC):
                    P.op(PE, (lambda mc, half, kc, k: lambda e: e.matmul(ps[k][:], lhsT=mn[:, kc, mc * 128:(mc + 1) * 128],
                                                                         rhs=wbuf[:, kc, D + half * 512:D + (half + 1) * 512],
                                                                         start=(kc == 0), stop=(kc == DC - 1)))(mc, half, kc, k),
                         reads=[mn_b[kc], wbuf_b[kc]], writes=[ps_b[k]])
                copy_op(evac_eng(), vm[:, mc, half * 512:(half + 1) * 512], ps[k][:], [ps_b[k]], [vm_b])

    def mem_attention(qm, qm_b, kmT, kmT_b, vm, vm_b, pT, pT_b, rl, rl_b, dst, dst_b, dst_c0):
        sc = 1.0 / 16.0
        for hh in range(4):
            for mc in range(2):
                k = bank()
                for kc in range(2):
                    P.op(PE, (lambda hh, mc, kc, k: lambda e: e.matmul(ps[k][:], lhsT=kmT[:, 2 * hh + kc, mc * 128:(mc + 1) * 128],
                                                                       rhs=qm[:, 2 * hh + kc, :], start=(kc == 0), stop=(kc == 1)))(hh, mc, kc, k),
                         reads=[kmT_b, qm_b[2 * hh + kc]], writes=[ps_b[k]])
                P.op(ACT, (lambda mc, k: lambda e: e.activation(out=pT[mc][:], in_=ps[k][:], func=AF.Exp, scale=sc))(mc, k),
                     reads=[ps_b[k]], writes=[pT_b[mc]])
            kl = bank()
            for mc in range(2):
                P.op(PE, (lambda mc, kl: lambda e: e.matmul(ps[kl][:], lhsT=ones_bf[:], rhs=pT[mc][:], start=(mc == 0), stop=(mc == 1)))(mc, kl),
                     reads=[pT_b[mc], b_const], writes=[ps_b[kl]])
            P.op(ACT, (lambda kl: lambda e: e.activation(out=rl[:], in_=ps[kl][:], func=AF.Ln))(kl), reads=[ps_b[kl]], writes=[rl_b])
            P.op(ACT, lambda e: e.activation(out=rl[:], in_=rl[:], func=AF.Exp, scale=-1.0), reads=[rl_b], writes=[rl_b])
            for oc in range(2):
                k = bank()
                for mc in range(2):
                    P.op(PE, (lambda hh, oc, mc, k: lambda e: e.matmul(ps[k][:], lhsT=vm[:, mc, hh * 256 + oc * 128:hh * 256 + (oc + 1) * 128],
                                                                       rhs=pT[mc][:], start=(mc == 0), stop=(mc == 1)))(hh, oc, mc, k),
                         reads=[vm_b, pT_b[mc]], writes=[ps_b[k]])
                ci = dst_c0 + 2 * hh + oc
                P.op(DVE, (lambda ci, k: lambda e: e.tensor_tensor(out=dst[:, ci, :], in0=ps[k][:], in1=rl[:], op=ALU.mult))(ci, k),
                     reads=[ps_b[k], rl_b], writes=[dst_b[ci]])

    def rope_chunk(k, dstT, dst_b, c, cs_t, cs_b, rtmp):
        kb, kb_b, t1, t1_b, t2, t2_b = rtmp
        copy_op(ACT, kb[:], ps[k][:], [ps_b[k]], [kb_b])
        P.op(DVE, (lambda k: lambda e: e.tensor_tensor(out=t2[:], in0=ps[k][:], in1=cs_t[:, 0, :], op=ALU.mult))(k),
             reads=[ps_b[k], cs_b], writes=[t2_b])
        k2 = bank()
        P.op(PE, (lambda k2: lambda e: e.matmul(ps[k2][:], lhsT=perm_bf[:], rhs=kb[:], start=True, stop=True))(k2),
             reads=[kb_b, b_const], writes=[ps_b[k2]])
        P.op(DVE, (lambda k2: lambda e: e.tensor_tensor(out=t1[:], in0=ps[k2][:], in1=cs_t[:, 1, :], op=ALU.mult))(k2),
             reads=[ps_b[k2], cs_b], writes=[t1_b])
        P.op(POOL, (lambda c: lambda e: e.tensor_tensor(out=dstT[:, c, :], in0=t1[:], in1=t2[:], op=ALU.add))(c),
             reads=[t1_b, t2_b], writes=[dst_b[c]])

    def rope_scratch():
        kb = sb("rkb", [128, T], BF16)
        t1 = sb("rt1", [128, T], F32)
        t2 = sb("rt2", [128, T], F32)
        return (kb, Buf("rkb"), t1, Buf("rt1"), t2, Buf("rt2"))

    def load_cs(cs_t, cs_b, t):
        P.dma(SP, lambda e: e.dma_start(out=cs_t[:], in_=cs_d[:, :, t * T:(t + 1) * T].rearrange("a p s -> p a s")), cs_b, writes=[cs_b])

    dbg_stop = [False]

    def stop(name):
        if stop_after == name:
            dbg_stop[0] = True
        return dbg_stop[0]

    def mixer_pool_phase(l):
        base = cx.off
        w1 = sb("w_in", [128, DC, 2 * D], BF16)
        w1_b = [Buf(f"w_in{i}") for i in range(DC)]
        w2_off = (cx.off + 31) // 32 * 32
        w2 = sb("w_out", [128, 16, D], BF16)
        w2_b = [Buf(f"w_out{i}") for i in range(16)]
        w2v = sb_at("w_mkvv", [128, DC, 2 * D], BF16, w2_off)
        w2v_b = [Buf(f"w_mkvv{i}") for i in range(DC)]
        pw = sb("pw", [128, 8, 256], BF16)
        pw_b = [Buf(f"pw{i}") for i in range(8)]
        kmT = sb("kmT", [128, DC, NMEM], BF16)
        vm = sb("vm", [128, 2, D], BF16)
        kmT_b, vm_b = Buf("kmT"), Buf("vm")
        nscr = norm_scratch("m")
        hbufs = [sb(f"hA{i}", [128, DC, T], F32) for i in range(2)]
        hbufs_b = [[Buf(f"hA{i}_{c}") for c in range(DC)] for i in range(2)]
        xn = sb("xn", [128, DC, T], BF16)
        xn_b = [Buf(f"xn{c}") for c in range(DC)]
        uext = sb("uext", [128, DC, 16 + T], F32)
        uext_b = [Buf(f"uext{c}") for c in range(DC)]
        lv = [sb(f"lv{i}", [128, 16 + T], F32) for i in range(4)]
        lv_b = [Buf(f"lv{i}") for i in range(4)]
        pq_off = (cx.off + 31) // 32 * 32
        pooled = sb("pooled", [128, DC, T], BF16)
        pooled_b = [Buf(f"pooled{c}") for c in range(DC)]
        qm = sb("qm", [128, DC, T], BF16)
        qm_b = [Buf(f"qm{c}") for c in range(DC)]
        cat = sb("cat", [128, 16, T], BF16)
        cat_b = [Buf(f"cat{c}") for c in range(16)]
        pT = [sb(f"pT{i}", [128, T], BF16) for i in range(2)]
        pT_b = [Buf(f"pT{i}") for i in range(2)]
        rl = sb("rl", [128, T], F32)
        rl_b = Buf("rl")
        stage = make_stage(2)
        xin = None
        if l == 0:
            xin = [sb_at(f"xin{i}", [128, D], F32, pq_off + i * 4096) for i in range(4)]
            xin_b = [Buf(f"xin{i}") for i in range(4)]
            xin_al = [pooled_b[0:4], pooled_b[4:8], qm_b[0:4], qm_b[4:8]]

        memrow = [sb(f"memrow{i}", [128, D], F32) for i in range(2)] if l != 0 else xin[0:2]
        memrow_b = [Buf(f"memrow{i}") for i in range(2)]
        memT, memT_b = hbufs[1], hbufs_b[1]
        for blk in range(2):
            P.dma(SP, (lambda blk: lambda e: e.dma_start(out=memrow[blk][:], in_=mem[blk * 128:(blk + 1) * 128, :]))(blk),
                  memrow_b[blk], writes=[memrow_b[blk]])
        for c in range(DC):
            k = bank()
            for blk in range(2):
                P.op(PE, (lambda c, blk, k: lambda e: e.transpose(ps[k][:, blk * 128:(blk + 1) * 128], memrow[blk][:, c * 128:(c + 1) * 128], ident[:]))(c, blk, k),
                     reads=[memrow_b[blk], b_const], writes=[ps_b[k]])
            copy_op(evac_eng(), memT[:, c, 0:NMEM], ps[k][:, 0:NMEM], [ps_b[k]], [memT_b[c]])
        mem_kv_prologue(l, w2v, w2v_b, kmT, kmT_b, vm, vm_b, memT, memT_b, xn, xn_b, nscr)
        P.barrier()
        load_w(w1, w1_b, w_in[l], DC)
        for g in range(4):
            for kc in range(2):
                i = g * 2 + kc
                P.dma(POOL, (lambda g, kc, i: lambda e: e.dma_start(out=pw[:, i, :], in_=pool_w[l, g, kc * 128:(kc + 1) * 128, :]))(g, kc, i),
                      pw_b[i], writes=[pw_b[i]])
        load_w(w2, w2_b, w_out[l], 16)
        for c in range(DC):
            P.op(POOL, (lambda c: lambda e: e.memset(uext[:, c, 0:16], 0.0))(c), writes=[uext_b[c]])

        def load_dma(t):
            hb, hb_b = hbufs[t % 2], hbufs_b[t % 2]
            if l == 0:
                for blk in range(4):
                    P.dma(SP, (lambda blk: lambda e: e.dma_start(out=xin[blk][:], in_=x[t * T + blk * 128:t * T + (blk + 1) * 128, :]))(blk),
                          xin_b[blk], writes=[xin_b[blk]] + xin_al[blk])
            else:
                load_h_chunks(hb, hb_b, t)

        def load_post(t):
            if l != 0:
                return
            hb, hb_b = hbufs[t % 2], hbufs_b[t % 2]
            for c in range(DC):
                k = bank()
                for blk in range(4):
                    P.op(PE, (lambda c, blk, k: lambda e: e.transpose(ps[k][:, blk * 128:(blk + 1) * 128], xin[blk][:, c * 128:(c + 1) * 128], ident[:]))(c, blk, k),
                         reads=[xin_b[blk], b_const] + xin_al[blk], writes=[ps_b[k]])
                copy_op(evac_eng(), hb[:, c, :], ps[k][:], [ps_b[k]], [hb_b[c]])

        load_dma(0)
        load_post(0)
        for t in range(NT):
            hb, hb_b = hbufs[t % 2], hbufs_b[t % 2]
            if t + 1 < NT and l != 0:
                load_dma(t + 1)
            norm_tile(hb, hb_b, 0 + l * 8, xn, xn_b, T, nscr)

            W = 16 + T

            def pool_chunk(c):
                g = c // 2
                A, A_b, B, B_b = lv[(c % 2) * 2], lv_b[(c % 2) * 2], lv[(c % 2) * 2 + 1], lv_b[(c % 2) * 2 + 1]
                PENG = POOL if g < 2 else DVE
                P.op(PENG, (lambda c, A: lambda e: e.tensor_tensor(out=A[:, 1:W], in0=uext[:, c, 1:W], in1=uext[:, c, 0:W - 1], op=ALU.add))(c, A),
                     reads=[uext_b[c]], writes=[A_b])
                fin, fin_b = A, A_b
                if g >= 1:
                    P.op(PENG, (lambda A, B: lambda e: e.tensor_tensor(out=B[:, 3:W], in0=A[:, 3:W], in1=A[:, 1:W - 2], op=ALU.add))(A, B),
                         reads=[A_b], writes=[B_b])
                    fin, fin_b = B, B_b
                if g >= 2:
                    P.op(PENG, (lambda A, B: lambda e: e.tensor_tensor(out=A[:, 7:W], in0=B[:, 7:W], in1=B[:, 3:W - 4], op=ALU.add))(A, B),
                         reads=[B_b], writes=[A_b])
                    fin, fin_b = A, A_b
                if g >= 3:
                    P.op(PENG, (lambda A, B: lambda e: e.tensor_tensor(out=B[:, 15:W], in0=A[:, 15:W], in1=A[:, 7:W - 8], op=ALU.add))(A, B),
                         reads=[A_b], writes=[B_b])
                    fin, fin_b = B, B_b
                if t == 0:
                    P.op(DVE, (lambda g, fin: lambda e: e.tensor_tensor(out=fin[:, 16:32], in0=fin[:, 16:32], in1=rcw[:, g * 16:(g + 1) * 16], op=ALU.mult))(g, fin),
                         reads=[fin_b, b_const], writes=[fin_b])
                P.op(DVE, (lambda c, g, fin: lambda e: e.scalar_tensor_tensor(out=pooled[:, c, :], in0=fin[:, 16:W], scalar=1.0 / POOL_W[g], in1=uext[:, c, 16:W],
                                                                            op0=ALU.mult, op1=ALU.subtract))(c, g, fin),
                     reads=[fin_b, uext_b[c]], writes=[pooled_b[c]])

            def ev_in(oc, k):
                if oc < DC:
                    copy_op(evac_eng(), uext[:, oc, 16:16 + T], ps[k][:], [ps_b[k]], [uext_b[oc]])
                    pool_chunk(oc)
                else:
                    copy_op(evac_eng(), qm[:, oc - DC, :], ps[k][:], [ps_b[k]], [qm_b[oc - DC]])
            proj(w1, w1_b, DC, xn, xn_b, range(16), ev_in)
            mem_attention(qm, qm_b, kmT, kmT_b, vm, vm_b, pT, pT_b, rl, rl_b, cat, cat_b, DC)
            for g in range(4):
                for oc in range(2):
                    k = bank()
                    for kc in range(2):
                        P.op(PE, (lambda g, oc, kc, k: lambda e: e.matmul(ps[k][:], lhsT=pw[:, g * 2 + kc, oc * 128:(oc + 1) * 128], rhs=pooled[:, 2 * g + kc, :],
                                                                          start=(kc == 0), stop=(kc == 1)))(g, oc, kc, k),
                             reads=[pw_b[g * 2 + kc], pooled_b[2 * g + kc]], writes=[ps_b[k]])
                    ci = 2 * g + oc
                    P.op(DVE, (lambda ci, k: lambda e: e.tensor_scalar(out=cat[:, ci, :], in0=ps[k][:], scalar1=vecs[:, 96 + l * 8 + ci:96 + l * 8 + ci + 1], scalar2=None,
                                                                       op0=ALU.mult))(ci, k),
                         reads=[ps_b[k], b_vecs], writes=[cat_b[ci]])
            for c in range(DC):
                P.op(POOL, (lambda c: lambda e: e.tensor_copy(out=uext[:, c, 0:16], in_=uext[:, c, T:T + 16]))(c),
                     reads=[uext_b[c]], writes=[uext_b[c]])
            if t + 1 < NT and l == 0:
                load_dma(t + 1)

            def ev_out(oc, k):
                residual_store(stage, hb, hb_b, oc, k, t)
            proj(w2, w2_b, 16, cat, cat_b, range(DC), ev_out)
            if t + 1 < NT:
                load_post(t + 1)
        P.barrier()
        cx.off = base

    def ffn_phase(l):
        base = cx.off
        wg = sb("wg", [128, DC, FF], BF16)
        wu = sb("wu", [128, DC, FF], BF16)
        wd = sb("wd", [128, FC, D], BF16)
        wg_b = [Buf(f"wg{i}") for i in range(DC)]
        wu_b = [Buf(f"wu{i}") for i in range(DC)]
        wd_b = [Buf(f"wd{i}") for i in range(FC)]
        nscr = norm_scratch("f")
        hA = sb("hAf", [128, DC, T], F32)
        hA_b = [Buf(f"hAf{c}") for c in range(DC)]
        xn = sb("xnf", [128, DC, T], BF16)
        xn_b = [Buf(f"xnf{c}") for c in range(DC)]
        aT = sb("aT", [128, FC, T], BF16)
        aT_b = [Buf(f"aT{c}") for c in range(FC)]
        sg = [sb(f"sg{i}", [128, T], F32) for i in range(2)]
        sg_b = [Buf(f"sg{i}") for i in range(2)]
        stage = make_stage(2)
        load_w(wg, wg_b, w_gate[l], DC)
        load_w(wu, wu_b, w_up[l], DC)
        load_w(wd, wd_b, w_down[l], FC)
        load_h_chunks(hA, hA_b, 0)
        for t in range(NT):
            norm_tile(hA, hA_b, 32 + l * 8, xn, xn_b, T, nscr)
            for fc in range(FC):
                kg = bank()
                ku = bank()
                for kc in range(DC):
                    P.op(PE, (lambda fc, kc, kg: lambda e: e.matmul(ps[kg][:], lhsT=wg[:, kc, fc * 128:(fc + 1) * 128], rhs=xn[:, kc, :],
                                                                    start=(kc == 0), stop=(kc == DC - 1)))(fc, kc, kg),
                         reads=[wg_b[kc], xn_b[kc]], writes=[ps_b[kg]])
                for kc in range(DC):
                    P.op(PE, (lambda fc, kc, ku: lambda e: e.matmul(ps[ku][:], lhsT=wu[:, kc, fc * 128:(fc + 1) * 128], rhs=xn[:, kc, :],
                                                                    start=(kc == 0), stop=(kc == DC - 1)))(fc, kc, ku),
                         reads=[wu_b[kc], xn_b[kc]], writes=[ps_b[ku]])
                s = fc % 2
                P.op(ACT, (lambda s, kg: lambda e: e.activation(out=sg[s][:], in_=ps[kg][:], func=AF.Silu))(s, kg), reads=[ps_b[kg]], writes=[sg_b[s]])
                P.op(DVE, (lambda fc, s, ku: lambda e: e.tensor_tensor(out=aT[:, fc, :], in0=sg[s][:], in1=ps[ku][:], op=ALU.mult))(fc, s, ku),
                     reads=[sg_b[s], ps_b[ku]], writes=[aT_b[fc]])

            def ev_down(oc, k):
                residual_store(stage, hA, hA_b, oc, k, t)
                if t + 1 < NT:
                    load_h_chunks(hA, hA_b, t + 1, cs=[oc])
            proj(wd, wd_b, FC, aT, aT_b, range(DC), ev_down)
        P.barrier()
        cx.off = base

    def kv_phase():
        base = cx.off
        wk = sb("wkv", [128, DC, 2 * D], BF16)
        wk_b = [Buf(f"wkv{i}") for i in range(DC)]
        nscr = norm_scratch("k")
        hbufs = [sb(f"hAk{i}", [128, DC, T], F32) for i in range(2)]
        hbufs_b = [[Buf(f"hAk{i}_{c}") for c in range(DC)] for i in range(2)]
        xn = sb("xnk", [128, DC, T], BF16)
        xn_b = [Buf(f"xnk{c}") for c in range(DC)]
        ksb = [sb(f"ksb{i}", [128, DC, T], BF16) for i in range(2)]
        ksb_b = [[Buf(f"ksb{i}_{c}") for c in range(DC)] for i in range(2)]
        vsb = [sb(f"vsb{i}", [128, 4, D], BF16) for i in range(2)]
        vsb_b = [[Buf(f"vsb{i}_{c}") for c in range(4)] for i in range(2)]
        cs_t = [sb(f"cs{i}", [128, 2, T], F32) for i in range(2)]
        cs_b = [Buf(f"cs{i}") for i in range(2)]
        rtmp = rope_scratch()
        load_w(wk, wk_b, w_kv, DC)
        load_h_chunks(hbufs[0], hbufs_b[0], 0)
        load_cs(cs_t[0], cs_b[0], 0)
        for t in range(NT):
            hb, hb_b = hbufs[t % 2], hbufs_b[t % 2]
            if t + 1 < NT:
                load_h_chunks(hbufs[(t + 1) % 2], hbufs_b[(t + 1) % 2], t + 1)
                load_cs(cs_t[(t + 1) % 2], cs_b[(t + 1) % 2], t + 1)
            norm_tile(hb, hb_b, 112, xn, xn_b, T, nscr)
            kk, kk_b = ksb[t % 2], ksb_b[t % 2]

            def ev_k(oc, k):
                import os
                if os.environ.get("KV_NOROPE"):
                    copy_op(ACT, kk[:, oc, :], ps[k][:], [ps_b[k]], [kk_b[oc]])
                else:
                    rope_chunk(k, kk, kk_b, oc, cs_t[t % 2], cs_b[t % 2], rtmp)
                P.dma(SP, (lambda oc, t, kk: lambda e: e.dma_start(out=kT[oc * 128:(oc + 1) * 128, t * T:(t + 1) * T], in_=kk[:, oc, :]))(oc, t, kk),
                      kk_b[oc], reads=[kk_b[oc]])
            proj(wk, wk_b, DC, xn, xn_b, range(DC), ev_k)
            vv, vv_b = vsb[t % 2], vsb_b[t % 2]
            import os
            for blk in (range(4) if not os.environ.get("KV_NOV") else []):
                for half in range(2):
                    k = bank()
                    for kc in range(DC):
                        P.op(PE, (lambda blk, half, kc, k: lambda e: e.matmul(ps[k][:], lhsT=xn[:, kc, blk * 128:(blk + 1) * 128],
                                                                              rhs=wk[:, kc, D + half * 512:D + (half + 1) * 512],
                                                                              start=(kc == 0), stop=(kc == DC - 1)))(blk, half, kc, k),
                             reads=[xn_b[kc], wk_b[kc]], writes=[ps_b[k]])
                    copy_op(evac_eng(), vv[:, blk, half * 512:(half + 1) * 512], ps[k][:], [ps_b[k]], [vv_b[blk]])
                P.dma(SP, (lambda blk, t, vv: lambda e: e.dma_start(out=vS[:, :, t * 4 + blk, :].rearrange("h p f -> p h f"),
                                                                    in_=vv[:, blk, :].rearrange("p (h f) -> p h f", h=4)))(blk, t, vv),
                      vv_b[blk], reads=[vv_b[blk]])
        P.barrier()
        cx.off = base

    def diff_a_phase(l):
        j = l - N_A
        base = cx.off
        w1 = sb("w_inq", [128, DC, 2 * D], BF16)
        w1_b = [Buf(f"w_inq{i}") for i in range(DC)]
        w2 = sb("w_mkv", [128, DC, 2 * D], BF16)
        w2_b = [Buf(f"w_mkv{i}") for i in range(DC)]
        kmT = sb("kmTq", [128, DC, NMEM], BF16)
        vm = sb("vmq", [128, 2, D], BF16)
        kmT_b, vm_b = Buf("kmTq"), Buf("vmq")
        nscr = norm_scratch("q")
        hbufs = [sb(f"hAq{i}", [128, DC, T], F32) for i in range(2)]
        hbufs_b = [[Buf(f"hAq{i}_{c}") for c in range(DC)] for i in range(2)]
        xn = sb("xnq", [128, DC, T], BF16)
        xn_b = [Buf(f"xnq{c}") for c in range(DC)]
        qsb = [sb(f"qsb{i}", [128, DC, T], BF16) for i in range(2)]
        qsb_b = [[Buf(f"qsb{i}_{c}") for c in range(DC)] for i in range(2)]
        qm = sb("qmq", [128, DC, T], BF16)
        qm_b = [Buf(f"qmq{c}") for c in range(DC)]
        mo = [sb(f"mo{i}", [128, DC, T], BF16) for i in range(2)]
        mo_b = [[Buf(f"mo{i}_{c}") for c in range(DC)] for i in range(2)]
        pT = [sb(f"pTq{i}", [128, T], BF16) for i in range(2)]
        pT_b = [Buf(f"pTq{i}") for i in range(2)]
        rl = sb("rlq", [128, T], F32)
        rl_b = Buf("rlq")
        cs_t = [sb(f"csq{i}", [128, 2, T], F32) for i in range(2)]
        cs_b = [Buf(f"csq{i}") for i in range(2)]
        rtmp = rope_scratch()
        memrow = [sb(f"memrowq{i}", [128, D], F32) for i in range(2)]
        memrow_b = [Buf(f"memrowq{i}") for i in range(2)]
        memT, memT_b = hbufs[1], hbufs_b[1]
        for blk in range(2):
            P.dma(SP, (lambda blk: lambda e: e.dma_start(out=memrow[blk][:], in_=mem[blk * 128:(blk + 1) * 128, :]))(blk),
                  memrow_b[blk], writes=[memrow_b[blk]])
        for c in range(DC):
            k = bank()
            for blk in range(2):
                P.op(PE, (lambda c, blk, k: lambda e: e.transpose(ps[k][:, blk * 128:(blk + 1) * 128], memrow[blk][:, c * 128:(c + 1) * 128], ident[:]))(c, blk, k),
                     reads=[memrow_b[blk], b_const], writes=[ps_b[k]])
            copy_op(evac_eng(), memT[:, c, 0:NMEM], ps[k][:, 0:NMEM], [ps_b[k]], [memT_b[c]])
        mem_kv_prologue(l, w2, w2_b, kmT, kmT_b, vm, vm_b, memT, memT_b, xn, xn_b, nscr)
        load_w(w1, w1_b, w_in[l], DC)
        load_h_chunks(hbufs[0], hbufs_b[0], 0)
        load_cs(cs_t[0], cs_b[0], 0)
        for t in range(NT):
            hb, hb_b = hbufs[t % 2], hbufs_b[t % 2]
            if t + 1 < NT:
                load_h_chunks(hbufs[(t + 1) % 2], hbufs_b[(t + 1) % 2], t + 1)
                load_cs(cs_t[(t + 1) % 2], cs_b[(t + 1) % 2], t + 1)
            norm_tile(hb, hb_b, 0 + l * 8, xn, xn_b, T, nscr)
            qq, qq_b = qsb[t % 2], qsb_b[t % 2]
            mm, mm_b = mo[t % 2], mo_b[t % 2]

            def ev_in(oc, k):
                if oc < DC:
                    rope_chunk(k, qq, qq_b, oc, cs_t[t % 2], cs_b[t % 2], rtmp)
                    P.dma(SP, (lambda oc, t, qq: lambda e: e.dma_start(out=qT[oc * 128:(oc + 1) * 128, t * T:(t + 1) * T], in_=qq[:, oc, :]))(oc, t, qq),
                          qq_b[oc], reads=[qq_b[oc]], writes=[qT_b[t][oc]])
                else:
                    copy_op(evac_eng(), qm[:, oc - DC, :], ps[k][:], [ps_b[k]], [qm_b[oc - DC]])
            proj(w1, w1_b, DC, xn, xn_b, range(16), ev_in)
            mem_attention(qm, qm_b, kmT, kmT_b, vm, vm_b, pT, pT_b, rl, rl_b, mm, mm_b, 0)
            for c in range(DC):
                P.dma(SP, (lambda c, t, mm: lambda e: e.dma_start(out=memoT[c * 128:(c + 1) * 128, t * T:(t + 1) * T], in_=mm[:, c, :]))(c, t, mm),
                      mm_b[c], reads=[mm_b[c]], writes=[memoT_b[t][c]])
        P.barrier()
        cx.off = base

    def diff_b_phase(l):
        j = l - N_A
        li = lambda_init[l]
        base = cx.off
        kres = [[sb(f"kres{i}_{m}", [128, S], BF16) for m in range(2)] for i in range(2)]
        kres_b = [[Buf(f"kres{i}_{m}") for m in range(2)] for i in range(2)]
        vres = [sb(f"vres{i}", [128, S // 128, 256], BF16) for i in range(2)]
        vres_b = [Buf(f"vres{i}") for i in range(2)]
        qt_sb = [sb(f"qt{i}", [128, 2, T], BF16) for i in range(2)]
        qt_b = [[Buf(f"qt{i}_{m}") for m in range(2)] for i in range(2)]
        NPT = 8
        pT = [sb(f"pTa{i}", [128, T], BF16) for i in range(NPT)]
        pT_b = [Buf(f"pTa{i}") for i in range(NPT)]
        rl = sb("rla", [128, T], F32)
        rl_b = Buf("rla")
        accD = [sb(f"accD{i}", [128, T], F32) for i in range(2)]
        accD_b = [Buf(f"accD{i}") for i in range(2)]
        accP = [sb(f"accP{i}", [128, T], F32) for i in range(2)]
        accP_b = [Buf(f"accP{i}") for i in range(2)]
        sumbf = sb("sumbf", [128, T], BF16)
        sumbf_b = Buf("sumbf")
        tmpo = sb("tmpo", [128, T], F32)
        tmpo_b = Buf("tmpo")
        oacc = sb("oacc", [128, 2, T], F32)
        oacc_b = [Buf(f"oacc{i}") for i in range(2)]
        sq = [sb(f"sqa{i}", [128, T], BF16) for i in range(2)]
        sq_b = [Buf(f"sqa{i}") for i in range(2)]
        rs = sb("rsa", [128, T], F32)
        rs_b = Buf("rsa")
        rstd = sb("rstda", [128, T], F32)
        rstd_b = Buf("rstda")
        mixs = [sb(f"mixs{i}", [128, 2, T], BF16) for i in range(2)]
        mixs_b = [[Buf(f"mixs{i}_{c}") for c in range(2)] for i in range(2)]
        sc = 128.0 ** -0.5
        LOOK = 3

        def load_head(hh):
            i = hh % 2
            for m in range(2):
                c = 2 * hh + m
                P.dma(SP, (lambda m, c: lambda e: e.dma_start(out=kres[i][m][:], in_=kT[c * 128:(c + 1) * 128, :]))(m, c),
                      kres_b[i][m], writes=[kres_b[i][m]])
            P.dma(SP, lambda e: e.dma_start(out=vres[i][:], in_=vS[hh]), vres_b[i], writes=[vres_b[i]])

        def load_q(g):
            hh, qt = divmod(g, NT)
            slot = g % 2
            for m in range(2):
                c = 2 * hh + m
                P.dma(SP, (lambda m, c: lambda e: e.dma_start(out=qt_sb[slot][:, m, :], in_=qT[c * 128:(c + 1) * 128, qt * T:(qt + 1) * T]))(m, c),
                      qt_b[slot][m], reads=[qT_b[qt][c]], writes=[qt_b[slot][m]])

        its = []
        for g in range(4 * NT):
            hh, qt = divmod(g, NT)
            nkc = 4 * (qt + 1)
            for m in range(2):
                for kc in range(nkc):
                    its.append(dict(g=g, hh=hh, qt=qt, m=m, kc=kc, nkc=nkc))
        N = len(its)
        pcnt = [0]

        def emit_score(it):
            g, hh, qt, m, kc = it["g"], it["hh"], it["qt"], it["m"], it["kc"]
            i, slot = hh % 2, g % 2
            if m == 0 and kc == 0:
                if g == 0:
                    load_head(0)
                    load_q(0)
                if g + 1 < 4 * NT:
                    load_q(g + 1)
            k = bank()
            p = pcnt[0] % NPT
            pcnt[0] += 1
            c0 = (kc - 4 * qt) * 128 if kc >= 4 * qt else 0
            it["k"], it["p"], it["c0"] = k, p, c0
            P.op(PE, (lambda m, kc, k, i, slot, c0: lambda e: e.matmul(ps[k][:, c0:T], lhsT=kres[i][m][:, kc * 128:(kc + 1) * 128], rhs=qt_sb[slot][:, m, c0:T],
                                                                       start=True, stop=True))(m, kc, k, i, slot, c0),
                 reads=[kres_b[i][m], qt_b[slot][m]], writes=[ps_b[k]])
            P.op(ACT, (lambda k, p, c0: lambda e: e.activation(out=pT[p][:, c0:T], in_=ps[k][:, c0:T], func=AF.Exp, scale=sc))(k, p, c0),
                 reads=[ps_b[k]], writes=[pT_b[p]])
            if kc >= 4 * qt:
                dd = kc - 4 * qt
                P.op(DVE, (lambda p, dd, c0: lambda e: e.tensor_tensor(out=pT[p][:, c0:c0 + 128], in0=pT[p][:, c0:c0 + 128], in1=masks[:, dd, c0:c0 + 128], op=ALU.mult))(p, dd, c0),
                     reads=[pT_b[p], b_const], writes=[pT_b[p]])

        acc = {}

        def finalize(it):
            g, hh, qt, m = it["g"], it["hh"], it["qt"], it["m"]
            slot = g % 2
            ko, kl = acc["ko"], acc["kl"]
            a_ = acc["set"]
            if acc["nD"] > 0 and acc["nP"] > 0:
                P.op(DVE, (lambda a_: lambda e: e.tensor_tensor(out=sumbf[:], in0=accD[a_][:], in1=accP[a_][:], op=ALU.add))(a_),
                     reads=[accD_b[a_], accP_b[a_]], writes=[sumbf_b])
            elif acc["nD"] > 0:
                P.op(DVE, (lambda a_: lambda e: e.tensor_copy(out=sumbf[:], in_=accD[a_][:]))(a_), reads=[accD_b[a_]], writes=[sumbf_b])
            elif acc["nP"] > 0:
                P.op(DVE, (lambda a_: lambda e: e.tensor_copy(out=sumbf[:], in_=accP[a_][:]))(a_), reads=[accP_b[a_]], writes=[sumbf_b])
            if acc["nD"] + acc["nP"] > 0:
                P.op(PE, (lambda kl: lambda e: e.matmul(ps[kl][:], lhsT=ones_bf[:], rhs=sumbf[:], start=False, stop=True))(kl),
                     reads=[sumbf_b, b_const], writes=[ps_b[kl]])
            P.op(ACT, (lambda kl: lambda e: e.activation(out=rl[:], in_=ps[kl][:], func=AF.Ln))(kl), reads=[ps_b[kl]], writes=[rl_b])
            P.op(ACT, lambda e: e.activation(out=rl[:], in_=rl[:], func=AF.Exp, scale=-1.0), reads=[rl_b], writes=[rl_b])
            for oc in range(2):
                if m == 0:
                    P.op(DVE, (lambda oc, k: lambda e: e.tensor_tensor(out=oacc[:, oc, :], in0=ps[k][:], in1=rl[:], op=ALU.mult))(oc, ko[oc]),
                         reads=[ps_b[ko[oc]], rl_b], writes=[oacc_b[oc]])
                else:
                    P.op(DVE, (lambda oc, k: lambda e: e.tensor_tensor(out=tmpo[:], in0=ps[k][:], in1=rl[:], op=ALU.mult))(oc, ko[oc]),
                         reads=[ps_b[ko[oc]], rl_b], writes=[tmpo_b])
                    P.op(DVE, (lambda oc: lambda e: e.scalar_tensor_tensor(out=oacc[:, oc, :], in0=tmpo[:], scalar=misc[:, 3 + j:4 + j], in1=oacc[:, oc, :],
                                                                           op0=ALU.mult, op1=ALU.add))(oc),
                         reads=[tmpo_b, oacc_b[oc], b_misc], writes=[oacc_b[oc]])
            release(ko[0])
            release(ko[1])
            release(kl)
            if m == 0:
                return
            k = bank()
            for oc in range(2):
                P.op(ACT, (lambda oc: lambda e: e.activation(out=sq[oc][:], in_=oacc[:, oc, :], func=AF.Square))(oc), reads=[oacc_b[oc]], writes=[sq_b[oc]])
                P.op(PE, (lambda oc, k: lambda e: e.matmul(ps[k][:], lhsT=ones_bf[:], rhs=sq[oc][:], start=(oc == 0), stop=(oc == 1)))(oc, k),
                     reads=[sq_b[oc], b_const], writes=[ps_b[k]])
            P.op(ACT, (lambda k: lambda e: e.activation(out=rs[:], in_=ps[k][:], func=AF.Ln, bias=misc[:, 1 + j:2 + j], scale=1.0 / (256.0 * (1.0 - li) ** 2)))(k),
                 reads=[ps_b[k], b_misc], writes=[rs_b])
            P.op(ACT, lambda e: e.activation(out=rstd[:], in_=rs[:], func=AF.Exp, scale=-0.5), reads=[rs_b], writes=[rstd_b])
            ms, ms_b = mixs[slot], mixs_b[slot]
            for oc in range(2):
                P.op(DVE, (lambda oc, ms: lambda e: e.scalar_tensor_tensor(out=ms[:, oc, :], in0=oacc[:, oc, :], scalar=vecs[:, 128 + 2 * j + oc:128 + 2 * j + oc + 1],
                                                                          in1=rstd[:], op0=ALU.mult, op1=ALU.mult))(oc, ms),
                     reads=[oacc_b[oc], rstd_b, b_vecs], writes=[ms_b[oc]])
                c = 2 * hh + oc
                P.dma(SP, (lambda oc, c, ms, qt: lambda e: e.dma_start(out=mixT[c * 128:(c + 1) * 128, qt * T:(qt + 1) * T], in_=ms[:, oc, :]))(oc, c, ms, qt),
                      ms_b[oc], reads=[ms_b[oc]], writes=[mixT_b[qt][c]])

        for n in range(min(LOOK, N)):
            emit_score(its[n])
        for n in range(N):
            it = its[n]
            if n + LOOK < N:
                emit_score(its[n + LOOK])
            kc, nkc, p, i = it["kc"], it["nkc"], it["p"], it["hh"] % 2
            if kc == 0 and it["m"] == 0 and it["qt"] == 0 and it["hh"] + 1 < 4:
                load_head(it["hh"] + 1)
            if kc == 0:
                acc["ko"] = [bank(hold=True), bank(hold=True)]
                acc["kl"] = bank(hold=True)
                acc["set"] = acc.get("set", 1) ^ 1
                acc["nD"] = 0
                acc["nP"] = 0
                kinds = []
                for kk_ in range(nkc):
                    if kk_ >= 4 * it["qt"]:
                        kinds.append("PE")
                    else:
                        kinds.append(("PE", "DVE", "POOL", "DVE", "POOL", "DVE")[acc.get("rr", 0) % 6])
                        acc["rr"] = acc.get("rr", 0) + 1
                acc["kinds"] = kinds
                acc["pe_first"] = kinds.index("PE")
                acc["pe_last"] = max(q_ for q_ in range(nkc) if kinds[q_] == "PE")
                acc["has_acc"] = any(q_ != "PE" for q_ in kinds)
            ko, kl = acc["ko"], acc["kl"]
            a_ = acc["set"]
            kind = acc["kinds"][kc]
            if kind == "POOL":
                if acc["nP"] == 0:
                    P.op(POOL, (lambda p, a_: lambda e: e.tensor_copy(out=accP[a_][:], in_=pT[p][:]))(p, a_), reads=[pT_b[p]], writes=[accP_b[a_]])
                else:
                    P.op(POOL, (lambda p, a_: lambda e: e.tensor_tensor(out=accP[a_][:], in0=accP[a_][:], in1=pT[p][:], op=ALU.add))(p, a_),
                         reads=[pT_b[p], accP_b[a_]], writes=[accP_b[a_]])
                acc["nP"] += 1
            elif kind == "DVE":
                if acc["nD"] == 0:
                    P.op(DVE, (lambda p, a_: lambda e: e.tensor_copy(out=accD[a_][:], in_=pT[p][:]))(p, a_), reads=[pT_b[p]], writes=[accD_b[a_]])
                else:
                    P.op(DVE, (lambda p, a_: lambda e: e.tensor_tensor(out=accD[a_][:], in0=accD[a_][:], in1=pT[p][:], op=ALU.add))(p, a_),
                         reads=[pT_b[p], accD_b[a_]], writes=[accD_b[a_]])
                acc["nD"] += 1
            else:
                P.op(PE, (lambda p, kl, st_, sp_, c0: lambda e: e.matmul(ps[kl][:, c0:T], lhsT=ones_bf[:], rhs=pT[p][:, c0:T], start=st_, stop=sp_))(
                    p, kl, kc == acc["pe_first"], (kc == acc["pe_last"]) and not acc["has_acc"], it["c0"]),
                     reads=[pT_b[p], b_const], writes=[ps_b[kl]])
            for oc in range(2):
                P.op(PE, (lambda kc, oc, p, i, kb_, st_, sp_, c0: lambda e: e.matmul(ps[kb_][:, c0:T], lhsT=vres[i][:, kc, oc * 128:(oc + 1) * 128], rhs=pT[p][:, c0:T],
                                                                                     start=st_, stop=sp_))(kc, oc, p, i, ko[oc], kc == 0, kc == nkc - 1, it["c0"]),
                     reads=[vres_b[i], pT_b[p]], writes=[ps_b[ko[oc]]])
            if kc == nkc - 1:
                finalize(it)
        P.barrier()
        cx.off = base

    def diff_c_phase(l):
        base = cx.off
        w2 = sb("w_outc", [128, 16, D], BF16)
        w2_b = [Buf(f"w_outc{i}") for i in range(16)]
        hbufs = [sb(f"hAc{i}", [128, DC, T], F32) for i in range(2)]
        hbufs_b = [[Buf(f"hAc{i}_{c}") for c in range(DC)] for i in range(2)]
        cats = [sb(f"catc{i}", [128, 16, T], BF16) for i in range(2)]
        cats_b = [[Buf(f"catc{i}_{c}") for c in range(16)] for i in range(2)]
        stage = make_stage(3)
        load_w(w2, w2_b, w_out[l], 16)

        def load(t):
            load_h_chunks(hbufs[t % 2], hbufs_b[t % 2], t)
            for c in range(DC):
                P.dma(SP, (lambda c: lambda e: e.dma_start(out=cats[t % 2][:, c, :], in_=mixT[c * 128:(c + 1) * 128, t * T:(t + 1) * T]))(c),
                      cats_b[t % 2][c], reads=[mixT_b[t][c]], writes=[cats_b[t % 2][c]])
                P.dma(SP, (lambda c: lambda e: e.dma_start(out=cats[t % 2][:, DC + c, :], in_=memoT[c * 128:(c + 1) * 128, t * T:(t + 1) * T]))(c),
                      cats_b[t % 2][DC + c], reads=[memoT_b[t][c]], writes=[cats_b[t % 2][DC + c]])
        load(0)
        for t in range(NT):
            if t + 1 < NT:
                load(t + 1)
            hb, hb_b = hbufs[t % 2], hbufs_b[t % 2]

            def ev_out(oc, k):
                residual_store(stage, hb, hb_b, oc, k, t)
            proj(w2, w2_b, 16, cats[t % 2], cats_b[t % 2], range(DC), ev_out)
        P.barrier()
        cx.off = base

    fin_ops = []

    def final_phase():
        base = cx.off
        nscr = norm_scratch("z")
        hbufs = [sb(f"hAz{i}", [128, DC, T], F32) for i in range(2)]
        hbufs_b = [[Buf(f"hAz{i}_{c}") for c in range(DC)] for i in range(2)]
        yn = sb("yn", [128, DC, T], F32)
        yn_b = [Buf(f"yn{c}") for c in range(DC)]
        rows = [sb(f"rows{i}", [128, D], F32) for i in range(4)]
        rows_b = [Buf(f"rows{i}") for i in range(4)]
        rc = [0]
        load_h_chunks(hbufs[0], hbufs_b[0], 0)
        for t in range(NT):
            hb, hb_b = hbufs[t % 2], hbufs_b[t % 2]
            if t + 1 < NT:
                load_h_chunks(hbufs[(t + 1) % 2], hbufs_b[(t + 1) % 2], t + 1)
            norm_tile(hb, hb_b, 120, yn, yn_b, T, nscr)
            for blk in range(4):
                r = rc[0] % 4
                rc[0] += 1
                for half in range(2):
                    k = bank()
                    for cc in range(4):
                        c = half * 4 + cc
                        P.op(PE, (lambda c, cc, blk, k: lambda e: e.transpose(ps[k][:, cc * 128:(cc + 1) * 128], yn[:, c, blk * 128:(blk + 1) * 128], ident[:]))(c, cc, blk, k),
                             reads=[yn_b[c], b_const], writes=[ps_b[k]])
                    copy_op(evac_eng(), rows[r][:, half * 512:(half + 1) * 512], ps[k][:], [ps_b[k]], [rows_b[r]])
                fin_ops.append(P.dma(SP, (lambda blk, r, t: lambda e: e.dma_start(out=out[t * T + blk * 128:t * T + (blk + 1) * 128, :], in_=rows[r][:]))(blk, r, t),
                                     rows_b[r], reads=[rows_b[r]]))
        cx.off = base

    phases = []
    for l in range(N_A):
        phases.append((f"mix{l}", lambda l=l: mixer_pool_phase(l)))
        phases.append((f"ffn{l}", lambda l=l: ffn_phase(l)))
    phases.append(("kv", kv_phase))
    for l in range(N_A, DEPTH):
        phases.append((f"da{l}", lambda l=l: diff_a_phase(l)))
        phases.append((f"db{l}", lambda l=l: diff_b_phase(l)))
        phases.append((f"dc{l}", lambda l=l: diff_c_phase(l)))
        phases.append((f"ffn{l}", lambda l=l: ffn_phase(l)))
    for name, fn in phases:
        if only is not None and name not in only:
            continue
        fn()
        if stop(name):
            break
    if only is None or "final" in only:
        final_phase()
    P.emit(nc, final_wait_ops=fin_ops[-8:] if len(fin_ops) > 8 else fin_ops)
    return nc


def host_constants():
    cst = np.zeros((128, NCST), np.float32)
    cst[:, 0:128] = np.eye(128, dtype=np.float32)
    perm = np.zeros((32, 32), np.float32)
    for i in range(16):
        perm[i + 16, i] = -1.0
        perm[i, i + 16] = 1.0
    cst[0:32, 128:160] = perm
    for g, w in enumerate(POOL_W):
        for tt in range(16):
            cst[:, 160 + g * 16 + tt] = float(w) / float(min(tt + 1, w))
    j = np.arange(128)[:, None]
    i = np.arange(512)[None, :]
    for c in range(4):
        cst[:, 224 + c * 512:224 + (c + 1) * 512] = ((c * 128 + j) <= i).astype(np.float32)
    pos = np.arange(S, dtype=np.float32)
    inv_freq = np.power(np.float32(500000.0), -np.arange(0, 32, 2, dtype=np.float32) / np.float32(32)).astype(np.float32)
    ang = pos[:, None] * inv_freq[None, :]
    ang = np.concatenate([ang, ang], axis=-1)
    cs = np.zeros((2, 128, S), np.float32)
    cs[0, :, :] = 1.0
    cs[0, 0:32, :] = np.cos(ang).T
    cs[1, 0:32, :] = np.sin(ang).T
    return cst, np.ascontiguousarray(cs)


def pack_vecs(inp):
    v = np.zeros((128, NV), np.float32)

    def col8(a):
        return np.asarray(a, np.float32).reshape(8, 128).T
    for l in range(DEPTH):
        v[:, 0 + l * 8:8 + l * 8] = col8(inp["g_mix"][l])
        v[:, 32 + l * 8:40 + l * 8] = col8(inp["g_ffn"][l])
        v[:, 64 + l * 8:72 + l * 8] = col8(inp["g_mem"][l])
    for j in range(2):
        v[:, 96 + j * 8:104 + j * 8] = col8(inp["pool_scale"][j])
        v[:, 128 + 2 * j:130 + 2 * j] = np.asarray(inp["g_subln"][j], np.float32).reshape(2, 128).T
        v[:, 132 + j] = inp["lam_q1"][j]
        v[:, 134 + j] = inp["lam_k1"][j]
        v[:, 136 + j] = inp["lam_q2"][j]
        v[:, 138 + j] = inp["lam_k2"][j]
    v[:, 112:120] = col8(inp["g_kv"])
    v[:, 120:128] = col8(inp["g_final"])
    return v


def make_in_maps(inp, ncores=8):
    cst, cs = host_constants()
    vecs = pack_vecs(inp)
    f = lambda a: np.ascontiguousarray(np.asarray(a, dtype=np.float32))
    shared = {
        "w_in": f(inp["w_in"]), "w_out": f(inp["w_out"]), "w_gate": f(inp["w_gate"]), "w_up": f(inp["w_up"]),
        "w_down": f(inp["w_down"]), "w_mem_kv": f(inp["w_mem_kv"]), "pool_w": f(inp["pool_w"]), "w_kv": f(inp["w_kv"]),
        "vecs": vecs, "cst": cst, "cs": cs,
    }
    maps = []
    for b in range(ncores):
        m = dict(shared)
        m["x"] = f(inp["x"][b])
        m["mem"] = f(inp["mem"][b])
        maps.append(m)
    return maps


_NC_CACHE = {}


def kernel(**inputs):
    if "nc" not in _NC_CACHE:
        _NC_CACHE["nc"] = build()
    nc = _NC_CACHE["nc"]
    maps = make_in_maps(inputs, 8)
    res = run_bass_kernel_spmd(nc, maps, core_ids=list(range(8)))
    return np.stack([np.asarray(r["out"], dtype=np.float32) for r in res.results], axis=0)
```

```python
import math
import numpy as np
from contextlib import ExitStack
import concourse.bass as bass
import concourse.mybir as mybir
from concourse.bass_utils import run_bass_kernel_spmd

F32 = mybir.dt.float32
BF16 = mybir.dt.bfloat16
AF = mybir.ActivationFunctionType
ALU = mybir.AluOpType

PE, ACT, DVE, POOL, SP = "tensor", "scalar", "vector", "gpsimd", "sync"
ENGS = [PE, ACT, DVE, POOL, SP]

S = 8192
D = 1024
T = 512
NT = S // T
DC = 8
FF = 2816
FC = 22
NMEM = 256
DEPTH = 4
N_A = 2
POOL_W = (2, 4, 8, 16)
NORM_EPS = 1e-6
SUBLN_EPS = 1e-5
NV = 140
NCST = 224 + 2048
SBUF_LO = 16512
SBUF_HI = 229344


class Buf:
    __slots__ = ("name", "last_w", "readers", "slot", "phase", "excl")

    def __init__(self, name, excl=False):
        self.name = name
        self.excl = excl
        self.last_w = None
        self.readers = []
        self.slot = None
        self.phase = -1


class Op:
    __slots__ = ("eng", "fn", "deps", "signals", "sigval", "is_dma", "dbuf", "slot")

    def __init__(self, eng, fn):
        self.eng = eng
        self.fn = fn
        self.deps = []
        self.signals = False
        self.sigval = 0
        self.is_dma = False
        self.dbuf = None


class Prog:
    def __init__(self):
        self.ops = {e: [] for e in ENGS}
        self.slot_cnt = []
        self.slot_kind = []
        self.free_slots = {"sw": [], "hw": []}
        self.phase = 0
        self.sems = None
        self.last = {e: None for e in ENGS}
        self.dma_since = {}
        self.pending = {e: None for e in ENGS}

    def _add(self, op, reads, writes):
        if any(b.excl for b in reads):
            writes = list(writes) + [b for b in reads if b.excl]
            reads = [b for b in reads if not b.excl]
        deps = {}
        for b in reads:
            w = b.last_w
            if w is not None:
                deps[id(w)] = w
        for b in writes:
            w = b.last_w
            if w is not None:
                deps[id(w)] = w
            for r in b.readers:
                if r.eng == op.eng == PE and not r.is_dma and not op.is_dma:
                    continue
                deps[id(r)] = r
        pb = self.pending[op.eng]
        if pb is not None:
            for d in pb:
                deps[id(d)] = d
            self.pending[op.eng] = None
        for d in deps.values():
            if d is op:
                continue
            if (not d.is_dma) and (not op.is_dma) and d.eng == PE and op.eng == PE:
                continue
            d.signals = True
            op.deps.append(d)
        for b in reads:
            b.readers.append(op)
        for b in writes:
            b.last_w = op
            b.readers = []
        self.ops[op.eng].append(op)
        if op.is_dma:
            self.dma_since[id(op.dbuf)] = op
        else:
            self.last[op.eng] = op
        return op

    def op(self, eng, fn, reads=(), writes=()):
        return self._add(Op(eng, fn), reads, writes)

    def dma(self, eng, fn, sbuf, reads=(), writes=()):
        o = Op(eng, fn)
        o.is_dma = True
        o.dbuf = sbuf
        o.signals = True
        kind = "sw" if eng == POOL else "hw"
        if sbuf.slot is None or sbuf.phase != self.phase or self.slot_kind[sbuf.slot] != kind:
            if self.free_slots[kind]:
                sbuf.slot = self.free_slots[kind].pop()
            else:
                sbuf.slot = len(self.slot_cnt)
                self.slot_cnt.append(0)
                self.slot_kind.append(kind)
            sbuf.phase = self.phase
        self.slot_cnt[sbuf.slot] += 1
        o.slot = sbuf.slot
        o.sigval = 16 * self.slot_cnt[sbuf.slot]
        return self._add(o, reads, writes)

    def barrier(self):
        deps = [o for o in self.last.values() if o is not None] + list(self.dma_since.values())
        for o in deps:
            o.signals = True
        for e in ENGS:
            cur = self.pending[e]
            self.pending[e] = list(deps) + (cur or [])
        self.dma_since = {}
        self.phase += 1
        self.free_slots = {"sw": [i for i, k in enumerate(self.slot_kind) if k == "sw"],
                           "hw": [i for i, k in enumerate(self.slot_kind) if k == "hw"]}

    def emit(self, nc, final_wait_ops=()):
        with ExitStack() as es:
            esem = {e: es.enter_context(nc.semaphore("s_" + e)) for e in ENGS}
            dsem = [es.enter_context(nc.semaphore(f"d_{i}")) for i in range(len(self.slot_cnt))]
            for e in ENGS:
                c = 0
                for o in self.ops[e]:
                    if o.is_dma:
                        continue
                    if o.signals:
                        c += 1
                        o.sigval = c
            block = es.enter_context(nc.Block())

            def run(engname, eh, tail=None):
                waited = {}

                def wait_for(d):
                    if d.is_dma:
                        s, v = dsem[d.slot], d.sigval
                    else:
                        s, v = esem[d.eng], d.sigval
                    k = id(s)
                    if waited.get(k, 0) < v:
                        eh.wait_ge(s, v)
                        waited[k] = v

                for o in self.ops[engname]:
                    for d in o.deps:
                        wait_for(d)
                    ins = o.fn(eh)
                    if o.is_dma:
                        ins.then_inc(dsem[o.slot], 16)
                    elif o.signals:
                        ins.then_inc(esem[engname], 1)
                if tail is not None:
                    for d in tail:
                        wait_for(d)

            @block.sync
            def _(eh):
                run(SP, eh, tail=final_wait_ops)

            @block.tensor
            def _(eh):
                run(PE, eh)

            @block.vector
            def _(eh):
                run(DVE, eh)

            @block.scalar
            def _(eh):
                run(ACT, eh)

            @block.gpsimd
            def _(eh):
                run(POOL, eh)


class Ctx:
    pass


def build(stop_after=None, debug=False, only=None):
    nc = bass.Bass("TRN2", target_bir_lowering=False)
    P = Prog()
    cx = Ctx()
    cx.nc, cx.P = nc, P
    okind = "ExternalOutput" if debug else None

    def dram_in(name, shape, dt=F32):
        return nc.dram_tensor(name, list(shape), dt, kind="ExternalInput").ap()

    def dram_scr(name, shape, dt):
        if debug:
            return nc.dram_tensor(name, list(shape), dt, kind="ExternalOutput").ap()
        return nc.dram_tensor(name, list(shape), dt).ap()

    x = dram_in("x", [S, D])
    mem = dram_in("mem", [NMEM, D])
    w_in = dram_in("w_in", [DEPTH, D, 2 * D])
    w_out = dram_in("w_out", [DEPTH, 2 * D, D])
    w_gate = dram_in("w_gate", [DEPTH, D, FF])
    w_up = dram_in("w_up", [DEPTH, D, FF])
    w_down = dram_in("w_down", [DEPTH, FF, D])
    w_mem_kv = dram_in("w_mem_kv", [DEPTH, D, 2 * D])
    pool_w = dram_in("pool_w", [N_A, 4, 256, 256])
    w_kv = dram_in("w_kv", [D, 2 * D])
    vecs_d = dram_in("vecs", [128, NV])
    cst_d = dram_in("cst", [128, NCST])
    cs_d = dram_in("cs", [2, 128, S])
    out = nc.dram_tensor("out", [S, D], F32, kind="ExternalOutput").ap()

    hT = dram_scr("hT", [D, S], F32)
    kT = dram_scr("kT", [D, S], BF16)
    vS = dram_scr("vS", [4, 128, S // 128, 256], BF16)
    qT = dram_scr("qT", [D, S], BF16)
    mixT = dram_scr("mixT", [D, S], BF16)
    memoT = dram_scr("memoT", [D, S], BF16)
    hT_b = [[Buf(f"hT{t}_{c}") for c in range(DC)] for t in range(NT)]
    kT_b = [Buf(f"kT{c}") for c in range(DC)]
    vS_b = [Buf(f"vS{h}") for h in range(4)]
    qT_b = [[Buf(f"qT{t}_{c}") for c in range(DC)] for t in range(NT)]
    mixT_b = [[Buf(f"mixT{t}_{c}") for c in range(DC)] for t in range(NT)]
    memoT_b = [[Buf(f"memoT{t}_{c}") for c in range(DC)] for t in range(NT)]

    cx.off = SBUF_LO
    cx.uid = 0

    def sb(name, shape, dt):
        size = 1
        for s_ in shape[1:]:
            size *= s_
        size *= 4 if dt == F32 else 2
        off = (cx.off + 31) // 32 * 32
        cx.off = off + size
        assert cx.off <= SBUF_HI, f"SBUF overflow at {name}: {cx.off}"
        cx.uid += 1
        return nc.alloc_sbuf_tensor_at(f"{name}_{cx.uid}", list(shape), dt, offset=off)

    def sb_at(name, shape, dt, off):
        cx.uid += 1
        return nc.alloc_sbuf_tensor_at(f"{name}_{cx.uid}", list(shape), dt, offset=off)

    ps = [nc.alloc_psum_tensor(f"psb{i}", [128, 512], F32) for i in range(8)]
    ps_b = [Buf(f"ps{i}", excl=True) for i in range(8)]
    held = [False] * 8
    bank_i = [0]

    def bank(hold=False):
        for _ in range(8):
            k = bank_i[0]
            bank_i[0] = (k + 1) % 8
            if not held[k]:
                if hold:
                    held[k] = True
                return k
        raise RuntimeError("no free psum bank")

    def release(k):
        held[k] = False

    evac_i = [0]

    def evac_eng():
        evac_i[0] += 1
        return ACT if evac_i[0] % 2 else DVE

    def copy_op(eng, dst, src, reads, writes):
        if eng == ACT:
            P.op(ACT, lambda e: e.copy(out=dst, in_=src), reads=reads, writes=writes)
        elif eng == DVE:
            P.op(DVE, lambda e: e.tensor_copy(out=dst, in_=src), reads=reads, writes=writes)
        else:
            P.op(POOL, lambda e: e.tensor_copy(out=dst, in_=src), reads=reads, writes=writes)

    ident = sb("ident", [128, 128], F32)
    ones_bf = sb("ones_bf", [128, 128], BF16)
    ones_f = sb("ones_f", [128, 128], F32)
    perm_bf = sb("perm_bf", [128, 128], BF16)
    rcw = sb("rcw", [128, 64], F32)
    vecs = sb("vecs", [128, NV], F32)
    misc = sb("misc", [128, 16], F32)
    masks = sb("masks", [128, 4, 512], BF16)
    b_const = Buf("const")
    b_vecs = Buf("vecs")
    b_misc = Buf("misc")
    persist_end = cx.off

    lambda_init = [0.8 - 0.6 * math.exp(-0.3 * i) for i in range(DEPTH)]

    cst_tmp = sb("cst_tmp", [128, NCST], F32)
    b_ctmp = Buf("cst_tmp")
    P.dma(SP, lambda e: e.dma_start(out=cst_tmp[:], in_=cst_d), b_ctmp, writes=[b_ctmp])
    P.dma(SP, lambda e: e.dma_start(out=vecs[:], in_=vecs_d), b_vecs, writes=[b_vecs])
    P.op(DVE, lambda e: e.tensor_copy(out=ident[:], in_=cst_tmp[:, 0:128]), reads=[b_ctmp], writes=[b_const])
    P.op(DVE, lambda e: e.memset(perm_bf[:], 0.0), writes=[b_const])
    P.op(DVE, lambda e: e.tensor_copy(out=perm_bf[0:32, 0:32], in_=cst_tmp[0:32, 128:160]), reads=[b_ctmp, b_const], writes=[b_const])
    P.op(DVE, lambda e: e.tensor_copy(out=rcw[:], in_=cst_tmp[:, 160:224]), reads=[b_ctmp], writes=[b_const])
    P.op(DVE, lambda e: e.tensor_copy(out=masks[:].rearrange("p a b -> p (a b)"), in_=cst_tmp[:, 224:224 + 2048]),
         reads=[b_ctmp], writes=[b_const])
    P.op(DVE, lambda e: e.memset(ones_bf[:], 1.0), writes=[b_const])
    P.op(DVE, lambda e: e.memset(ones_f[:], 1.0), writes=[b_const])
    P.op(DVE, lambda e: e.memset(misc[:, 0:1], NORM_EPS), writes=[b_misc])
    for j in range(2):
        li = lambda_init[N_A + j]
        P.op(DVE, (lambda j, li: lambda e: e.memset(misc[:, 1 + j:2 + j], SUBLN_EPS / (1.0 - li) ** 2))(j, li), writes=[b_misc])
    for j in range(2):
        li = lambda_init[N_A + j]
        P.op(DVE, (lambda j: lambda e: e.tensor_tensor(out=misc[:, 5:6], in0=vecs[:, 132 + j:133 + j], in1=vecs[:, 134 + j:135 + j], op=ALU.mult))(j),
             reads=[b_vecs, b_misc], writes=[b_misc])
        P.op(DVE, (lambda j: lambda e: e.tensor_tensor(out=misc[:, 6:7], in0=vecs[:, 136 + j:137 + j], in1=vecs[:, 138 + j:139 + j], op=ALU.mult))(j),
             reads=[b_vecs, b_misc], writes=[b_misc])
        k = bank()
        P.op(PE, (lambda k: lambda e: e.matmul(ps[k][:, 0:2], lhsT=ones_f[:], rhs=misc[:, 5:7], start=True, stop=True))(k),
             reads=[b_const, b_misc], writes=[ps_b[k]])
        P.op(ACT, (lambda k: lambda e: e.activation(out=misc[:, 7:9], in_=ps[k][:, 0:2], func=AF.Exp))(k), reads=[ps_b[k], b_misc], writes=[b_misc])
        P.op(DVE, (lambda j, li: lambda e: e.scalar_tensor_tensor(out=misc[:, 3 + j:4 + j], in0=misc[:, 8:9], scalar=-li, in1=misc[:, 7:8],
                                                                  op0=ALU.add, op1=ALU.subtract))(j, li),
             reads=[b_misc], writes=[b_misc])
    P.barrier()
    cx.off = persist_end

    def load_w(dst, dst_b, src_rows_ap, kc_n, eng=POOL):
        for kc in range(kc_n):
            P.dma(eng, (lambda kc: lambda e: e.dma_start(out=dst[:, kc, :], in_=src_rows_ap[kc * 128:(kc + 1) * 128, :]))(kc),
                  dst_b[kc], writes=[dst_b[kc]])

    def norm_tile(hA, hA_b, gcol, xn, xn_b, ncol, scr, out_eng_f32=False, eps_col=0, scale=1.0 / D, nch=DC):
        sq, sq_b, rs, rs_b, rstd, rstd_b = scr
        k = bank()
        for c in range(nch):
            s = c % len(sq)
            P.op(ACT, (lambda c, s: lambda e: e.activation(out=sq[s][:, 0:ncol], in_=hA[:, c, 0:ncol], func=AF.Square))(c, s),
                 reads=[hA_b[c]], writes=[sq_b[s]])
            P.op(PE, (lambda c, s, k: lambda e: e.matmul(ps[k][:, 0:ncol], lhsT=ones_bf[:], rhs=sq[s][:, 0:ncol], start=(c == 0), stop=(c == nch - 1)))(c, s, k),
                 reads=[sq_b[s], b_const], writes=[ps_b[k]])
        P.op(ACT, (lambda k: lambda e: e.activation(out=rs[:, 0:ncol], in_=ps[k][:, 0:ncol], func=AF.Ln, bias=misc[:, eps_col:eps_col + 1], scale=scale))(k),
             reads=[ps_b[k], b_misc], writes=[rs_b])
        P.op(ACT, lambda e: e.activation(out=rstd[:, 0:ncol], in_=rs[:, 0:ncol], func=AF.Exp, scale=-0.5), reads=[rs_b], writes=[rstd_b])
        for c in range(nch):
            P.op(DVE, (lambda c: lambda e: e.scalar_tensor_tensor(out=xn[:, c, 0:ncol], in0=hA[:, c, 0:ncol], scalar=vecs[:, gcol + c:gcol + c + 1],
                                                                  in1=rstd[:, 0:ncol], op0=ALU.mult, op1=ALU.mult))(c),
                 reads=[hA_b[c], rstd_b, b_vecs], writes=[xn_b[c]])

    def norm_scratch(tag):
        sq = [sb(f"sq{tag}{i}", [128, T], BF16) for i in range(4)]
        sq_b = [Buf(f"sq{tag}{i}") for i in range(4)]
        rs = sb(f"rs{tag}", [128, T], F32)
        rstd = sb(f"rstd{tag}", [128, T], F32)
        return (sq, sq_b, rs, Buf(f"rs{tag}"), rstd, Buf(f"rstd{tag}"))

    def proj(w, w_b, kcn, x_, x_b, ocs, evac, ncol=T, wcol0=0):
        ocs = list(ocs)
        first, rest = ocs[:4], ocs[4:]
        bk = [bank() for _ in first]
        for kc in range(kcn):
            for i_, oc in enumerate(first):
                k = bk[i_]
                P.op(PE, (lambda oc, kc, k: lambda e: e.matmul(ps[k][:, 0:ncol], lhsT=w[:, kc, wcol0 + oc * 128:wcol0 + (oc + 1) * 128],
                                                                  rhs=x_[:, kc, 0:ncol], start=(kc == 0), stop=(kc == kcn - 1)))(oc, kc, k),
                     reads=[w_b[kc], x_b[kc]], writes=[ps_b[k]])
        for i_, oc in enumerate(first):
            evac(oc, bk[i_])
        for oc in rest:
            k = bank()
            for kc in range(kcn):
                P.op(PE, (lambda oc, kc, k: lambda e: e.matmul(ps[k][:, 0:ncol], lhsT=w[:, kc, wcol0 + oc * 128:wcol0 + (oc + 1) * 128],
                                                                  rhs=x_[:, kc, 0:ncol], start=(kc == 0), stop=(kc == kcn - 1)))(oc, kc, k),
                     reads=[w_b[kc], x_b[kc]], writes=[ps_b[k]])
            evac(oc, k)

    def load_h_chunks(hA, hA_b, t, cs=range(DC)):
        for c in cs:
            P.dma(SP, (lambda c: lambda e: e.dma_start(out=hA[:, c, :], in_=hT[c * 128:(c + 1) * 128, t * T:(t + 1) * T]))(c),
                  hA_b[c], reads=[hT_b[t][c]], writes=[hA_b[c]])

    stg_state = {}

    def make_stage(n=3):
        st = [sb(f"stg{i}", [128, T], F32) for i in range(n)]
        stb = [Buf(f"stg{i}") for i in range(n)]
        return st, stb, [0]

    def residual_store(stage, hA, hA_b, c, k, t):
        st, stb, cnt = stage
        s = cnt[0] % len(st)
        cnt[0] += 1
        P.op(DVE, (lambda c, k, s: lambda e: e.tensor_tensor(out=st[s][:], in0=hA[:, c, :], in1=ps[k][:], op=ALU.add))(c, k, s),
             reads=[hA_b[c], ps_b[k]], writes=[stb[s]])
        P.dma(SP, (lambda c, s: lambda e: e.dma_start(out=hT[c * 128:(c + 1) * 128, t * T:(t + 1) * T], in_=st[s][:]))(c, s),
              stb[s], reads=[stb[s]], writes=[hT_b[t][c]])

    def mem_kv_prologue(l, wbuf, wbuf_b, kmT, kmT_b, vm, vm_b, memT, memT_b, mn, mn_b, nscr):
        load_w(wbuf, wbuf_b, w_mem_kv[l], DC)
        norm_tile(memT, memT_b, 64 + l * 8, mn, mn_b, NMEM, nscr)

        def ev_k(oc, k):
            copy_op(evac_eng(), kmT[:, oc, :], ps[k][:, 0:NMEM], [ps_b[k]], [kmT_b])
        proj(wbuf, wbuf_b, DC, mn, mn_b, range(DC), ev_k, ncol=NMEM)
        for mc in range(2):
            for half in range(2):
                k = bank()
                for kc in range(DC):
                    P.op(PE, (lambda mc, half, kc, k: lambda e: e.matmul(ps[k][:], lhsT=mn[:, kc, mc * 128:(mc + 1) * 128],
                                                                         rhs=wbuf[:, kc, D + half * 512:D + (half + 1) * 512],
                                                                         start=(kc == 0), stop=(kc == DC - 1)))(mc, half, kc, k),
                         reads=[mn_b[kc], wbuf_b[kc]], writes=[ps_b[k]])
                copy_op(evac_eng(), vm[:, mc, half * 512:(half + 1) * 512], ps[k][:], [ps_b[k]], [vm_b])

    def mem_attention(qm, qm_b, kmT, kmT_b, vm, vm_b, pT, pT_b, rl, rl_b, dst, dst_b, dst_c0):
        sc = 1.0 / 16.0
        for hh in range(4):
            for mc in range(2):
                k = bank()
                for kc in range(2):
                    P.op(PE, (lambda hh, mc, kc, k: lambda e: e.matmul(ps[k][:], lhsT=kmT[:, 2 * hh + kc, mc * 128:(mc + 1) * 128],
                                                                       rhs=qm[:, 2 * hh + kc, :], start=(kc == 0), stop=(kc == 1)))(hh, mc, kc, k),
                         reads=[kmT_b, qm_b[2 * hh + kc]], writes=[ps_b[k]])
                P.op(ACT, (lambda mc, k: lambda e: e.activation(out=pT[mc][:], in_=ps[k][:], func=AF.Exp, scale=sc))(mc, k),
                     reads=[ps_b[k]], writes=[pT_b[mc]])
            kl = bank()
            for mc in range(2):
                P.op(PE, (lambda mc, kl: lambda e: e.matmul(ps[kl][:], lhsT=ones_bf[:], rhs=pT[mc][:], start=(mc == 0), stop=(mc == 1)))(mc, kl),
                     reads=[pT_b[mc], b_const], writes=[ps_b[kl]])
            P.op(ACT, (lambda kl: lambda e: e.activation(out=rl[:], in_=ps[kl][:], func=AF.Ln))(kl), reads=[ps_b[kl]], writes=[rl_b])
            P.op(ACT, lambda e: e.activation(out=rl[:], in_=rl[:], func=AF.Exp, scale=-1.0), reads=[rl_b], writes=[rl_b])
            for oc in range(2):
                k = bank()
                for mc in range(2):
                    P.op(PE, (lambda hh, oc, mc, k: lambda e: e.matmul(ps[k][:], lhsT=vm[:, mc, hh * 256 + oc * 128:hh * 256 + (oc + 1) * 128],
                                                                       rhs=pT[mc][:], start=(mc == 0), stop=(mc == 1)))(hh, oc, mc, k),
                         reads=[vm_b, pT_b[mc]], writes=[ps_b[k]])
                ci = dst_c0 + 2 * hh + oc
                P.op(DVE, (lambda ci, k: lambda e: e.tensor_tensor(out=dst[:, ci, :], in0=ps[k][:], in1=rl[:], op=ALU.mult))(ci, k),
                     reads=[ps_b[k], rl_b], writes=[dst_b[ci]])

    def rope_chunk(k, dstT, dst_b, c, cs_t, cs_b, rtmp):
        kb, kb_b, t1, t1_b, t2, t2_b = rtmp
        copy_op(ACT, kb[:], ps[k][:], [ps_b[k]], [kb_b])
        P.op(DVE, (lambda k: lambda e: e.tensor_tensor(out=t2[:], in0=ps[k][:], in1=cs_t[:, 0, :], op=ALU.mult))(k),
             reads=[ps_b[k], cs_b], writes=[t2_b])
        k2 = bank()
        P.op(PE, (lambda k2: lambda e: e.matmul(ps[k2][:], lhsT=perm_bf[:], rhs=kb[:], start=True, stop=True))(k2),
             reads=[kb_b, b_const], writes=[ps_b[k2]])
        P.op(DVE, (lambda k2: lambda e: e.tensor_tensor(out=t1[:], in0=ps[k2][:], in1=cs_t[:, 1, :], op=ALU.mult))(k2),
             reads=[ps_b[k2], cs_b], writes=[t1_b])
        P.op(POOL, (lambda c: lambda e: e.tensor_tensor(out=dstT[:, c, :], in0=t1[:], in1=t2[:], op=ALU.add))(c),
             reads=[t1_b, t2_b], writes=[dst_b[c]])

    def rope_scratch():
        kb = sb("rkb", [128, T], BF16)
        t1 = sb("rt1", [128, T], F32)
        t2 = sb("rt2", [128, T], F32)
        return (kb, Buf("rkb"), t1, Buf("rt1"), t2, Buf("rt2"))

    def load_cs(cs_t, cs_b, t):
        P.dma(SP, lambda e: e.dma_start(out=cs_t[:], in_=cs_d[:, :, t * T:(t + 1) * T].rearrange("a p s -> p a s")), cs_b, writes=[cs_b])

    dbg_stop = [False]

    def stop(name):
        if stop_after == name:
            dbg_stop[0] = True
        return dbg_stop[0]

    def mixer_pool_phase(l):
        base = cx.off
        w1 = sb("w_in", [128, DC, 2 * D], BF16)
        w1_b = [Buf(f"w_in{i}") for i in range(DC)]
        w2_off = (cx.off + 31) // 32 * 32
        w2 = sb("w_out", [128, 16, D], BF16)
        w2_b = [Buf(f"w_out{i}") for i in range(16)]
        w2v = sb_at("w_mkvv", [128, DC, 2 * D], BF16, w2_off)
        w2v_b = [Buf(f"w_mkvv{i}") for i in range(DC)]
        pw = sb("pw", [128, 8, 256], BF16)
        pw_b = [Buf(f"pw{i}") for i in range(8)]
        kmT = sb("kmT", [128, DC, NMEM], BF16)
        vm = sb("vm", [128, 2, D], BF16)
        kmT_b, vm_b = Buf("kmT"), Buf("vm")
        nscr = norm_scratch("m")
        hbufs = [sb(f"hA{i}", [128, DC, T], F32) for i in range(2)]
        hbufs_b = [[Buf(f"hA{i}_{c}") for c in range(DC)] for i in range(2)]
        xn = sb("xn", [128, DC, T], BF16)
        xn_b = [Buf(f"xn{c}") for c in range(DC)]
        uext = sb("uext", [128, DC, 16 + T], F32)
        uext_b = [Buf(f"uext{c}") for c in range(DC)]
        lv = [sb(f"lv{i}", [128, 16 + T], F32) for i in range(4)]
        lv_b = [Buf(f"lv{i}") for i in range(4)]
        pq_off = (cx.off + 31) // 32 * 32
        pooled = sb("pooled", [128, DC, T], BF16)
        pooled_b = [Buf(f"pooled{c}") for c in range(DC)]
        qm = sb("qm", [128, DC, T], BF16)
        qm_b = [Buf(f"qm{c}") for c in range(DC)]
        cat = sb("cat", [128, 16, T], BF16)
        cat_b = [Buf(f"cat{c}") for c in range(16)]
        pT = [sb(f"pT{i}", [128, T], BF16) for i in range(2)]
        pT_b = [Buf(f"pT{i}") for i in range(2)]
        rl = sb("rl", [128, T], F32)
        rl_b = Buf("rl")
        stage = make_stage(2)
        xin = None
        if l == 0:
            xin = [sb_at(f"xin{i}", [128, D], F32, pq_off + i * 4096) for i in range(4)]
            xin_b = [Buf(f"xin{i}") for i in range(4)]
            xin_al = [pooled_b[0:4], pooled_b[4:8], qm_b[0:4], qm_b[4:8]]

        memrow = [sb(f"memrow{i}", [128, D], F32) for i in range(2)] if l != 0 else xin[0:2]
        memrow_b = [Buf(f"memrow{i}") for i in range(2)]
        memT, memT_b = hbufs[1], hbufs_b[1]
        for blk in range(2):
            P.dma(SP, (lambda blk: lambda e: e.dma_start(out=memrow[blk][:], in_=mem[blk * 128:(blk + 1) * 128, :]))(blk),
                  memrow_b[blk], writes=[memrow_b[blk]])
        for c in range(DC):
            k = bank()
            for blk in range(2):
                P.op(PE, (lambda c, blk, k: lambda e: e.transpose(ps[k][:, blk * 128:(blk + 1) * 128], memrow[blk][:, c * 128:(c + 1) * 128], ident[:]))(c, blk, k),
                     reads=[memrow_b[blk], b_const], writes=[ps_b[k]])
            copy_op(evac_eng(), memT[:, c, 0:NMEM], ps[k][:, 0:NMEM], [ps_b[k]], [memT_b[c]])
        mem_kv_prologue(l, w2v, w2v_b, kmT, kmT_b, vm, vm_b, memT, memT_b, xn, xn_b, nscr)
        P.barrier()
        load_w(w1, w1_b, w_in[l], DC)
        for g in range(4):
            for kc in range(2):
                i = g * 2 + kc
                P.dma(POOL, (lambda g, kc, i: lambda e: e.dma_start(out=pw[:, i, :], in_=pool_w[l, g, kc * 128:(kc + 1) * 128, :]))(g, kc, i),
                      pw_b[i], writes=[pw_b[i]])
        load_w(w2, w2_b, w_out[l], 16)
        for c in range(DC):
            P.op(POOL, (lambda c: lambda e: e.memset(uext[:, c, 0:16], 0.0))(c), writes=[uext_b[c]])

        def load_dma(t):
            hb, hb_b = hbufs[t % 2], hbufs_b[t % 2]
            if l == 0:
                for blk in range(4):
                    P.dma(SP, (lambda blk: lambda e: e.dma_start(out=xin[blk][:], in_=x[t * T + blk * 128:t * T + (blk + 1) * 128, :]))(blk),
                          xin_b[blk], writes=[xin_b[blk]] + xin_al[blk])
            else:
                load_h_chunks(hb, hb_b, t)

        def load_post(t):
            if l != 0:
                return
            hb, hb_b = hbufs[t % 2], hbufs_b[t % 2]
            for c in range(DC):
                k = bank()
                for blk in range(4):
                    P.op(PE, (lambda c, blk, k: lambda e: e.transpose(ps[k][:, blk * 128:(blk + 1) * 128], xin[blk][:, c * 128:(c + 1) * 128], ident[:]))(c, blk, k),
                         reads=[xin_b[blk], b_const] + xin_al[blk], writes=[ps_b[k]])
                copy_op(evac_eng(), hb[:, c, :], ps[k][:], [ps_b[k]], [hb_b[c]])

        load_dma(0)
        load_post(0)
        for t in range(NT):
            hb, hb_b = hbufs[t % 2], hbufs_b[t % 2]
            if t + 1 < NT and l != 0:
                load_dma(t + 1)
            norm_tile(hb, hb_b, 0 + l * 8, xn, xn_b, T, nscr)

            W = 16 + T

            def pool_chunk(c):
                g = c // 2
                A, A_b, B, B_b = lv[(c % 2) * 2], lv_b[(c % 2) * 2], lv[(c % 2) * 2 + 1], lv_b[(c % 2) * 2 + 1]
                PENG = POOL if g < 2 else DVE
                P.op(PENG, (lambda c, A: lambda e: e.tensor_tensor(out=A[:, 1:W], in0=uext[:, c, 1:W], in1=uext[:, c, 0:W - 1], op=ALU.add))(c, A),
                     reads=[uext_b[c]], writes=[A_b])
                fin, fin_b = A, A_b
                if g >= 1:
                    P.op(PENG, (lambda A, B: lambda e: e.tensor_tensor(out=B[:, 3:W], in0=A[:, 3:W], in1=A[:, 1:W - 2], op=ALU.add))(A, B),
                         reads=[A_b], writes=[B_b])
                    fin, fin_b = B, B_b
                if g >= 2:
                    P.op(PENG, (lambda A, B: lambda e: e.tensor_tensor(out=A[:, 7:W], in0=B[:, 7:W], in1=B[:, 3:W - 4], op=ALU.add))(A, B),
                         reads=[B_b], writes=[A_b])
                    fin, fin_b = A, A_b
                if g >= 3:
                    P.op(PENG, (lambda A, B: lambda e: e.tensor_tensor(out=B[:, 15:W], in0=A[:, 15:W], in1=A[:, 7:W - 8], op=ALU.add))(A, B),
                         reads=[A_b], writes=[B_b])
                    fin, fin_b = B, B_b
                if t == 0:
                    P.op(DVE, (lambda g, fin: lambda e: e.tensor_tensor(out=fin[:, 16:32], in0=fin[:, 16:32], in1=rcw[:, g * 16:(g + 1) * 16], op=ALU.mult))(g, fin),
                         reads=[fin_b, b_const], writes=[fin_b])
                P.op(DVE, (lambda c, g, fin: lambda e: e.scalar_tensor_tensor(out=pooled[:, c, :], in0=fin[:, 16:W], scalar=1.0 / POOL_W[g], in1=uext[:, c, 16:W],
                                                                            op0=ALU.mult, op1=ALU.subtract))(c, g, fin),
                     reads=[fin_b, uext_b[c]], writes=[pooled_b[c]])

            def ev_in(oc, k):
                if oc < DC:
                    copy_op(evac_eng(), uext[:, oc, 16:16 + T], ps[k][:], [ps_b[k]], [uext_b[oc]])
                    pool_chunk(oc)
                else:
                    copy_op(evac_eng(), qm[:, oc - DC, :], ps[k][:], [ps_b[k]], [qm_b[oc - DC]])
            proj(w1, w1_b, DC, xn, xn_b, range(16), ev_in)
            mem_attention(qm, qm_b, kmT, kmT_b, vm, vm_b, pT, pT_b, rl, rl_b, cat, cat_b, DC)
            for g in range(4):
                for oc in range(2):
                    k = bank()
                    for kc in range(2):
                        P.op(PE, (lambda g, oc, kc, k: lambda e: e.matmul(ps[k][:], lhsT=pw[:, g * 2 + kc, oc * 128:(oc + 1) * 128], rhs=pooled[:, 2 * g + kc, :],
                                                                          start=(kc == 0), stop=(kc == 1)))(g, oc, kc, k),
                             reads=[pw_b[g * 2 + kc], pooled_b[2 * g + kc]], writes=[ps_b[k]])
                    ci = 2 * g + oc
                    P.op(DVE, (lambda ci, k: lambda e: e.tensor_scalar(out=cat[:, ci, :], in0=ps[k][:], scalar1=vecs[:, 96 + l * 8 + ci:96 + l * 8 + ci + 1], scalar2=None,
                                                                       op0=ALU.mult))(ci, k),
                         reads=[ps_b[k], b_vecs], writes=[cat_b[ci]])
            for c in range(DC):
                P.op(POOL, (lambda c: lambda e: e.tensor_copy(out=uext[:, c, 0:16], in_=uext[:, c, T:T + 16]))(c),
                     reads=[uext_b[c]], writes=[uext_b[c]])
            if t + 1 < NT and l == 0:
                load_dma(t + 1)

            def ev_out(oc, k):
                residual_store(stage, hb, hb_b, oc, k, t)
            proj(w2, w2_b, 16, cat, cat_b, range(DC), ev_out)
            if t + 1 < NT:
                load_post(t + 1)
        P.barrier()
        cx.off = base

    WGU_BYTES = DC * FF * 2
    WG_OFF = (SBUF_HI - 2 * WGU_BYTES) // 32 * 32
    ffn_pre = {}

    def alloc_wgu(l):
        wg = sb_at(f"wg{l}", [128, DC, FF], BF16, WG_OFF)
        wu = sb_at(f"wu{l}", [128, DC, FF], BF16, WG_OFF + WGU_BYTES)
        wg_b = [Buf(f"wg{i}") for i in range(DC)]
        wu_b = [Buf(f"wu{i}") for i in range(DC)]
        return wg, wu, wg_b, wu_b

    def ffn_phase(l):
        base = cx.off
        if l in ffn_pre:
            wg, wu, wg_b, wu_b = ffn_pre[l]
        else:
            wg, wu, wg_b, wu_b = alloc_wgu(l)
        wd = sb("wd", [128, FC, D], BF16)
        wd_b = [Buf(f"wd{i}") for i in range(FC)]
        nscr = norm_scratch("f")
        hA = sb("hAf", [128, DC, T], F32)
        hA_b = [Buf(f"hAf{c}") for c in range(DC)]
        xn = sb("xnf", [128, DC, T], BF16)
        xn_b = [Buf(f"xnf{c}") for c in range(DC)]
        aT = sb("aT", [128, FC, T], BF16)
        aT_b = [Buf(f"aT{c}") for c in range(FC)]
        sg = [sb(f"sg{i}", [128, T], F32) for i in range(2)]
        sg_b = [Buf(f"sg{i}") for i in range(2)]
        stage = make_stage(2)
        assert cx.off <= WG_OFF, "ffn scratch overlaps gate/up weight region"
        if l not in ffn_pre:
            load_w(wg, wg_b, w_gate[l], DC)
            load_w(wu, wu_b, w_up[l], DC)
        load_w(wd, wd_b, w_down[l], FC)
        load_h_chunks(hA, hA_b, 0)
        for t in range(NT):
            norm_tile(hA, hA_b, 32 + l * 8, xn, xn_b, T, nscr)
            for fc in range(FC):
                kg = bank()
                ku = bank()
                for kc in range(DC):
                    P.op(PE, (lambda fc, kc, kg: lambda e: e.matmul(ps[kg][:], lhsT=wg[:, kc, fc * 128:(fc + 1) * 128], rhs=xn[:, kc, :],
                                                                    start=(kc == 0), stop=(kc == DC - 1)))(fc, kc, kg),
                         reads=[wg_b[kc], xn_b[kc]], writes=[ps_b[kg]])
                for kc in range(DC):
                    P.op(PE, (lambda fc, kc, ku: lambda e: e.matmul(ps[ku][:], lhsT=wu[:, kc, fc * 128:(fc + 1) * 128], rhs=xn[:, kc, :],
                                                                    start=(kc == 0), stop=(kc == DC - 1)))(fc, kc, ku),
                         reads=[wu_b[kc], xn_b[kc]], writes=[ps_b[ku]])
                s = fc % 2
                P.op(ACT, (lambda s, kg: lambda e: e.activation(out=sg[s][:], in_=ps[kg][:], func=AF.Silu))(s, kg), reads=[ps_b[kg]], writes=[sg_b[s]])
                P.op(DVE, (lambda fc, s, ku: lambda e: e.tensor_tensor(out=aT[:, fc, :], in0=sg[s][:], in1=ps[ku][:], op=ALU.mult))(fc, s, ku),
                     reads=[sg_b[s], ps_b[ku]], writes=[aT_b[fc]])

            def ev_down(oc, k):
                residual_store(stage, hA, hA_b, oc, k, t)
                if t + 1 < NT:
                    load_h_chunks(hA, hA_b, t + 1, cs=[oc])
            proj(wd, wd_b, FC, aT, aT_b, range(DC), ev_down)
        P.barrier()
        cx.off = base

    def kv_phase():
        base = cx.off
        wk = sb("wkv", [128, DC, 2 * D], BF16)
        wk_b = [Buf(f"wkv{i}") for i in range(DC)]
        nscr = norm_scratch("k")
        hbufs = [sb(f"hAk{i}", [128, DC, T], F32) for i in range(2)]
        hbufs_b = [[Buf(f"hAk{i}_{c}") for c in range(DC)] for i in range(2)]
        xn = sb("xnk", [128, DC, T], BF16)
        xn_b = [Buf(f"xnk{c}") for c in range(DC)]
        ksb = [sb(f"ksb{i}", [128, DC, T], BF16) for i in range(2)]
        ksb_b = [[Buf(f"ksb{i}_{c}") for c in range(DC)] for i in range(2)]
        vsb = [sb(f"vsb{i}", [128, 4, D], BF16) for i in range(2)]
        vsb_b = [[Buf(f"vsb{i}_{c}") for c in range(4)] for i in range(2)]
        cs_t = [sb(f"cs{i}", [128, 2, T], F32) for i in range(2)]
        cs_b = [Buf(f"cs{i}") for i in range(2)]
        rtmp = rope_scratch()
        load_w(wk, wk_b, w_kv, DC)
        load_h_chunks(hbufs[0], hbufs_b[0], 0)
        load_cs(cs_t[0], cs_b[0], 0)
        for t in range(NT):
            hb, hb_b = hbufs[t % 2], hbufs_b[t % 2]
            if t + 1 < NT:
                load_h_chunks(hbufs[(t + 1) % 2], hbufs_b[(t + 1) % 2], t + 1)
                load_cs(cs_t[(t + 1) % 2], cs_b[(t + 1) % 2], t + 1)
            norm_tile(hb, hb_b, 112, xn, xn_b, T, nscr)
            kk, kk_b = ksb[t % 2], ksb_b[t % 2]

            def ev_k(oc, k):
                import os
                if os.environ.get("KV_NOROPE"):
                    copy_op(ACT, kk[:, oc, :], ps[k][:], [ps_b[k]], [kk_b[oc]])
                else:
                    rope_chunk(k, kk, kk_b, oc, cs_t[t % 2], cs_b[t % 2], rtmp)
                P.dma(SP, (lambda oc, t, kk: lambda e: e.dma_start(out=kT[oc * 128:(oc + 1) * 128, t * T:(t + 1) * T], in_=kk[:, oc, :]))(oc, t, kk),
                      kk_b[oc], reads=[kk_b[oc]])
            proj(wk, wk_b, DC, xn, xn_b, range(DC), ev_k)
            vv, vv_b = vsb[t % 2], vsb_b[t % 2]
            import os
            for blk in (range(4) if not os.environ.get("KV_NOV") else []):
                for half in range(2):
                    k = bank()
                    for kc in range(DC):
                        P.op(PE, (lambda blk, half, kc, k: lambda e: e.matmul(ps[k][:], lhsT=xn[:, kc, blk * 128:(blk + 1) * 128],
                                                                              rhs=wk[:, kc, D + half * 512:D + (half + 1) * 512],
                                                                              start=(kc == 0), stop=(kc == DC - 1)))(blk, half, kc, k),
                             reads=[xn_b[kc], wk_b[kc]], writes=[ps_b[k]])
                    copy_op(evac_eng(), vv[:, blk, half * 512:(half + 1) * 512], ps[k][:], [ps_b[k]], [vv_b[blk]])
                P.dma(SP, (lambda blk, t, vv: lambda e: e.dma_start(out=vS[:, :, t * 4 + blk, :].rearrange("h p f -> p h f"),
                                                                    in_=vv[:, blk, :].rearrange("p (h f) -> p h f", h=4)))(blk, t, vv),
                      vv_b[blk], reads=[vv_b[blk]])
        P.barrier()
        cx.off = base

    def diff_a_phase(l):
        j = l - N_A
        base = cx.off
        w1 = sb("w_inq", [128, DC, 2 * D], BF16)
        w1_b = [Buf(f"w_inq{i}") for i in range(DC)]
        w2 = sb("w_mkv", [128, DC, 2 * D], BF16)
        w2_b = [Buf(f"w_mkv{i}") for i in range(DC)]
        kmT = sb("kmTq", [128, DC, NMEM], BF16)
        vm = sb("vmq", [128, 2, D], BF16)
        kmT_b, vm_b = Buf("kmTq"), Buf("vmq")
        nscr = norm_scratch("q")
        hbufs = [sb(f"hAq{i}", [128, DC, T], F32) for i in range(2)]
        hbufs_b = [[Buf(f"hAq{i}_{c}") for c in range(DC)] for i in range(2)]
        xn = sb("xnq", [128, DC, T], BF16)
        xn_b = [Buf(f"xnq{c}") for c in range(DC)]
        qsb = [sb(f"qsb{i}", [128, DC, T], BF16) for i in range(2)]
        qsb_b = [[Buf(f"qsb{i}_{c}") for c in range(DC)] for i in range(2)]
        qm = sb("qmq", [128, DC, T], BF16)
        qm_b = [Buf(f"qmq{c}") for c in range(DC)]
        mo = [sb(f"mo{i}", [128, DC, T], BF16) for i in range(2)]
        mo_b = [[Buf(f"mo{i}_{c}") for c in range(DC)] for i in range(2)]
        pT = [sb(f"pTq{i}", [128, T], BF16) for i in range(2)]
        pT_b = [Buf(f"pTq{i}") for i in range(2)]
        rl = sb("rlq", [128, T], F32)
        rl_b = Buf("rlq")
        cs_t = [sb(f"csq{i}", [128, 2, T], F32) for i in range(2)]
        cs_b = [Buf(f"csq{i}") for i in range(2)]
        rtmp = rope_scratch()
        memrow = [sb(f"memrowq{i}", [128, D], F32) for i in range(2)]
        memrow_b = [Buf(f"memrowq{i}") for i in range(2)]
        memT, memT_b = hbufs[1], hbufs_b[1]
        for blk in range(2):
            P.dma(SP, (lambda blk: lambda e: e.dma_start(out=memrow[blk][:], in_=mem[blk * 128:(blk + 1) * 128, :]))(blk),
                  memrow_b[blk], writes=[memrow_b[blk]])
        for c in range(DC):
            k = bank()
            for blk in range(2):
                P.op(PE, (lambda c, blk, k: lambda e: e.transpose(ps[k][:, blk * 128:(blk + 1) * 128], memrow[blk][:, c * 128:(c + 1) * 128], ident[:]))(c, blk, k),
                     reads=[memrow_b[blk], b_const], writes=[ps_b[k]])
            copy_op(evac_eng(), memT[:, c, 0:NMEM], ps[k][:, 0:NMEM], [ps_b[k]], [memT_b[c]])
        mem_kv_prologue(l, w2, w2_b, kmT, kmT_b, vm, vm_b, memT, memT_b, xn, xn_b, nscr)
        load_w(w1, w1_b, w_in[l], DC)
        load_h_chunks(hbufs[0], hbufs_b[0], 0)
        load_cs(cs_t[0], cs_b[0], 0)
        for t in range(NT):
            hb, hb_b = hbufs[t % 2], hbufs_b[t % 2]
            if t + 1 < NT:
                load_h_chunks(hbufs[(t + 1) % 2], hbufs_b[(t + 1) % 2], t + 1)
                load_cs(cs_t[(t + 1) % 2], cs_b[(t + 1) % 2], t + 1)
            norm_tile(hb, hb_b, 0 + l * 8, xn, xn_b, T, nscr)
            qq, qq_b = qsb[t % 2], qsb_b[t % 2]
            mm, mm_b = mo[t % 2], mo_b[t % 2]

            def ev_in(oc, k):
                if oc < DC:
                    rope_chunk(k, qq, qq_b, oc, cs_t[t % 2], cs_b[t % 2], rtmp)
                    P.dma(SP, (lambda oc, t, qq: lambda e: e.dma_start(out=qT[oc * 128:(oc + 1) * 128, t * T:(t + 1) * T], in_=qq[:, oc, :]))(oc, t, qq),
                          qq_b[oc], reads=[qq_b[oc]], writes=[qT_b[t][oc]])
                else:
                    copy_op(evac_eng(), qm[:, oc - DC, :], ps[k][:], [ps_b[k]], [qm_b[oc - DC]])
            proj(w1, w1_b, DC, xn, xn_b, range(16), ev_in)
            mem_attention(qm, qm_b, kmT, kmT_b, vm, vm_b, pT, pT_b, rl, rl_b, mm, mm_b, 0)
            for c in range(DC):
                P.dma(SP, (lambda c, t, mm: lambda e: e.dma_start(out=memoT[c * 128:(c + 1) * 128, t * T:(t + 1) * T], in_=mm[:, c, :]))(c, t, mm),
                      mm_b[c], reads=[mm_b[c]], writes=[memoT_b[t][c]])
        P.barrier()
        cx.off = base

    def diff_b_phase(l):
        j = l - N_A
        li = lambda_init[l]
        base = cx.off
        kres = [[sb(f"kres{i}_{m}", [128, S], BF16) for m in range(2)] for i in range(2)]
        kres_b = [[Buf(f"kres{i}_{m}") for m in range(2)] for i in range(2)]
        vres = [sb(f"vres{i}", [128, S // 128, 256], BF16) for i in range(2)]
        vres_b = [Buf(f"vres{i}") for i in range(2)]
        qt_sb = [sb(f"qt{i}", [128, 2, T], BF16) for i in range(2)]
        qt_b = [[Buf(f"qt{i}_{m}") for m in range(2)] for i in range(2)]
        NPT = 8
        pT = [sb(f"pTa{i}", [128, T], BF16) for i in range(NPT)]
        pT_b = [Buf(f"pTa{i}") for i in range(NPT)]
        rl = sb("rla", [128, T], F32)
        rl_b = Buf("rla")
        accD = [sb(f"accD{i}", [128, T], F32) for i in range(2)]
        accD_b = [Buf(f"accD{i}") for i in range(2)]
        accP = [sb(f"accP{i}", [128, T], F32) for i in range(2)]
        accP_b = [Buf(f"accP{i}") for i in range(2)]
        sumbf = sb("sumbf", [128, T], BF16)
        sumbf_b = Buf("sumbf")
        tmpo = sb("tmpo", [128, T], F32)
        tmpo_b = Buf("tmpo")
        oacc = sb("oacc", [128, 2, T], F32)
        oacc_b = [Buf(f"oacc{i}") for i in range(2)]
        sq = [sb(f"sqa{i}", [128, T], BF16) for i in range(2)]
        sq_b = [Buf(f"sqa{i}") for i in range(2)]
        rs = sb("rsa", [128, T], F32)
        rs_b = Buf("rsa")
        rstd = sb("rstda", [128, T], F32)
        rstd_b = Buf("rstda")
        mixs = [sb(f"mixs{i}", [128, 2, T], BF16) for i in range(2)]
        mixs_b = [[Buf(f"mixs{i}_{c}") for c in range(2)] for i in range(2)]
        sc = 128.0 ** -0.5
        LOOK = 3

        def load_head(hh):
            i = hh % 2
            for m in range(2):
                c = 2 * hh + m
                P.dma(SP, (lambda m, c: lambda e: e.dma_start(out=kres[i][m][:], in_=kT[c * 128:(c + 1) * 128, :]))(m, c),
                      kres_b[i][m], writes=[kres_b[i][m]])
            P.dma(SP, lambda e: e.dma_start(out=vres[i][:], in_=vS[hh]), vres_b[i], writes=[vres_b[i]])

        def load_q(g):
            hh, qt = divmod(g, NT)
            slot = g % 2
            for m in range(2):
                c = 2 * hh + m
                P.dma(SP, (lambda m, c: lambda e: e.dma_start(out=qt_sb[slot][:, m, :], in_=qT[c * 128:(c + 1) * 128, qt * T:(qt + 1) * T]))(m, c),
                      qt_b[slot][m], reads=[qT_b[qt][c]], writes=[qt_b[slot][m]])

        its = []
        for g in range(4 * NT):
            hh, qt = divmod(g, NT)
            nkc = 4 * (qt + 1)
            for m in range(2):
                for kc in range(nkc):
                    its.append(dict(g=g, hh=hh, qt=qt, m=m, kc=kc, nkc=nkc))
        N = len(its)
        pcnt = [0]

        def emit_score(it):
            g, hh, qt, m, kc = it["g"], it["hh"], it["qt"], it["m"], it["kc"]
            i, slot = hh % 2, g % 2
            if m == 0 and kc == 0:
                if g == 0:
                    load_head(0)
                    load_q(0)
                if g + 1 < 4 * NT:
                    load_q(g + 1)
            k = bank()
            p = pcnt[0] % NPT
            pcnt[0] += 1
            c0 = (kc - 4 * qt) * 128 if kc >= 4 * qt else 0
            it["k"], it["p"], it["c0"] = k, p, c0
            P.op(PE, (lambda m, kc, k, i, slot, c0: lambda e: e.matmul(ps[k][:, c0:T], lhsT=kres[i][m][:, kc * 128:(kc + 1) * 128], rhs=qt_sb[slot][:, m, c0:T],
                                                                       start=True, stop=True))(m, kc, k, i, slot, c0),
                 reads=[kres_b[i][m], qt_b[slot][m]], writes=[ps_b[k]])
            P.op(ACT, (lambda k, p, c0: lambda e: e.activation(out=pT[p][:, c0:T], in_=ps[k][:, c0:T], func=AF.Exp, scale=sc))(k, p, c0),
                 reads=[ps_b[k]], writes=[pT_b[p]])
            if kc >= 4 * qt:
                dd = kc - 4 * qt
                P.op(DVE, (lambda p, dd, c0: lambda e: e.tensor_tensor(out=pT[p][:, c0:c0 + 128], in0=pT[p][:, c0:c0 + 128], in1=masks[:, dd, c0:c0 + 128], op=ALU.mult))(p, dd, c0),
                     reads=[pT_b[p], b_const], writes=[pT_b[p]])

        acc = {}

        def finalize(it):
            g, hh, qt, m = it["g"], it["hh"], it["qt"], it["m"]
            slot = g % 2
            ko, kl = acc["ko"], acc["kl"]
            a_ = acc["set"]
            if acc["nD"] > 0 and acc["nP"] > 0:
                P.op(DVE, (lambda a_: lambda e: e.tensor_tensor(out=sumbf[:], in0=accD[a_][:], in1=accP[a_][:], op=ALU.add))(a_),
                     reads=[accD_b[a_], accP_b[a_]], writes=[sumbf_b])
            elif acc["nD"] > 0:
                P.op(DVE, (lambda a_: lambda e: e.tensor_copy(out=sumbf[:], in_=accD[a_][:]))(a_), reads=[accD_b[a_]], writes=[sumbf_b])
            elif acc["nP"] > 0:
                P.op(DVE, (lambda a_: lambda e: e.tensor_copy(out=sumbf[:], in_=accP[a_][:]))(a_), reads=[accP_b[a_]], writes=[sumbf_b])
            if acc["nD"] + acc["nP"] > 0:
                P.op(PE, (lambda kl: lambda e: e.matmul(ps[kl][:], lhsT=ones_bf[:], rhs=sumbf[:], start=False, stop=True))(kl),
                     reads=[sumbf_b, b_const], writes=[ps_b[kl]])
            P.op(ACT, (lambda kl: lambda e: e.activation(out=rl[:], in_=ps[kl][:], func=AF.Ln))(kl), reads=[ps_b[kl]], writes=[rl_b])
            P.op(ACT, lambda e: e.activation(out=rl[:], in_=rl[:], func=AF.Exp, scale=-1.0), reads=[rl_b], writes=[rl_b])
            for oc in range(2):
                if m == 0:
                    P.op(DVE, (lambda oc, k: lambda e: e.tensor_tensor(out=oacc[:, oc, :], in0=ps[k][:], in1=rl[:], op=ALU.mult))(oc, ko[oc]),
                         reads=[ps_b[ko[oc]], rl_b], writes=[oacc_b[oc]])
                else:
                    P.op(DVE, (lambda oc, k: lambda e: e.tensor_tensor(out=tmpo[:], in0=ps[k][:], in1=rl[:], op=ALU.mult))(oc, ko[oc]),
                         reads=[ps_b[ko[oc]], rl_b], writes=[tmpo_b])
                    P.op(DVE, (lambda oc: lambda e: e.scalar_tensor_tensor(out=oacc[:, oc, :], in0=tmpo[:], scalar=misc[:, 3 + j:4 + j], in1=oacc[:, oc, :],
                                                                           op0=ALU.mult, op1=ALU.add))(oc),
                         reads=[tmpo_b, oacc_b[oc], b_misc], writes=[oacc_b[oc]])
            release(ko[0])
            release(ko[1])
            release(kl)
            if m == 0:
                return
            k = bank()
            for oc in range(2):
                P.op(ACT, (lambda oc: lambda e: e.activation(out=sq[oc][:], in_=oacc[:, oc, :], func=AF.Square))(oc), reads=[oacc_b[oc]], writes=[sq_b[oc]])
                P.op(PE, (lambda oc, k: lambda e: e.matmul(ps[k][:], lhsT=ones_bf[:], rhs=sq[oc][:], start=(oc == 0), stop=(oc == 1)))(oc, k),
                     reads=[sq_b[oc], b_const], writes=[ps_b[k]])
            P.op(ACT, (lambda k: lambda e: e.activation(out=rs[:], in_=ps[k][:], func=AF.Ln, bias=misc[:, 1 + j:2 + j], scale=1.0 / (256.0 * (1.0 - li) ** 2)))(k),
                 reads=[ps_b[k], b_misc], writes=[rs_b])
            P.op(ACT, lambda e: e.activation(out=rstd[:], in_=rs[:], func=AF.Exp, scale=-0.5), reads=[rs_b], writes=[rstd_b])
            ms, ms_b = mixs[slot], mixs_b[slot]
            for oc in range(2):
                P.op(DVE, (lambda oc, ms: lambda e: e.scalar_tensor_tensor(out=ms[:, oc, :], in0=oacc[:, oc, :], scalar=vecs[:, 128 + 2 * j + oc:128 + 2 * j + oc + 1],
                                                                          in1=rstd[:], op0=ALU.mult, op1=ALU.mult))(oc, ms),
                     reads=[oacc_b[oc], rstd_b, b_vecs], writes=[ms_b[oc]])
                c = 2 * hh + oc
                P.dma(SP, (lambda oc, c, ms, qt: lambda e: e.dma_start(out=mixT[c * 128:(c + 1) * 128, qt * T:(qt + 1) * T], in_=ms[:, oc, :]))(oc, c, ms, qt),
                      ms_b[oc], reads=[ms_b[oc]], writes=[mixT_b[qt][c]])

        for n in range(min(LOOK, N)):
            emit_score(its[n])
        for n in range(N):
            it = its[n]
            if n + LOOK < N:
                emit_score(its[n + LOOK])
            kc, nkc, p, i = it["kc"], it["nkc"], it["p"], it["hh"] % 2
            if kc == 0 and it["m"] == 0 and it["qt"] == 0 and it["hh"] + 1 < 4:
                load_head(it["hh"] + 1)
            if kc == 0:
                acc["ko"] = [bank(hold=True), bank(hold=True)]
                acc["kl"] = bank(hold=True)
                acc["set"] = acc.get("set", 1) ^ 1
                acc["nD"] = 0
                acc["nP"] = 0
                kinds = []
                for kk_ in range(nkc):
                    if kk_ >= 4 * it["qt"]:
                        kinds.append("PE")
                    else:
                        kinds.append(("PE", "DVE", "POOL", "DVE", "POOL", "DVE")[acc.get("rr", 0) % 6])
                        acc["rr"] = acc.get("rr", 0) + 1
                acc["kinds"] = kinds
                acc["pe_first"] = kinds.index("PE")
                acc["pe_last"] = max(q_ for q_ in range(nkc) if kinds[q_] == "PE")
                acc["has_acc"] = any(q_ != "PE" for q_ in kinds)
            ko, kl = acc["ko"], acc["kl"]
            a_ = acc["set"]
            kind = acc["kinds"][kc]
            if kind == "POOL":
                if acc["nP"] == 0:
                    P.op(POOL, (lambda p, a_: lambda e: e.tensor_copy(out=accP[a_][:], in_=pT[p][:]))(p, a_), reads=[pT_b[p]], writes=[accP_b[a_]])
                else:
                    P.op(POOL, (lambda p, a_: lambda e: e.tensor_tensor(out=accP[a_][:], in0=accP[a_][:], in1=pT[p][:], op=ALU.add))(p, a_),
                         reads=[pT_b[p], accP_b[a_]], writes=[accP_b[a_]])
                acc["nP"] += 1
            elif kind == "DVE":
                if acc["nD"] == 0:
                    P.op(DVE, (lambda p, a_: lambda e: e.tensor_copy(out=accD[a_][:], in_=pT[p][:]))(p, a_), reads=[pT_b[p]], writes=[accD_b[a_]])
                else:
                    P.op(DVE, (lambda p, a_: lambda e: e.tensor_tensor(out=accD[a_][:], in0=accD[a_][:], in1=pT[p][:], op=ALU.add))(p, a_),
                         reads=[pT_b[p], accD_b[a_]], writes=[accD_b[a_]])
                acc["nD"] += 1
            else:
                P.op(PE, (lambda p, kl, st_, sp_, c0: lambda e: e.matmul(ps[kl][:, c0:T], lhsT=ones_bf[:], rhs=pT[p][:, c0:T], start=st_, stop=sp_))(
                    p, kl, kc == acc["pe_first"], (kc == acc["pe_last"]) and not acc["has_acc"], it["c0"]),
                     reads=[pT_b[p], b_const], writes=[ps_b[kl]])
            for oc in range(2):
                P.op(PE, (lambda kc, oc, p, i, kb_, st_, sp_, c0: lambda e: e.matmul(ps[kb_][:, c0:T], lhsT=vres[i][:, kc, oc * 128:(oc + 1) * 128], rhs=pT[p][:, c0:T],
                                                                                     start=st_, stop=sp_))(kc, oc, p, i, ko[oc], kc == 0, kc == nkc - 1, it["c0"]),
                     reads=[vres_b[i], pT_b[p]], writes=[ps_b[ko[oc]]])
            if kc == nkc - 1:
                finalize(it)
        P.barrier()
        cx.off = base

    def diff_c_phase(l):
        base = cx.off
        w2 = sb("w_outc", [128, 16, D], BF16)
        w2_b = [Buf(f"w_outc{i}") for i in range(16)]
        hbufs = [sb(f"hAc{i}", [128, DC, T], F32) for i in range(2)]
        hbufs_b = [[Buf(f"hAc{i}_{c}") for c in range(DC)] for i in range(2)]
        cats = [sb(f"catc{i}", [128, 16, T], BF16) for i in range(2)]
        cats_b = [[Buf(f"catc{i}_{c}") for c in range(16)] for i in range(2)]
        stage = make_stage(3)
        load_w(w2, w2_b, w_out[l], 16)
        assert cx.off <= WG_OFF, "dc scratch overlaps gate/up weight region"
        ffn_pre[l] = alloc_wgu(l)
        load_w(ffn_pre[l][0], ffn_pre[l][2], w_gate[l], DC)
        load_w(ffn_pre[l][1], ffn_pre[l][3], w_up[l], DC)

        def load(t):
            load_h_chunks(hbufs[t % 2], hbufs_b[t % 2], t)
            for c in range(DC):
                P.dma(SP, (lambda c: lambda e: e.dma_start(out=cats[t % 2][:, c, :], in_=mixT[c * 128:(c + 1) * 128, t * T:(t + 1) * T]))(c),
                      cats_b[t % 2][c], reads=[mixT_b[t][c]], writes=[cats_b[t % 2][c]])
                P.dma(SP, (lambda c: lambda e: e.dma_start(out=cats[t % 2][:, DC + c, :], in_=memoT[c * 128:(c + 1) * 128, t * T:(t + 1) * T]))(c),
                      cats_b[t % 2][DC + c], reads=[memoT_b[t][c]], writes=[cats_b[t % 2][DC + c]])
        load(0)
        for t in range(NT):
            if t + 1 < NT:
                load(t + 1)
            hb, hb_b = hbufs[t % 2], hbufs_b[t % 2]

            def ev_out(oc, k):
                residual_store(stage, hb, hb_b, oc, k, t)
            proj(w2, w2_b, 16, cats[t % 2], cats_b[t % 2], range(DC), ev_out)
        P.barrier()
        cx.off = base

    fin_ops = []

    def final_phase():
        base = cx.off
        nscr = norm_scratch("z")
        hbufs = [sb(f"hAz{i}", [128, DC, T], F32) for i in range(2)]
        hbufs_b = [[Buf(f"hAz{i}_{c}") for c in range(DC)] for i in range(2)]
        yn = sb("yn", [128, DC, T], F32)
        yn_b = [Buf(f"yn{c}") for c in range(DC)]
        rows = [sb(f"rows{i}", [128, D], F32) for i in range(4)]
        rows_b = [Buf(f"rows{i}") for i in range(4)]
        rc = [0]
        load_h_chunks(hbufs[0], hbufs_b[0], 0)
        for t in range(NT):
            hb, hb_b = hbufs[t % 2], hbufs_b[t % 2]
            if t + 1 < NT:
                load_h_chunks(hbufs[(t + 1) % 2], hbufs_b[(t + 1) % 2], t + 1)
            norm_tile(hb, hb_b, 120, yn, yn_b, T, nscr)
            for blk in range(4):
                r = rc[0] % 4
                rc[0] += 1
                for half in range(2):
                    k = bank()
                    for cc in range(4):
                        c = half * 4 + cc
                        P.op(PE, (lambda c, cc, blk, k: lambda e: e.transpose(ps[k][:, cc * 128:(cc + 1) * 128], yn[:, c, blk * 128:(blk + 1) * 128], ident[:]))(c, cc, blk, k),
                             reads=[yn_b[c], b_const], writes=[ps_b[k]])
                    copy_op(evac_eng(), rows[r][:, half * 512:(half + 1) * 512], ps[k][:], [ps_b[k]], [rows_b[r]])
                fin_ops.append(P.dma(SP, (lambda blk, r, t: lambda e: e.dma_start(out=out[t * T + blk * 128:t * T + (blk + 1) * 128, :], in_=rows[r][:]))(blk, r, t),
                                     rows_b[r], reads=[rows_b[r]]))
        cx.off = base

    phases = []
    for l in range(N_A):
        phases.append((f"mix{l}", lambda l=l: mixer_pool_phase(l)))
        phases.append((f"ffn{l}", lambda l=l: ffn_phase(l)))
    phases.append(("kv", kv_phase))
    for l in range(N_A, DEPTH):
        phases.append((f"da{l}", lambda l=l: diff_a_phase(l)))
        phases.append((f"db{l}", lambda l=l: diff_b_phase(l)))
        phases.append((f"dc{l}", lambda l=l: diff_c_phase(l)))
        phases.append((f"ffn{l}", lambda l=l: ffn_phase(l)))
    for name, fn in phases:
        if only is not None and name not in only:
            continue
        fn()
        if stop(name):
            break
    if only is None or "final" in only:
        final_phase()
    P.emit(nc, final_wait_ops=fin_ops[-8:] if len(fin_ops) > 8 else fin_ops)
    return nc


def host_constants():
    cst = np.zeros((128, NCST), np.float32)
    cst[:, 0:128] = np.eye(128, dtype=np.float32)
    perm = np.zeros((32, 32), np.float32)
    for i in range(16):
        perm[i + 16, i] = -1.0
        perm[i, i + 16] = 1.0
    cst[0:32, 128:160] = perm
    for g, w in enumerate(POOL_W):
        for tt in range(16):
            cst[:, 160 + g * 16 + tt] = float(w) / float(min(tt + 1, w))
    j = np.arange(128)[:, None]
    i = np.arange(512)[None, :]
    for c in range(4):
        cst[:, 224 + c * 512:224 + (c + 1) * 512] = ((c * 128 + j) <= i).astype(np.float32)
    pos = np.arange(S, dtype=np.float32)
    inv_freq = np.power(np.float32(500000.0), -np.arange(0, 32, 2, dtype=np.float32) / np.float32(32)).astype(np.float32)
    ang = pos[:, None] * inv_freq[None, :]
    ang = np.concatenate([ang, ang], axis=-1)
    cs = np.zeros((2, 128, S), np.float32)
    cs[0, :, :] = 1.0
    cs[0, 0:32, :] = np.cos(ang).T
    cs[1, 0:32, :] = np.sin(ang).T
    return cst, np.ascontiguousarray(cs)


def pack_vecs(inp):
    v = np.zeros((128, NV), np.float32)

    def col8(a):
        return np.asarray(a, np.float32).reshape(8, 128).T
    for l in range(DEPTH):
        v[:, 0 + l * 8:8 + l * 8] = col8(inp["g_mix"][l])
        v[:, 32 + l * 8:40 + l * 8] = col8(inp["g_ffn"][l])
        v[:, 64 + l * 8:72 + l * 8] = col8(inp["g_mem"][l])
    for j in range(2):
        v[:, 96 + j * 8:104 + j * 8] = col8(inp["pool_scale"][j])
        v[:, 128 + 2 * j:130 + 2 * j] = np.asarray(inp["g_subln"][j], np.float32).reshape(2, 128).T
        v[:, 132 + j] = inp["lam_q1"][j]
        v[:, 134 + j] = inp["lam_k1"][j]
        v[:, 136 + j] = inp["lam_q2"][j]
        v[:, 138 + j] = inp["lam_k2"][j]
    v[:, 112:120] = col8(inp["g_kv"])
    v[:, 120:128] = col8(inp["g_final"])
    return v


def make_in_maps(inp, ncores=8):
    cst, cs = host_constants()
    vecs = pack_vecs(inp)
    f = lambda a: np.ascontiguousarray(np.asarray(a, dtype=np.float32))
    shared = {
        "w_in": f(inp["w_in"]), "w_out": f(inp["w_out"]), "w_gate": f(inp["w_gate"]), "w_up": f(inp["w_up"]),
        "w_down": f(inp["w_down"]), "w_mem_kv": f(inp["w_mem_kv"]), "pool_w": f(inp["pool_w"]), "w_kv": f(inp["w_kv"]),
        "vecs": vecs, "cst": cst, "cs": cs,
    }
    maps = []
    for b in range(ncores):
        m = dict(shared)
        m["x"] = f(inp["x"][b])
        m["mem"] = f(inp["mem"][b])
        maps.append(m)
    return maps


_NC_CACHE = {}


def kernel(**inputs):
    if "nc" not in _NC_CACHE:
        _NC_CACHE["nc"] = build()
    nc = _NC_CACHE["nc"]
    maps = make_in_maps(inputs, 8)
    res = run_bass_kernel_spmd(nc, maps, core_ids=list(range(8)))
    return np.stack([np.asarray(r["out"], dtype=np.float32) for r in res.results], axis=0)
```
